# Optimizing a Trainium2 kernel written in Bass

```python
import jax, jax.numpy as jnp
from jax import lax
import numpy as np

D_MODEL = 1024
BATCH = 8
SEQ = 4096
DEPTH = 4

MIX_WIDTH = D_MODEL
RWKV_WIDTH = MIX_WIDTH // 2
RWKV_HEAD = 64
RWKV_HEADS = RWKV_WIDTH // RWKV_HEAD
DECAY_LORA = 64
ICLR_LORA = 64
VMIX_LORA = 32
MLA_WIDTH = MIX_WIDTH - RWKV_WIDTH
MLA_V_HEAD = 64
MLA_HEADS = MLA_WIDTH // MLA_V_HEAD
MLA_NOPE = 64
MLA_ROPE = 32
Q_LORA = 384
KV_LORA = 256
ROPE_THETA = 10000.0
Q_BLOCK = 128
NORM_EPS = 1e-6
GN_EPS = 64e-5

SHIFT_WIDTH = 3 * RWKV_WIDTH + DECAY_LORA + ICLR_LORA
O_GR = SHIFT_WIDTH
O_CQ = O_GR + RWKV_WIDTH
O_CKV = O_CQ + Q_LORA
O_KR = O_CKV + KV_LORA
O_GM = O_KR + MLA_ROPE
IN_WIDTH = O_GM + MLA_WIDTH

kernel_name = 'hymba_rwkv7_mla_adaln_block'


def rms_norm(x, g, eps=NORM_EPS):
    xf = x.astype(jnp.float32)
    y = xf * lax.rsqrt(jnp.mean(xf * xf, axis=-1, keepdims=True) + eps)
    return (y * g.astype(jnp.float32)).astype(x.dtype)


def token_shift_lerp(p, mu):
    prev = jnp.pad(p[:, :-1], ((0, 0), (1, 0), (0, 0)))
    return p + (prev - p) * mu


def rope_tables(positions):
    inv = ROPE_THETA ** (-jnp.arange(0, MLA_ROPE, 2, dtype=jnp.float32) / MLA_ROPE)
    ang = positions.astype(jnp.float32)[..., None] * inv
    ang = jnp.concatenate([ang, ang], axis=-1)
    return jnp.cos(ang), jnp.sin(ang)


def apply_rope(x, cos, sin):
    x1, x2 = jnp.split(x, 2, axis=-1)
    rot = jnp.concatenate([-x2, x1], axis=-1)
    return (x.astype(jnp.float32) * cos + rot.astype(jnp.float32) * sin).astype(x.dtype)


def rwkv7_scan(r, w, k, v, kk, a):
    B, S, H, N = r.shape

    def step(state, inp):
        r_t, w_t, k_t, v_t, kk_t, a_t = inp
        sa = jnp.einsum('bhvk,bhk->bhv', state, kk_t)
        state = (state * w_t[:, :, None, :]
                 - sa[..., None] * (kk_t * a_t)[:, :, None, :]
                 + v_t[..., None] * k_t[:, :, None, :])
        y = jnp.einsum('bhvk,bhk->bhv', state, r_t)
        return state, y

    xs = tuple(jnp.moveaxis(t, 1, 0) for t in (r, w, k, v, kk, a))
    s0 = jnp.zeros((B, H, N, N), jnp.float32)
    _, ys = lax.scan(step, s0, xs)
    return jnp.moveaxis(ys, 0, 1)


def group_norm_heads(y, w, b):
    mean = jnp.mean(y, axis=-1, keepdims=True)
    var = jnp.mean(jnp.square(y - mean), axis=-1, keepdims=True)
    yn = (y - mean) * lax.rsqrt(var + GN_EPS)
    return (yn * w.reshape(RWKV_HEADS, RWKV_HEAD).astype(jnp.float32)
            + b.reshape(RWKV_HEADS, RWKV_HEAD).astype(jnp.float32))


def rwkv7_time_mix(p, p_vmix, v_first, mu, mu_v, w0, w_dec_up, a0, w_icl_up, v0, w_vmix_up,
                   k_k, k_a, r_k, lnx_w, lnx_b):
    B, S, _ = p.shape
    H, N = RWKV_HEADS, RWKV_HEAD
    ps = token_shift_lerp(p, mu)
    r, k, v, w_lo, a_lo = jnp.split(
        ps, [RWKV_WIDTH, 2 * RWKV_WIDTH, 3 * RWKV_WIDTH, 3 * RWKV_WIDTH + DECAY_LORA], axis=-1)
    w_log = -jax.nn.softplus(-(w0 + jnp.tanh(w_lo) @ w_dec_up)) - 0.5
    decay = jnp.exp(-jnp.exp(w_log.astype(jnp.float32)))
    a = jax.nn.sigmoid(a0 + a_lo @ w_icl_up)
    if v_first is None:
        v_first = v
    else:
        v_lo = token_shift_lerp(p_vmix, mu_v)
        v = v + (v_first - v) * jax.nn.sigmoid(v0 + v_lo @ w_vmix_up)

    def heads(t):
        return t.reshape(B, S, H, N).astype(jnp.float32)

    kk = heads(k * k_k)
    kk = kk / jnp.maximum(jnp.sqrt(jnp.sum(kk * kk, axis=-1, keepdims=True)), 1e-12)
    k = k * (1 + (a - 1) * k_a)
    rh, kh, vh, ah = heads(r), heads(k), heads(v), heads(a)
    y = rwkv7_scan(rh, heads(decay), kh, vh, kk, ah)
    y = group_norm_heads(y, lnx_w, lnx_b)
    y = y + jnp.sum(rh * kh * r_k.astype(jnp.float32), axis=-1, keepdims=True) * vh
    return y.reshape(B, S, RWKV_WIDTH).astype(p.dtype), v_first


def causal_block_attention(q_nope, q_rope, k_nope, k_rope, v):
    B, S, H, _ = q_nope.shape
    nb = S // Q_BLOCK
    scale = (MLA_NOPE + MLA_ROPE) ** -0.5
    qn = q_nope.reshape(B, nb, Q_BLOCK, H, MLA_NOPE).transpose(1, 0, 2, 3, 4)
    qr = q_rope.reshape(B, nb, Q_BLOCK, H, MLA_ROPE).transpose(1, 0, 2, 3, 4)
    key_idx = jnp.arange(S)

    def block(args):
        qn_b, qr_b, start = args
        s = (jnp.einsum('bqhd,bkhd->bhqk', qn_b, k_nope)
             + jnp.einsum('bqhd,bkd->bhqk', qr_b, k_rope)).astype(jnp.float32) * scale
        q_idx = start + jnp.arange(Q_BLOCK)
        s = jnp.where(key_idx[None, :] <= q_idx[:, None], s, -jnp.inf)
        prob = jax.nn.softmax(s, axis=-1).astype(v.dtype)
        return jnp.einsum('bhqk,bkhd->bqhd', prob, v)

    starts = jnp.arange(nb) * Q_BLOCK
    out = lax.map(block, (qn, qr, starts))
    return out.transpose(1, 0, 2, 3, 4).reshape(B, S, H, MLA_V_HEAD)


def mla_branch(c_q, c_kv, k_rope_in, cos, sin, q_norm_g, kv_norm_g, w_uq, w_ukv):
    B, S, _ = c_q.shape
    H = MLA_HEADS
    q = (rms_norm(c_q, q_norm_g) @ w_uq).reshape(B, S, H, MLA_NOPE + MLA_ROPE)
    q_nope = q[..., :MLA_NOPE]
    q_rope = apply_rope(q[..., MLA_NOPE:], cos[:, :, None, :], sin[:, :, None, :])
    kv = (rms_norm(c_kv, kv_norm_g) @ w_ukv).reshape(B, S, H, MLA_NOPE + MLA_V_HEAD)
    k_nope, v = kv[..., :MLA_NOPE], kv[..., MLA_NOPE:]
    k_rope = apply_rope(k_rope_in, cos, sin)
    y = causal_block_attention(q_nope, q_rope, k_nope, k_rope, v)
    return y.reshape(B, S, MLA_WIDTH)


def setup_inputs(seed: int = 0) -> dict:
    key = jax.random.key(seed)
    ks = iter(jax.random.split(key, 40))
    L, D, Lv = DEPTH, D_MODEL, DEPTH - 1

    def nrm(shape, s):
        return jax.random.normal(next(ks), shape, jnp.float32) * s

    def uni(shape, lo, hi):
        return jax.random.uniform(next(ks), shape, jnp.float32, lo, hi)

    return {
        'x': nrm((BATCH, SEQ, D), 1.0),
        'c': nrm((BATCH, D), 1.0),
        'positions': jnp.broadcast_to(jnp.arange(SEQ, dtype=jnp.int32), (BATCH, SEQ)),
        'norm_g': 1.0 + nrm((L, D), 0.02),
        'w_ada': nrm((L, D, 3 * D), 0.5 * D ** -0.5),
        'b_ada': nrm((L, 3 * D), 0.01),
        'w_in': nrm((L, D, IN_WIDTH), D ** -0.5),
        'w_vmix_down': nrm((Lv, D, VMIX_LORA), D ** -0.5),
        'mu_shift': uni((L, SHIFT_WIDTH), 0.0, 1.0),
        'mu_vmix': uni((Lv, VMIX_LORA), 0.0, 1.0),
        'w0': uni((L, RWKV_WIDTH), -6.5, -1.5),
        'w_decay_up': nrm((L, DECAY_LORA, RWKV_WIDTH), 0.1 * DECAY_LORA ** -0.5),
        'a0': nrm((L, RWKV_WIDTH), 0.1),
        'w_iclr_up': nrm((L, ICLR_LORA, RWKV_WIDTH), 0.3 * ICLR_LORA ** -0.5),
        'v0': 1.0 + nrm((Lv, RWKV_WIDTH), 0.1),
        'w_vmix_up': nrm((Lv, VMIX_LORA, RWKV_WIDTH), 0.3 * VMIX_LORA ** -0.5),
        'k_k': 0.85 + nrm((L, RWKV_WIDTH), 0.02),
        'k_a': 1.0 + nrm((L, RWKV_WIDTH), 0.02),
        'r_k': nrm((L, RWKV_HEADS, RWKV_HEAD), 0.1),
        'lnx_w': 1.0 + nrm((L, RWKV_WIDTH), 0.02),
        'lnx_b': nrm((L, RWKV_WIDTH), 0.01),
        'q_norm_g': 1.0 + nrm((L, Q_LORA), 0.02),
        'kv_norm_g': 1.0 + nrm((L, KV_LORA), 0.02),
        'w_uq': nrm((L, Q_LORA, MLA_HEADS * (MLA_NOPE + MLA_ROPE)), Q_LORA ** -0.5),
        'w_ukv': nrm((L, KV_LORA, MLA_HEADS * (MLA_NOPE + MLA_V_HEAD)), KV_LORA ** -0.5),
        'w_out': nrm((L, MIX_WIDTH, D), MIX_WIDTH ** -0.5),
        'final_g': 1.0 + nrm((D,), 0.02),
    }


def reference(x, c, positions, norm_g, w_ada, b_ada, w_in, w_vmix_down, mu_shift, mu_vmix,
              w0, w_decay_up, a0, w_iclr_up, v0, w_vmix_up, k_k, k_a, r_k, lnx_w, lnx_b,
              q_norm_g, kv_norm_g, w_uq, w_ukv, w_out, final_g):
    cos, sin = rope_tables(positions)
    c_act = jax.nn.silu(c)
    v_first = None
    for l in range(DEPTH):
        mod = c_act @ w_ada[l] + b_ada[l]
        shift, scale, gate = jnp.split(mod, 3, axis=-1)
        h = rms_norm(x, norm_g[l]) * (1 + scale[:, None, :]) + shift[:, None, :]
        if l == 0:
            proj = h @ w_in[0]
            p_vmix, mu_v, v0_l, w_vu = None, None, None, None
        else:
            proj = h @ jnp.concatenate([w_in[l], w_vmix_down[l - 1]], axis=1)
            p_vmix, mu_v, v0_l, w_vu = proj[..., IN_WIDTH:], mu_vmix[l - 1], v0[l - 1], w_vmix_up[l - 1]
        y_rwkv, v_first = rwkv7_time_mix(
            proj[..., :SHIFT_WIDTH], p_vmix, v_first, mu_shift[l], mu_v, w0[l], w_decay_up[l],
            a0[l], w_iclr_up[l], v0_l, w_vu, k_k[l], k_a[l], r_k[l], lnx_w[l], lnx_b[l])
        y_mla = mla_branch(proj[..., O_CQ:O_CKV], proj[..., O_CKV:O_KR], proj[..., O_KR:O_GM],
                           cos, sin, q_norm_g[l], kv_norm_g[l], w_uq[l], w_ukv[l])
        y = jnp.concatenate([y_rwkv * jax.nn.silu(proj[..., O_GR:O_CQ]),
                             y_mla * jax.nn.silu(proj[..., O_GM:IN_WIDTH])], axis=-1)
        x = x + gate[:, None, :] * (y @ w_out[l])
    return rms_norm(x, final_g)
```

```python
import numpy as np
import ml_dtypes
import concourse.bass as bass
import concourse.mybir as mybir
from concourse.bass_utils import run_bass_kernel_spmd
from contextlib import ExitStack

F32 = mybir.dt.float32
BF16 = mybir.dt.bfloat16
I32 = mybir.dt.int32
ALU = mybir.AluOpType
AF = mybir.ActivationFunctionType
AX = mybir.AxisListType

ENGS = ['pe', 'act', 'dve', 'pool', 'sp']
EPOCH = 20000
NDSEM = 24
DT_BYTES = {F32: 4, BF16: 2, I32: 4}


class Buf:
    __slots__ = ('name', 'last_w', 'readers', 'dma_ws', 'excl')

    def __init__(self, name, excl=False):
        self.name = name
        self.excl = excl
        self.last_w = None
        self.dma_ws = []
        self.readers = {}


class V:
    __slots__ = ('ap', 'buf')

    def __init__(self, ap, buf):
        self.ap = ap
        self.buf = buf

    def __getitem__(self, k):
        return V(self.ap[k], self.buf)

    def re(self, s, **kw):
        return V(self.ap.rearrange(s, **kw), self.buf)

    def bc(self, shape):
        return V(self.ap.to_broadcast(list(shape)), self.buf)

    def un(self, ax):
        return V(self.ap.unsqueeze(ax), self.buf)

    def cast(self, dt):
        return V(self.ap.bitcast(dt), self.buf)


class Op:
    __slots__ = ('eng', 'fn', 'deps', 'sig', 'is_dma', 'has_dep', 'gidx')

    def __init__(self, eng, fn, is_dma):
        self.eng = eng
        self.fn = fn
        self.is_dma = is_dma
        self.deps = []
        self.sig = None
        self.has_dep = False


class Prog:
    def __init__(self, nc, es, arena_bytes):
        self.nc = nc
        self.es = es
        self.ops = {e: [] for e in ENGS}
        self.n = 0
        self.final_dmas = []
        h = es.enter_context(nc.sbuf_tensor("arena", [128, arena_bytes // 2], BF16))
        self.arena = h[:]
        self.arena_bytes = arena_bytes
        self.live = []
        self.sp_ = 0
        self.banks = []
        for i in range(8):
            hb = es.enter_context(nc.psum_tensor(f"bank{i}", [128, 512], F32))
            self.banks.append(V(hb[:], Buf(f"bank{i}", excl=True)))
        self.bi = 0

    def sb(self, name, shape, dt=F32):
        h = self.es.enter_context(self.nc.sbuf_tensor("s_" + name, list(shape), dt))
        return V(h[:], Buf(name))

    def mark(self):
        return self.sp_

    def release(self, m):
        self.sp_ = m

    def al(self, name, shape, dt=F32):
        per = int(np.prod(shape[1:])) * DT_BYTES[dt]
        start = (self.sp_ + 63) // 64 * 64
        end = start + per
        assert end <= self.arena_bytes, (name, end, self.arena_bytes)
        self.sp_ = end
        b = Buf(name)
        keep = []
        for (s0, e0, ob) in self.live:
            if s0 < end and start < e0:
                cands = list(ob.readers.values()) + list(ob.dma_ws)
                if ob.last_w is not None:
                    cands.append(ob.last_w)
                for d in cands:
                    k = ('dma', id(d)) if d.is_dma else d.eng
                    if k not in b.readers or (not d.is_dma and b.readers[k].gidx < d.gidx):
                        b.readers[k] = d
                if not (start <= s0 and e0 <= end):
                    keep.append((s0, e0, ob))
            else:
                keep.append((s0, e0, ob))
        keep.append((start, end, b))
        self.live = keep
        ap = self.arena[0:shape[0], start // 2: end // 2]
        if dt != BF16:
            ap = ap.bitcast(dt)
        if len(shape) == 3:
            ap = ap.rearrange("p (a b) -> p a b", a=shape[1])
        elif len(shape) == 4:
            ap = ap.rearrange("p (a b c) -> p a b c", a=shape[1], b=shape[2])
        return V(ap, b)

    def bank(self):
        b = self.banks[self.bi % 8]
        self.bi += 1
        return b

    def dram(self, name, shape, dt, kind="Internal"):
        t = self.nc.dram_tensor(name, list(shape), dt, kind=kind)
        return V(t.ap(), Buf(name))

    def op(self, eng, fn, r=(), w=(), is_dma=False):
        o = Op(eng, fn, is_dma)
        o.gidx = self.n
        self.n += 1
        deps = {}

        def add(d):
            if d is None or d is o:
                return
            if d.is_dma:
                deps[('dma', id(d))] = d
            else:
                if d.eng == eng and eng == 'pe' and not is_dma:
                    return
                k = d.eng
                if k not in deps or deps[k].gidx < d.gidx:
                    deps[k] = d
        rb = [x.buf for x in r]
        wb = [x.buf for x in w]
        for b in rb:
            add(b.last_w)
            for d in b.dma_ws:
                add(d)
            if b.excl:
                for d in b.readers.values():
                    if d.eng != eng:
                        add(d)
        for b in wb:
            add(b.last_w)
            for d in b.readers.values():
                add(d)
            if not is_dma:
                for d in b.dma_ws:
                    add(d)
        for b in wb:
            if is_dma:
                if b.readers:
                    b.dma_ws = []
                    b.readers = {}
                b.dma_ws.append(o)
            else:
                b.last_w = o
                b.dma_ws = []
                b.readers = {}
        for b in rb:
            if b in wb:
                continue
            if is_dma:
                b.readers[('dma', id(o))] = o
            else:
                b.readers[eng] = o
        o.deps = list(deps.values())
        for d in o.deps:
            d.has_dep = True
        self.ops[eng].append(o)
        return o

    def dma(self, out, in_, eng='sp', final=False, **kw):
        o = self.op(eng, lambda e: e.dma_start(out=out.ap, in_=in_.ap, **kw), r=[in_], w=[out], is_dma=True)
        o.has_dep = True
        if final:
            self.final_dmas.append(o)
        return o

    def mm(self, out, lhsT, rhs, start=True, stop=True, **kw):
        return self.op('pe', lambda e: e.matmul(out.ap, lhsT.ap, rhs.ap, start=start, stop=stop, **kw),
                       r=[lhsT, rhs] + ([] if start else [out]), w=[out])

    def tr(self, out, in_, ident):
        return self.op('pe', lambda e: e.transpose(out.ap, in_.ap, ident.ap), r=[in_, ident], w=[out])

    def act(self, out, in_, func, bias=None, scale=None, accum=None):
        r = [in_]
        kw = {}
        if bias is not None:
            if isinstance(bias, V):
                r.append(bias)
                kw['bias'] = bias.ap
            else:
                kw['bias'] = float(bias)
        if scale is not None:
            if isinstance(scale, V):
                r.append(scale)
                kw['scale'] = scale.ap
            else:
                kw['scale'] = float(scale)
        w = [out]
        if accum is not None:
            kw['accum_out'] = accum.ap
            w.append(accum)
        return self.op('act', lambda e: e.activation(out.ap, in_.ap, func, **kw), r=r, w=w)

    def tt(self, eng, out, in0, in1, op):
        return self.op(eng, lambda e: e.tensor_tensor(out.ap, in0.ap, in1.ap, op), r=[in0, in1], w=[out])

    def ts(self, eng, out, in0, s1, s2=None, op0=ALU.mult, op1=None):
        r = [in0]
        a1 = s1.ap if isinstance(s1, V) else float(s1)
        if isinstance(s1, V):
            r.append(s1)
        a2 = None
        if s2 is not None:
            a2 = s2.ap if isinstance(s2, V) else float(s2)
            if isinstance(s2, V):
                r.append(s2)
        if op1 is None:
            return self.op(eng, lambda e: e.tensor_scalar(out.ap, in0.ap, a1, None, op0), r=r, w=[out])
        return self.op(eng, lambda e: e.tensor_scalar(out.ap, in0.ap, a1, a2, op0, op1), r=r, w=[out])

    def stt(self, out, in0, s, in1, op0, op1):
        r = [in0, in1]
        a = s.ap if isinstance(s, V) else float(s)
        if isinstance(s, V):
            r.append(s)
        return self.op('dve', lambda e: e.scalar_tensor_tensor(out.ap, in0.ap, a, in1.ap, op0, op1), r=r, w=[out])

    def cp(self, eng, out, in_):
        if eng == 'act':
            return self.op('act', lambda e: e.copy(out.ap, in_.ap), r=[in_], w=[out])
        return self.op(eng, lambda e: e.tensor_copy(out.ap, in_.ap), r=[in_], w=[out])

    def memset(self, eng, out, val):
        return self.op(eng, lambda e: e.memset(out.ap, val), r=[], w=[out])

    def scan(self, out, d0, d1, init, op0, op1):
        return self.op('dve', lambda e: e.tensor_tensor_scan(out.ap, d0.ap, d1.ap, float(init), op0, op1),
                       r=[d0, d1], w=[out])

    def recip(self, out, in_):
        return self.op('dve', lambda e: e.reciprocal(out.ap, in_.ap), r=[in_], w=[out])

    def reduce(self, out, in_, op=ALU.add):
        return self.op('dve', lambda e: e.tensor_reduce(out.ap, in_.ap, AX.X, op), r=[in_], w=[out])

    def emit(self):
        nc = self.nc
        sem_names = []
        for eng in ENGS:
            cnt = 0
            dcnt = 0
            for o in self.ops[eng]:
                if o.is_dma:
                    j = dcnt % NDSEM
                    name = f"d_{eng}_{j}"
                    o.sig = (name, 16 * (dcnt // NDSEM + 1), 16)
                    dcnt += 1
                elif o.has_dep:
                    ep = cnt // EPOCH
                    name = f"c_{eng}_{ep}"
                    o.sig = (name, cnt % EPOCH + 1, 1)
                    cnt += 1
                else:
                    continue
                if name not in sem_names:
                    sem_names.append(name)
        sems = {}
        for nm in sem_names:
            sems[nm] = self.es.enter_context(nc.semaphore(nm))
        self.nsem = len(sem_names)
        block = self.es.enter_context(nc.Block())
        prog = self

        def run(eng, e):
            known = {}
            for o in prog.ops[eng]:
                if o.is_dma:
                    nm, val, inc = o.sig
                    if val > 16 and known.get(nm, 0) < val - 16:
                        e.wait_ge(sems[nm], val - 16)
                        known[nm] = val - 16
                for d in o.deps:
                    nm, val, inc = d.sig
                    if known.get(nm, 0) < val:
                        e.wait_ge(sems[nm], val)
                        known[nm] = val
                inst = o.fn(e)
                if o.sig is not None:
                    nm, val, inc = o.sig
                    inst.then_inc(sems[nm], inc)
            if eng == 'sp':
                last = {}
                for en in ENGS:
                    for o in prog.ops[en]:
                        if o.is_dma:
                            nm, val, inc = o.sig
                            last[nm] = max(last.get(nm, 0), val)
                for nm, val in last.items():
                    if known.get(nm, 0) < val:
                        e.wait_ge(sems[nm], val)
                        known[nm] = val

        @block.tensor
        def _(e):
            run('pe', e)

        @block.scalar
        def _(e):
            run('act', e)

        @block.vector
        def _(e):
            run('dve', e)

        @block.gpsimd
        def _(e):
            run('pool', e)

        @block.sync
        def _(e):
            run('sp', e)


D = 1024
NKC = 8
NH = 8
NCH = 28
WCOLS = NCH * 128
C_DEC = float(np.exp(-0.5))
ATT_SCALE = float(96 ** -0.5)
NORM_EPS = 1e-6
GN_EPS = 64e-5
NV = 80
VG, VB, VMU, VMUV, VW0, VA0, VV0, VKK, VKA, VRK, VQG, VKVG = 0, 8, 32, 45, 46, 50, 54, 58, 62, 66, 70, 73
CI, CMT, CML, CTRI, CBO, CBC, CINV, CRST, NCC = 0, 128, 640, 768, 896, 1024, 1026, 1028, 1540
TWO_PI = float(2 * np.pi)
CW1 = 6.28125
CW2 = float(2 * np.pi - 6.28125)


def make_consts():
    c = np.zeros((128, NCC), np.float32)
    i = np.arange(128)
    c[:, CI:CI + 128] = np.eye(128)
    strict = (i[None, :] > i[:, None]).astype(np.float32)
    incl = (i[None, :] >= i[:, None]).astype(np.float32)
    c[:, CMT:CMT + 512] = np.concatenate([strict, incl, -strict, incl], axis=1)
    c[:, CML:CML + 128] = -(i[None, :] < i[:, None]).astype(np.float32)
    c[:, CTRI:CTRI + 128] = incl
    bo = np.zeros((128, 128), np.float32)
    bo[:64, :64] = 1
    bo[64:, 64:] = 1
    c[:, CBO:CBO + 128] = bo
    c[:64, CBC] = 1
    c[64:, CBC + 1] = 1
    inv = (10000.0 ** (-np.arange(0, 32, 2, dtype=np.float32) / 32)).astype(np.float32)
    c[:, CINV] = inv[i % 16]
    rs = np.ones(512, np.float32)
    rs[::128] = 0
    c[:, CRST:CRST + 512] = rs[None, :]
    return c


class _Stop(Exception):
    pass


def build(S, L, taps=(), stop=None):
    NB = S // 512
    NT = S // 128
    nc = bass.Bass("TRN2", target_bir_lowering=False)
    es = ExitStack()
    P = Prog(nc, es, arena_bytes=195 * 1024)
    tapset = set(taps)

    def din(name, shape, dt=F32):
        t = nc.dram_tensor(name, list(shape), dt, kind="ExternalInput")
        return V(t.ap(), Buf(name))

    def dscr(name, shape, dt):
        kind = "ExternalOutput" if name in tapset else "Internal"
        return P.dram(name, shape, dt, kind=kind)

    def finish():
        P.emit()
        es.close()
        return nc

    def chk(tag):
        if stop == tag:
            raise _Stop()

    x_d = din("x", [S, D])
    cT_d = din("cT", [128, 8])
    pos_d = din("pos", [1, S], I32)
    consts_d = din("consts", [128, NCC])
    vecs_d = din("vecs", [128, L, NV])
    rows_d = din("rows", [L, 1024])
    masku_d = din("masku", [128, 7 * 128], BF16)
    fing_d = din("final_g", [1, D])
    wada_d = din("w_ada", [L, D, 3 * D])
    win_d = din("w_in", [L, D, 3360])
    wvd_d = din("w_vd", [max(L - 1, 1), D, 32])
    wdec_d = din("w_dec", [L, 64, 512])
    wicl_d = din("w_icl", [L, 64, 512])
    wvup_d = din("w_vup", [max(L - 1, 1), 32, 512])
    wuq_d = din("w_uq", [L, 384, 768])
    wukv_d = din("w_ukv", [L, 256, 1024])
    wout_d = din("w_out", [L, D, D])
    out_d = P.dram("out", [S, D], F32, kind="ExternalOutput")

    xT_d = dscr("xT", [D, S], F32)
    cs_d = dscr("cs", [2, 128, S], F32)
    vfirst_d = dscr("vfirst", [512, S], F32)
    feat_d = dscr("feat", [NT, 64, NH, 4, 128], BF16)
    tokm_d = dscr("tokm", [S, 4, 512], BF16)
    gc_d = dscr("gc", [512, NT], F32)
    sbon_d = dscr("sbon", [S, NH], F32)
    gate_d = dscr("gate", [D, S], BF16)
    qT_d = dscr("qT", [NH, 96, S], BF16)
    kT_d = dscr("kT", [NH, 64, S], BF16)
    krT_d = dscr("krT", [32, S], BF16)
    v1_d = dscr("v1", [S, NH, 65], BF16)
    ytok_d = dscr("ytok", [S, D], BF16)
    xT_v = xT_d.re("(c p) t -> p c t", p=128)

    consts = P.sb("consts", [128, NCC])
    vecs = P.sb("vecs", [128, L, NV])
    mods = P.sb("mods", [128, L, 24])
    gs = P.sb("gs", [128, L, 8])
    omka = P.sb("omka", [128, L, 4])
    identb = P.sb("identb", [128, 128], BF16)
    onesb = P.sb("onesb", [128, 128], BF16)
    bonesb = P.sb("bonesb", [128, 128], BF16)
    bcolsb = P.sb("bcolsb", [128, 2], BF16)
    epsn = P.sb("epsn", [128, 1])
    epsg = P.sb("epsg", [128, 1])
    halfpi = P.sb("halfpi", [128, 1])
    masku = P.sb("masku", [128, 7, 128], BF16)
    ident = consts[:, CI:CI + 128]
    maskT = consts[:, CMT:CMT + 512]
    masklow = consts[:, CML:CML + 128]
    tri = consts[:, CTRI:CTRI + 128]
    invf = consts[:, CINV:CINV + 1]
    resetm = consts[:, CRST:CRST + 512]

    P.dma(consts, consts_d)
    P.dma(vecs, vecs_d)
    P.dma(masku, masku_d.re("p (j t) -> p j t", j=7))
    P.cp('dve', identb, ident)
    P.memset('pool', onesb, 1.0)
    P.memset('pool', epsn, NORM_EPS)
    P.memset('pool', epsg, GN_EPS)
    P.memset('pool', halfpi, float(np.pi / 2))
    P.cp('dve', bonesb, consts[:, CBO:CBO + 128])
    P.cp('dve', bcolsb, consts[:, CBC:CBC + 2])

    m0 = P.mark()
    cact = P.al("cact", [128, 8])
    P.dma(cact, cT_d)
    P.act(cact, cact, AF.Silu)
    wst_t = [P.al("wst0", [128, 8, 512]), P.al("wst1", [128, 8, 512])]
    mbank = P.bank()
    n = 0
    for l in range(L):
        for cg in range(6):
            wst = wst_t[n % 2]
            n += 1
            P.dma(wst, wada_d[l, :, cg * 512:(cg + 1) * 512].re("(kc p) n -> p kc n", p=128))
            for j in range(4):
                col = l * 24 + cg * 4 + j
                for kc in range(8):
                    P.mm(mbank[:, col:col + 1], wst[:, kc, j * 128:(j + 1) * 128], cact[:, kc:kc + 1],
                         start=(kc == 0), stop=(kc == 7))
    P.tt('dve', mods, mbank[:, 0:L * 24].re("p (l j) -> p l j", l=L), vecs[:, :, VB:VB + 24], ALU.add)
    P.stt(gs, mods[:, :, 8:16], 1.0, vecs[:, :, VG:VG + 8], ALU.add, ALU.mult)
    P.ts('dve', omka, vecs[:, :, VKA:VKA + 4], -1.0, 1.0, ALU.mult, ALU.add)

    xs = P.al("xs", [128, 4, D])
    xts = P.al("xts", [128, 8, 512])
    posi = P.al("posi", [128, 512], I32)
    ang = P.al("ang", [128, 512])
    rk_ = P.al("rk_", [128, 512])
    rki = P.al("rki", [128, 512], I32)
    rr = P.al("rr", [128, 512])
    mk = P.al("mk", [128, 512])
    for b in range(NB):
        t0 = b * 512
        P.dma(xs, x_d[t0:t0 + 512, :].re("(j p) d -> p j d", p=128))
        for c in range(8):
            bk = P.bank()
            for j in range(4):
                P.tr(bk[:, j * 128:(j + 1) * 128], xs[:, j, c * 128:(c + 1) * 128], ident)
            P.cp('act' if c % 2 else 'dve', xts[:, c, :], bk)
        P.dma(xT_v[:, :, t0:t0 + 512], xts)
        P.dma(posi, V(pos_d.ap[:, t0:t0 + 512].partition_broadcast(128), pos_d.buf))
        P.cp('pool', ang, posi)
        P.ts('pool', ang, ang, invf, None, ALU.mult)
        P.ts('pool', rk_, ang, 1.0 / TWO_PI, None, ALU.mult)
        P.cp('dve', rki, rk_)
        P.cp('dve', rk_, rki)
        P.stt(rr, rk_, -CW1, ang, ALU.mult, ALU.add)
        P.stt(rr, rk_, -CW2, rr, ALU.mult, ALU.add)
        P.ts('pool', mk, rr, float(np.pi), None, ALU.is_gt)
        P.stt(rr, mk, -TWO_PI, rr, ALU.mult, ALU.add)
        P.ts('pool', mk, rr, float(-np.pi), None, ALU.is_lt)
        P.stt(rr, mk, TWO_PI, rr, ALU.mult, ALU.add)
        P.ts('pool', rr, rr, float(np.pi), float(-np.pi), ALU.min, ALU.max)
        P.act(mk, rr, AF.Sin)
        P.dma(cs_d[1, :, t0:t0 + 512], mk)
        P.stt(rk_, rr, -1.0, rr, ALU.mult, ALU.max)
        P.act(ang, rk_, AF.Sin, bias=halfpi, scale=-1.0)
        P.dma(cs_d[0, :, t0:t0 + 512], ang)
    P.release(m0)
    if stop == 'p0':
        return finish()

    def layer(l):
        mL = P.mark()
        W = P.al("W", [128, 8, WCOLS], BF16)
        lora_up = P.al("lora_up", [128, 512], BF16)
        vmix_up = P.al("vmix_up", [128, 512], BF16)
        Wq = P.al("Wq", [128, 3, 1024], BF16)
        Wkv = P.al("Wkv", [128, 2, 1024], BF16)
        mW = P.mark()
        wstage = P.al("wstage", [128, WCOLS])
        P.memset('pool', wstage, 0.0)
        for kc in range(8):
            rows = slice(kc * 128, (kc + 1) * 128)
            P.dma(wstage[:, 0:2816], win_d[l, rows, 0:2816])
            P.dma(wstage[:, 2816:3328], win_d[l, rows, 2848:3360])
            P.dma(wstage[:, 3328:3360], win_d[l, rows, 2816:2848])
            if l > 0:
                P.dma(wstage[:, 3392:3424], wvd_d[l - 1, rows, :])
            P.dma(wstage[:, 3456:3472], win_d[l, rows, 2832:2848])
            P.dma(wstage[:, 3472:3488], win_d[l, rows, 2816:2832])
            P.cp('dve' if kc % 2 else 'pool', W[:, kc, :], wstage)
            P.ts('pool', W[:, kc, 3456:3472], W[:, kc, 3456:3472], -1.0, None, ALU.mult)
        lst = P.al("lst", [128, 512])
        P.dma(lst[0:64], wdec_d[l])
        P.dma(lst[64:128], wicl_d[l])
        P.cp('pool', lora_up, lst)
        if l > 0:
            vst = P.al("vst", [128, 512])
            P.dma(vst[64:96], wvup_d[l - 1])
            P.cp('pool', vmix_up[64:96], vst[64:96])
        qst = P.al("qst", [128, 1024])
        for kc in range(3):
            src = wuq_d[l, kc * 128:(kc + 1) * 128, :].re("p (h e) -> p h e", e=96)
            P.dma(qst[:, 0:512].re("p (h e) -> p h e", e=64), src[:, :, 0:64])
            P.dma(qst[:, 512:768].re("p (h e) -> p h e", e=32), src[:, :, 64:96])
            rot = qst[:, 768:1024].re("p (h e) -> p h e", e=32)
            P.dma(rot[:, :, 0:16], src[:, :, 80:96])
            P.dma(rot[:, :, 16:32], src[:, :, 64:80])
            P.ts('dve', Wq[:, kc, :], qst, vecs[:, l, VQG + kc:VQG + kc + 1], None, ALU.mult)
            wr = Wq[:, kc, 768:1024].re("p (h e) -> p h e", e=32)[:, :, 0:16]
            P.ts('pool', wr, wr, -1.0, None, ALU.mult)
        for kc in range(2):
            src = wukv_d[l, kc * 128:(kc + 1) * 128, :].re("p (h e) -> p h e", e=128)
            P.dma(qst[:, 0:512].re("p (h e) -> p h e", e=64), src[:, :, 0:64])
            P.dma(qst[:, 512:1024].re("p (h e) -> p h e", e=64), src[:, :, 64:128])
            P.ts('dve', Wkv[:, kc, :], qst, vecs[:, l, VKVG + kc:VKVG + kc + 1], None, ALU.mult)

        chk(f'w_{l}')
        P.release(mW)
        carry = {j: P.al(f"carry{j}", [128, 1]) for j in list(range(13)) + [26]}
        for j in carry:
            P.memset('pool', carry[j], 0.0)
        pe2 = [P.al("pe0", [128, 513]), P.al("pe1", [128, 513])]
        npe = [0]
        xt = P.al("xt", [128, 8, 512])
        sq = P.al("sq", [128, 8, 512], BF16)
        hT = P.al("hT", [128, 8, 512], BF16)
        rstd = P.al("rstd", [128, 512])
        u2 = [P.al("u0", [128, 512]), P.al("u1", [128, 512])]
        dtmp = P.al("dtmp", [128, 512])
        lsh = P.al("lsh", [128, 512])
        lora_in = P.al("lora_in", [128, 512], BF16)
        vlo_in = P.al("vlo_in", [128, 512], BF16)
        csb = P.al("csb", [128, 2, 512])
        F_ = {nm: P.al(nm, [128, 512]) for nm in
              ("rs", "ks", "vs", "sg", "aa", "gv", "vf", "kk", "nrm", "bb", "Gs", "g1")}
        B_ = {nm: P.al(nm, [128, 512], BF16) for nm in
              ("kk2", "Rt", "KKt", "Kt", "Bt", "Kh", "Bh", "Vb", "rkb")}
        nb4 = P.al("nb4", [128, 4])
        gC4 = P.al("gC4", [128, 4])
        sb4 = P.al("sb4", [128, 4, 2])
        tmo = P.al("tmo", [128, 4, 4, 128], BF16)
        gst = [P.al("gst0", [128, 512], BF16), P.al("gst1", [128, 512], BF16)]
        cq = P.al("cq", [128, 3, 512], BF16)
        sqq = P.al("sqq", [128, 3, 512], BF16)
        rq, crs, srs, a1, a2, rkv = (F_[k] for k in ("sg", "aa", "gv", "vf", "kk", "nrm"))
        qn = [P.al("qn0", [128, 512], BF16), P.al("qn1", [128, 512], BF16)]
        qrp = [P.al("qrp0", [128, 512], BF16), P.al("qrp1", [128, 512], BF16)]
        ckv = P.al("ckv", [128, 2, 512], BF16)
        sqk = P.al("sqk", [128, 2, 512], BF16)
        rcol = P.al("rcol", [128, 1])
        v1s = P.al("v1s", [128, 4, NH, 65], BF16)
        krb = P.al("krb", [128, 512], BF16)
        P.memset('pool', v1s[:, :, :, 64:65], 1.0)

        for b in range(NB):
            t0 = b * 512
            tsl = slice(t0, t0 + 512)
            P.dma(xt, xT_v[:, :, tsl])
            P.dma(csb, cs_d[:, :, tsl].re("w p t -> p w t"))
            for c in range(8):
                if c % 2:
                    P.act(sq[:, c, :], xt[:, c, :], AF.Square)
                else:
                    P.tt('pool', sq[:, c, :], xt[:, c, :], xt[:, c, :], ALU.mult)
            bk = P.bank()
            for c in range(8):
                P.mm(bk, onesb, sq[:, c, :], start=(c == 0), stop=(c == 7))
            P.act(rstd, bk, AF.Sqrt, bias=epsn, scale=1.0 / D)
            P.recip(rstd, rstd)
            for c in range(8):
                u = u2[c % 2]
                P.tt('pool' if c % 2 else 'dve', u, xt[:, c, :], rstd, ALU.mult)
                P.act(hT[:, c, :], u, AF.Identity, bias=mods[:, l, c:c + 1], scale=gs[:, l, c:c + 1])

            def proj(j, M=128):
                bk = P.bank()
                for kc in range(8):
                    P.mm(bk[0:M, :], W[:, kc, j * 128:j * 128 + M], hT[:, kc, :], start=(kc == 0), stop=(kc == 7))
                return bk

            def shifted(j, out, mucol, rows=slice(0, 128)):
                bk = proj(j)
                pe = pe2[npe[0] % 2]
                npe[0] += 1
                P.cp('pool', pe[rows, 0:1], carry[j][rows, :])
                P.cp('act', pe[:, 1:513], bk)
                P.tt('pool', dtmp[rows], pe[rows, 0:512], pe[rows, 1:513], ALU.subtract)
                P.stt(out[rows], dtmp[rows], vecs[rows, l, mucol:mucol + 1], pe[rows, 1:513], ALU.mult, ALU.add)
                P.cp('pool', carry[j][rows, :], pe[rows, 512:513])
                return pe

            chk(f'h_{l}')
            shifted(12, lsh, VMU + 12)
            P.act(lora_in[0:64], lsh[0:64], AF.Tanh)
            P.cp('pool', lora_in[64:128], lsh[64:128])
            if l > 0:
                pe26 = shifted(26, lsh, VMUV, rows=slice(64, 96))
                P.cp('pool', vlo_in[64:96], lsh[64:96])
            else:
                bk26 = proj(26)
                pe26 = pe2[npe[0] % 2]
                npe[0] += 1
                P.cp('act', pe26[:, 1:513], bk26)
            bk27 = proj(27, M=32)
            P.tt('dve', a1[0:32], pe26[0:32, 1:513], csb[0:32, 0, :], ALU.mult)
            P.tt('dve', a2[0:32], bk27[0:32, :], csb[0:32, 1, :], ALU.mult)
            P.tt('pool', krb[0:32], a1[0:32], a2[0:32], ALU.add)
            P.dma(krT_d[:, tsl], krb[0:32])

            chk(f'c26_{l}')
            for hp in range(4):
                rs, ks, vs, sg, aa, gv, vf = (F_[k] for k in ("rs", "ks", "vs", "sg", "aa", "gv", "vf"))
                kk, nrm, bb, Gs, g1 = (F_[k] for k in ("kk", "nrm", "bb", "Gs", "g1"))
                kkn = kk
                ff = nrm
                kmod = nrm
                gi = e1 = ge = ginv = gcr = g1
                rk = rs
                shifted(hp, rs, VMU + hp)
                shifted(4 + hp, ks, VMU + 4 + hp)
                shifted(8 + hp, vs, VMU + 8 + hp)
                cols = slice(hp * 128, (hp + 1) * 128)
                prow = slice(hp * 128, (hp + 1) * 128)
                bkw = P.bank()
                P.mm(bkw, lora_up[0:64, cols], lora_in[0:64, :])
                P.act(sg, bkw, AF.Sigmoid, bias=vecs[:, l, VW0 + hp:VW0 + hp + 1])
                bka = P.bank()
                P.mm(bka, lora_up[64:128, cols], lora_in[64:128, :])
                P.act(aa, bka, AF.Sigmoid, bias=vecs[:, l, VA0 + hp:VA0 + hp + 1])
                if l > 0:
                    bkv = P.bank()
                    P.mm(bkv, vmix_up[64:96, cols], vlo_in[64:96, :])
                    P.act(gv, bkv, AF.Sigmoid, bias=vecs[:, l, VV0 + hp:VV0 + hp + 1])
                    P.dma(vf, vfirst_d[prow, tsl])
                    P.tt('pool', vf, vf, vs, ALU.subtract)
                    P.tt('pool', vf, vf, gv, ALU.mult)
                    P.tt('pool', vs, vs, vf, ALU.add)
                else:
                    P.dma(vfirst_d[prow, tsl], vs)
                P.ts('pool', kk, ks, vecs[:, l, VKK + hp:VKK + hp + 1], None, ALU.mult)
                P.tt('pool', B_["kk2"], kk, kk, ALU.mult)
                bks = P.bank()
                P.mm(bks, bonesb, B_["kk2"])
                P.act(nrm, bks, AF.Sqrt)
                P.ts('dve', nrm, nrm, 1e-12, None, ALU.max)
                P.recip(nrm, nrm)
                P.tt('pool', kkn, kk, nrm, ALU.mult)
                P.ts('dve', ff, aa, vecs[:, l, VKA + hp:VKA + hp + 1], omka[:, l, hp:hp + 1], ALU.mult, ALU.add)
                P.tt('pool', kmod, ff, ks, ALU.mult)
                P.tt('pool', bb, kkn, aa, ALU.mult)
                P.scan(Gs, resetm, sg, 0.0, ALU.mult, ALU.add)
                P.act(gi, Gs, AF.Exp, scale=-C_DEC)
                P.tt('dve', B_["Rt"], rs, gi, ALU.mult)
                P.tt('pool', e1, Gs, sg, ALU.subtract)
                P.act(ge, e1, AF.Exp, scale=-C_DEC)
                P.tt('dve', B_["KKt"], kkn, ge, ALU.mult)
                P.act(ginv, Gs, AF.Exp, scale=C_DEC)
                P.tt('pool', B_["Kt"], kmod, ginv, ALU.mult)
                P.tt('dve', B_["Bt"], bb, ginv, ALU.mult)
                GsC = Gs.re("p (q t) -> p q t", t=128)[:, :, 127]
                P.ts('dve', nb4, GsC, -C_DEC, None, ALU.mult)
                for q in range(4):
                    P.act(gcr[:, q * 128:(q + 1) * 128], Gs[:, q * 128:(q + 1) * 128], AF.Exp, scale=C_DEC,
                          bias=nb4[:, q:q + 1])
                P.tt('dve', B_["Kh"], kmod, gcr, ALU.mult)
                P.tt('pool', B_["Bh"], bb, gcr, ALU.mult)
                P.act(gC4, nb4, AF.Exp)
                P.dma(gc_d[prow, b * 4:(b + 1) * 4], gC4)
                P.tt('pool', rk, rs, kmod, ALU.mult)
                P.ts('pool', B_["rkb"], rk, vecs[:, l, VRK + hp:VRK + hp + 1], None, ALU.mult)
                bkb = P.bank()
                for q in range(4):
                    P.mm(bkb[:, q * 2:(q + 1) * 2], B_["rkb"][:, q * 128:(q + 1) * 128], bcolsb)
                P.cp('dve', sb4, bkb[:, 0:8].re("p (q h) -> p q h", h=2))
                P.dma(sbon_d.re("(q p) h -> p q h", p=128)[:, b * 4:(b + 1) * 4, 2 * hp:2 * hp + 2], sb4)
                P.cp('pool', B_["Vb"], vs)
                for opi, nm in enumerate(("Kt", "Bt", "KKt", "Rt")):
                    for hh in range(2):
                        P.dma(feat_d[b * 4:(b + 1) * 4, :, 2 * hp + hh, opi, :].re("c k t -> k c t"),
                              B_[nm][hh * 64:(hh + 1) * 64, :].re("k (c t) -> k c t", c=4))
                for q in range(4):
                    bkt = P.bank().cast(BF16)
                    for opi, nm in enumerate(("KKt", "Vb", "Kh", "Bh")):
                        P.tr(bkt[:, opi * 128:(opi + 1) * 128], B_[nm][:, q * 128:(q + 1) * 128], identb)
                    P.cp('act' if q % 2 else 'dve', tmo[:, q, :, :], bkt[:, 0:512].re("p (o f) -> p o f", o=4))
                for q in range(4):
                    P.dma(tokm_d[t0 + q * 128:t0 + (q + 1) * 128, :, cols], tmo[:, q, :, :])

            chk(f'rw_{l}')
            for gi_, j in enumerate([13, 14, 15, 16, 22, 23, 24, 25]):
                bk = proj(j)
                g = gst[gi_ % 2]
                P.act(g, bk, AF.Silu)
                P.dma(gate_d[gi_ * 128:(gi_ + 1) * 128, tsl], g)
            chk(f'g_{l}')
            for i in range(3):
                bk = proj(17 + i)
                P.cp('act', cq[:, i, :], bk)
                P.act(sqq[:, i, :], bk, AF.Square)
            bk = P.bank()
            for i in range(3):
                P.mm(bk, onesb, sqq[:, i, :], start=(i == 0), stop=(i == 2))
            P.act(rq, bk, AF.Sqrt, bias=epsn, scale=1.0 / 384)
            P.recip(rq, rq)
            chk(f'q1_{l}')
            P.tt('pool', crs, csb[:, 0, :], rq, ALU.mult)
            P.tt('pool', srs, csb[:, 1, :], rq, ALU.mult)
            for i in range(4):
                bk = P.bank()
                for kc in range(3):
                    P.mm(bk, Wq[:, kc, i * 128:(i + 1) * 128], cq[:, kc, :], start=(kc == 0), stop=(kc == 2))
                qq = qn[i % 2]
                P.tt('dve', qq, bk, rq, ALU.mult)
                for hh in range(2):
                    P.dma(qT_d[2 * i + hh, 0:64, tsl], qq[hh * 64:(hh + 1) * 64, :])
            chk(f'q2_{l}')
            for i in range(2):
                bk1 = P.bank()
                for kc in range(3):
                    P.mm(bk1, Wq[:, kc, 512 + i * 128:512 + (i + 1) * 128], cq[:, kc, :], start=(kc == 0),
                         stop=(kc == 2))
                bk2 = P.bank()
                for kc in range(3):
                    P.mm(bk2, Wq[:, kc, 768 + i * 128:768 + (i + 1) * 128], cq[:, kc, :], start=(kc == 0),
                         stop=(kc == 2))
                P.tt('dve', a1, bk1, crs, ALU.mult)
                P.tt('dve', a2, bk2, srs, ALU.mult)
                qq = qrp[i % 2]
                P.tt('dve', qq, a1, a2, ALU.add)
                for hh in range(4):
                    P.dma(qT_d[4 * i + hh, 64:96, tsl], qq[hh * 32:(hh + 1) * 32, :])
            if 'dbg' in tapset and b == NB - 1:
                dbg_d = P.dram("dbg", [6, 128, 512], F32, kind="ExternalOutput")
                for ii, tl in enumerate((a1, a2, crs, srs, rq, csb[:, 0, :])):
                    P.dma(dbg_d[ii], tl)
            chk(f'q_{l}')
            for i in range(2):
                bk = proj(20 + i)
                P.cp('act', ckv[:, i, :], bk)
                P.act(sqk[:, i, :], bk, AF.Square)
            bk = P.bank()
            for i in range(2):
                P.mm(bk, onesb, sqk[:, i, :], start=(i == 0), stop=(i == 1))
            P.act(rkv, bk, AF.Sqrt, bias=epsn, scale=1.0 / 256)
            P.recip(rkv, rkv)
            for i in range(4):
                bk = P.bank()
                for kc in range(2):
                    P.mm(bk, Wkv[:, kc, i * 128:(i + 1) * 128], ckv[:, kc, :], start=(kc == 0), stop=(kc == 1))
                qq = qn[i % 2]
                P.tt('dve', qq, bk, rkv, ALU.mult)
                for hh in range(2):
                    P.dma(kT_d[2 * i + hh, :, tsl], qq[hh * 64:(hh + 1) * 64, :])
            for q in range(4):
                qs = slice(q * 128, (q + 1) * 128)
                bkc = P.bank()
                for i in range(2):
                    P.mm(bkc[:, 0:1], sqk[:, i, qs], onesb[:, 0:1], start=(i == 0), stop=(i == 1))
                P.act(rcol, bkc[:, 0:1], AF.Sqrt, bias=epsn, scale=1.0 / 256)
                P.recip(rcol, rcol)
                bkv = P.bank()
                for i in range(2):
                    P.mm(bkv, ckv[:, i, qs], Wkv[:, i, 512:1024], start=(i == 0), stop=(i == 1))
                P.ts('dve', v1s[:, q, :, 0:64], bkv.re("p (h e) -> p h e", e=64), rcol, None, ALU.mult)
            for q in range(4):
                P.dma(v1_d[t0 + q * 128:t0 + (q + 1) * 128], v1s[:, q, :, :])
        P.release(mL)
        if stop == f'p1_{l}':
            return True

        gcs = P.al("gcs", [64, NH, NT])
        P.dma(gcs, gc_d.re("(h k) c -> k h c", k=64))
        sbn = P.al("sbn", [128, NT, NH])
        P.dma(sbn, sbon_d.re("(c p) h -> p c h", p=128))
        lnw = P.al("lnw", [128, 1024])
        P.dma(lnw, V(rows_d.ap[l:l + 1, :].partition_broadcast(128), rows_d.buf))
        lnw3 = lnw[:, 0:512].re("p (h e) -> p h e", e=64)
        lnb3 = lnw[:, 512:1024].re("p (h e) -> p h e", e=64)
        Sf = P.al("Sf", [64, NH, 64])
        Sb = P.al("Sb", [64, NH, 64], BF16)
        P.memset('pool', Sf, 0.0)
        P.memset('pool', Sb, 0.0)
        GL = 3

        def slot_tiles(i):
            d = {}
            d['F'] = P.al(f"F{i}", [64, NH, 4, 128], BF16)
            d['T'] = P.al(f"T{i}", [128, 4, 512], BF16)
            d['XK'] = P.al(f"XK{i}", [128, NH, 128], BF16)
            d['AT'] = P.al(f"AT{i}", [128, NH, 512], BF16)
            for nm in ('Nk', 'Dk', 'Wk', 'Y1', 'Dl'):
                d[nm] = [P.al(f"{nm}{i}_{g}", [128, 4, 128], BF16) for g in range(2)]
            d['MU'] = [P.al(f"MU{i}_{g}", [128, 4, 6, 128], BF16) for g in range(2)]
            d['nPQ'] = P.al(f"nPQ{i}", [128, NH, 128], BF16)
            d['nZ'] = P.al(f"nZ{i}", [64, NH, 64], BF16)
            d['Psi'] = P.al(f"Psi{i}", [64, NH, 64])
            d['Ry'] = P.al(f"Ry{i}", [64, NH, 128], BF16)
            d['t1'] = P.al(f"t1{i}", [64, NH, 64])
            d['y'] = P.al(f"y{i}", [128, NH, 64])
            d['yc'] = P.al(f"yc{i}", [128, NH, 64])
            d['ysq'] = P.al(f"ysq{i}", [128, NH, 64])
            d['mean'] = P.al(f"mean{i}", [128, NH])
            d['var'] = P.al(f"var{i}", [128, NH])
            d['ybf'] = P.al(f"ybf{i}", [128, 512], BF16)
            return d
        slots = [slot_tiles(i) for i in range(GL)]

        def chunk_gen(c):
            d = slots[c % GL]
            F, T, XK, AT, Nk, Dk, Wk, Y1, Dl, MU = (d[k] for k in ('F', 'T', 'XK', 'AT', 'Nk', 'Dk', 'Wk', 'Y1', 'Dl', 'MU'))
            nPQ, nZ, Psi, Ry, t1, y, yc, ysq, mean, var, ybf = (d[k] for k in
                                                               ('nPQ', 'nZ', 'Psi', 'Ry', 't1', 'y', 'yc', 'ysq', 'mean', 'var', 'ybf'))
            crow = slice(c * 128, (c + 1) * 128)
            P.dma(F, feat_d[c])
            P.dma(T, tokm_d[crow])
            P.dma(XK[:, :, 0:64], tokm_d[crow, 0, :].re("p (h e) -> p h e", e=64))
            yield
            for h in range(NH):
                bk = P.bank()
                KR = F[:, h, 2:4, :].re("k o t -> k (o t)")
                P.mm(bk[:, 0:256], F[:, h, 0, :], KR)
                P.mm(bk[:, 256:512], F[:, h, 1, :], KR)
                P.tt('dve', AT[:, h, :], bk, maskT, ALU.mult)
                if h % 4 == 3:
                    yield
            for g in range(2):
                bkn = P.bank()
                for hh in range(4):
                    h = 4 * g + hh
                    P.mm(bkn[:, hh * 128:(hh + 1) * 128], F[:, h, 2, :], F[:, h, 1, :])
                P.tt('dve', Nk[g], bkn.re("p (h s) -> p h s", h=4), masklow.un(1).bc([128, 4, 128]), ALU.mult)
                m0b = masku[:, 0, :].un(1).bc([128, 4, 128])
                idb = identb.un(1).bc([128, 4, 128])
                P.tt('pool', Dk[g], Nk[g], m0b, ALU.mult)
                P.tt('pool', Dk[g], Dk[g], idb, ALU.add)
                P.tt('pool', Wk[g], AT[:, 4 * g:4 * g + 4, 256:384], m0b, ALU.mult)
                P.tt('pool', Wk[g], Wk[g], idb, ALU.add)
                for hh in range(4):
                    P.tt('pool', MU[g][:, hh, :, :], AT[:, 4 * g + hh, 256:384].un(1).bc([128, 6, 128]),
                         masku[:, 1:7, :], ALU.mult)
                yield
            for j in range(1, 7):
                bkYs = []
                for g in range(2):
                    bkY = P.bank()
                    for hh in range(4):
                        P.mm(bkY[:, hh * 128:(hh + 1) * 128], MU[g][:, hh, j - 1, :], Dk[g][:, hh, :])
                    P.cp('act', Y1[g], bkY.re("p (h s) -> p h s", h=4))
                yield
                bkDs = []
                for g in range(2):
                    bkD = P.bank()
                    bkDs.append(bkD)
                    for hh in range(4):
                        P.mm(bkD[:, hh * 128:(hh + 1) * 128], Wk[g][:, hh, :], Y1[g][:, hh, :])
                    P.cp('act', Dl[g], bkD.re("p (h s) -> p h s", h=4))
                    if j < 6:
                        P.tt('dve', Dk[g], bkD.re("p (h s) -> p h s", h=4), Dk[g], ALU.add)
                yield
                for g in range(2):
                    bkT = P.bank().cast(BF16)
                    for hh in range(4):
                        P.tr(bkT[:, hh * 128:(hh + 1) * 128], Dl[g][:, hh, :], identb)
                    P.tt('dve', Wk[g], bkT[:, 0:512].re("p (h s) -> p h s", h=4), Wk[g], ALU.add)
                yield
            bkx = P.bank()
            for h in range(NH):
                hc = slice(h * 64, (h + 1) * 64)
                P.mm(bkx[:, hc], AT[:, h, 0:128], T[:, 1, hc])
            P.cp('act', XK[:, :, 64:128], bkx.re("p (h e) -> p h e", e=64))
            yield
            for g in range(2):
                bkp = P.bank()
                for hh in range(4):
                    h = 4 * g + hh
                    P.mm(bkp[:, hh * 128:(hh + 1) * 128], Wk[g][:, hh, :], XK[:, h, :])
                P.act(nPQ[:, 4 * g:4 * g + 4, :], bkp.re("p (h s) -> p h s", h=4), AF.Copy, scale=-1.0)
            yield
            bkz = P.bank()
            for h in range(NH):
                hc = slice(h * 64, (h + 1) * 64)
                P.mm(bkz[0:64, hc], nPQ[:, h, 0:64], T[:, 3, hc])
            P.cp('act', nZ, bkz[0:64, :].re("k (h e) -> k h e", e=64))
            bkpsi = P.bank()
            for h in range(NH):
                hc = slice(h * 64, (h + 1) * 64)
                P.mm(bkpsi[0:64, hc], T[:, 2, hc], T[:, 1, hc], start=True, stop=False)
                P.mm(bkpsi[0:64, hc], T[:, 3, hc], nPQ[:, h, 64:128], start=False, stop=True)
            P.cp('dve', Psi, bkpsi[0:64, :].re("k (h e) -> k h e", e=64))
            for g in range(2):
                bkr = P.bank()
                for hh in range(4):
                    h = 4 * g + hh
                    P.mm(bkr[0:64, hh * 128:(hh + 1) * 128], nPQ[:, h, 0:64], AT[:, h, 384:512])
                P.tt('dve', Ry[:, 4 * g:4 * g + 4, :], bkr[0:64, :].re("k (h t) -> k h t", h=4),
                     F[:, 4 * g:4 * g + 4, 3, :], ALU.add)
            yield
            bky = P.bank()
            for h in range(NH):
                hc = slice(h * 64, (h + 1) * 64)
                P.mm(bky[:, hc], AT[:, h, 128:256], T[:, 1, hc], start=True, stop=False)
                P.mm(bky[:, hc], AT[:, h, 384:512], nPQ[:, h, 64:128], start=False, stop=False)
                P.mm(bky[:, hc], Ry[:, h, :], Sb[:, h, :], start=False, stop=True)
            bku = P.bank()
            for h in range(NH):
                hc = slice(h * 64, (h + 1) * 64)
                P.mm(bku[0:64, hc], nZ[:, h, :], Sb[:, h, :])
            P.tt('pool', t1, Sf, gcs[:, :, c].un(2).bc([64, NH, 64]), ALU.mult)
            P.tt('pool', t1, t1, Psi, ALU.add)
            bku3 = bku[0:64, :].re("k (h e) -> k h e", e=64)
            P.tt('dve', Sb, bku3, t1, ALU.add)
            P.tt('dve', Sf, bku3, t1, ALU.add)
            yield
            P.cp('act', y, bky.re("p (h e) -> p h e", e=64))
            P.reduce(mean, y)
            P.ts('dve', mean, mean, 1.0 / 64, None, ALU.mult)
            P.tt('pool', yc, y, mean.un(2).bc([128, NH, 64]), ALU.subtract)
            P.tt('pool', ysq, yc, yc, ALU.mult)
            P.reduce(var, ysq)
            P.act(var, var, AF.Sqrt, bias=epsg, scale=1.0 / 64)
            P.recip(var, var)
            yield
            P.tt('pool', yc, yc, var.un(2).bc([128, NH, 64]), ALU.mult)
            P.tt('pool', yc, yc, lnw3, ALU.mult)
            P.tt('pool', yc, yc, lnb3, ALU.add)
            P.tt('pool', ysq, T[:, 1, :].re("p (h e) -> p h e", e=64), sbn[:, c, :].un(2).bc([128, NH, 64]), ALU.mult)
            P.tt('pool', ybf.re("p (h e) -> p h e", e=64), yc, ysq, ALU.add)
            P.dma(ytok_d[crow, 0:512], ybf)

        def run_lockstep(gens):
            live = list(gens)
            while live:
                nxt = []
                for gg in live:
                    try:
                        next(gg)
                        nxt.append(gg)
                    except StopIteration:
                        pass
                live = nxt
        for c0 in range(0, NT, GL):
            run_lockstep([chunk_gen(c) for c in range(c0, min(NT, c0 + GL))])
        P.release(mL)
        if stop == f'p2_{l}':
            return True

        v1a = P.al("v1a", [128, NT, NH * 65], BF16)
        P.dma(v1a, v1_d.re("(j p) h e -> p j (h e)", p=128))
        kTq = [P.al(f"kTh{i}", [96, S], BF16) for i in range(2)]
        qTq = [P.al(f"qTh{i}", [96, S], BF16) for i in range(2)]
        Et = [P.al(f"E{i}", [128, 512], BF16) for i in range(3)]
        trib = P.al("trib", [128, 128], BF16)
        P.cp('pool', trib, tri)
        rden = P.al("rden", [128, 1])
        yh = [P.al(f"yh{i}", [128, 64], BF16) for i in range(2)]
        Ob = P.banks[0:4]
        Sbk = P.banks[4:8]
        nsb = 0
        for h in range(NH):
            kTh = kTq[h % 2]
            qTh = qTq[h % 2]
            P.dma(kTh[0:64], kT_d[h])
            P.dma(kTh[64:96], krT_d)
            P.dma(qTh, qT_d[h])
            for Q in range(NB):
                nkb = 4 * Q + 4
                qend = (Q + 1) * 512

                def qk(j):
                    nonlocal nsb
                    qlo = max(Q * 512, j * 128)
                    N = qend - qlo
                    bs = Sbk[nsb % 4]
                    E = Et[nsb % 3]
                    nsb += 1
                    P.mm(bs[:, 0:N], kTh[:, j * 128:(j + 1) * 128], qTh[:, qlo:qend])
                    P.act(E[:, 0:N], bs[:, 0:N], AF.Exp, scale=ATT_SCALE)
                    if j >= 4 * Q:
                        P.tt('pool', E[:, 0:128], E[:, 0:128], trib, ALU.mult)
                    return E, qlo

                def pv(j, E, qlo):
                    for t in range(max(4 * Q, j), 4 * Q + 4):
                        off = t * 128 - qlo
                        P.mm(Ob[t - 4 * Q][:, 0:65], E[:, off:off + 128], v1a[:, j, h * 65:(h + 1) * 65],
                             start=(j == 0), stop=(j == t))
                pend = qk(0)
                for j in range(nkb):
                    nxt = qk(j + 1) if j + 1 < nkb else None
                    pv(j, *pend)
                    pend = nxt
                for tq in range(4):
                    t = 4 * Q + tq
                    yy = yh[tq % 2]
                    P.recip(rden, Ob[tq][:, 64:65])
                    P.ts('dve', yy, Ob[tq][:, 0:64], rden, None, ALU.mult)
                    P.dma(ytok_d[t * 128:(t + 1) * 128, 512 + h * 64:512 + (h + 1) * 64], yy)
        P.release(mL)
        if stop == f'p3_{l}':
            return True

        Wo = P.al("Wo", [128, 8, D], BF16)
        ost = P.al("ost", [128, D])
        for kc in range(8):
            P.dma(ost, wout_d[l, kc * 128:(kc + 1) * 128, :])
            P.cp('pool' if kc % 2 else 'dve', Wo[:, kc, :], ost)
        yt = P.al("yt", [128, 4, D], BF16)
        gt = P.al("gt", [128, 8, 512], BF16)
        xt4 = P.al("xt4", [128, 8, 512])
        yT = P.al("yT", [128, 8, 512], BF16)
        for b in range(NB):
            tsl = slice(b * 512, (b + 1) * 512)
            P.dma(yt, ytok_d[tsl].re("(q p) f -> p q f", p=128))
            P.dma(gt, gate_d.re("(c p) t -> p c t", p=128)[:, :, tsl])
            P.dma(xt4, xT_v[:, :, tsl])
            for f in range(8):
                bkt = P.bank().cast(BF16)
                for q in range(4):
                    P.tr(bkt[:, q * 128:(q + 1) * 128], yt[:, q, f * 128:(f + 1) * 128], identb)
                P.tt('dve', yT[:, f, :], bkt[:, 0:512], gt[:, f, :], ALU.mult)
            for dch in range(8):
                bk = P.bank()
                for f in range(8):
                    P.mm(bk, Wo[:, f, dch * 128:(dch + 1) * 128], yT[:, f, :], start=(f == 0), stop=(f == 7))
                P.stt(xt4[:, dch, :], bk, mods[:, l, 16 + dch:17 + dch], xt4[:, dch, :], ALU.mult, ALU.add)
            P.dma(xT_v[:, :, tsl], xt4)
        P.release(mL)
        return False

    try:
        for l in range(L):
            if layer(l):
                return finish()
    except _Stop:
        return finish()

    m0 = P.mark()
    fg = P.al("fg", [128, D])
    P.dma(fg, V(fing_d.ap.partition_broadcast(128), fing_d.buf))
    xtf = P.al("xtf", [128, 8, 512])
    xo = [P.al("xo0", [128, D]), P.al("xo1", [128, D])]
    junk = P.al("junkf", [128, D])
    ssq = [P.al("ssq0", [128, 1]), P.al("ssq1", [128, 1])]
    for b in range(NB):
        t0 = b * 512
        P.dma(xtf, xT_v[:, :, t0:t0 + 512])
        for j in range(4):
            bk0 = P.bank()
            bk1 = P.bank()
            for c in range(8):
                bk = bk0 if c < 4 else bk1
                P.tr(bk[:, (c % 4) * 128:(c % 4 + 1) * 128], xtf[:, c, j * 128:(j + 1) * 128], ident)
            o = xo[j % 2]
            s = ssq[j % 2]
            P.cp('dve', o[:, 0:512], bk0)
            P.cp('act', o[:, 512:1024], bk1)
            P.act(junk, o, AF.Square, accum=s)
            P.act(s, s, AF.Sqrt, bias=epsn, scale=1.0 / D)
            P.recip(s, s)
            P.stt(o, o, s, fg, ALU.mult, ALU.mult)
            P.dma(out_d[t0 + j * 128:t0 + (j + 1) * 128, :], o, final=True)
    P.release(m0)
    return finish()


def make_masku():
    i = np.arange(128)
    m = np.zeros((128, 7, 128), np.float32)
    for j in range(7):
        bs = 1 << j
        m[:, j, :] = ((i[:, None] // (2 * bs)) == (i[None, :] // (2 * bs))) & ((i[:, None] // bs) != (i[None, :] // bs))
    return m.reshape(128, 7 * 128).astype(ml_dtypes.bfloat16)


def host_layout(inputs, S, L):
    f = lambda a: np.ascontiguousarray(np.asarray(a, dtype=np.float32))

    def cols(v):
        v = np.asarray(v, np.float32)
        return v.reshape(-1, 128).T

    vecs = np.zeros((128, L, NV), np.float32)
    rows = np.zeros((L, 1024), np.float32)
    for l in range(L):
        vecs[:, l, VG:VG + 8] = cols(inputs['norm_g'][l])
        vecs[:, l, VB:VB + 24] = cols(inputs['b_ada'][l])
        vecs[:, l, VMU:VMU + 13] = cols(inputs['mu_shift'][l])
        if l > 0:
            vecs[64:96, l, VMUV] = np.asarray(inputs['mu_vmix'][l - 1], np.float32)
            vecs[:, l, VV0:VV0 + 4] = cols(inputs['v0'][l - 1])
        vecs[:, l, VW0:VW0 + 4] = cols(inputs['w0'][l])
        vecs[:, l, VA0:VA0 + 4] = cols(inputs['a0'][l])
        vecs[:, l, VKK:VKK + 4] = cols(inputs['k_k'][l])
        vecs[:, l, VKA:VKA + 4] = cols(inputs['k_a'][l])
        vecs[:, l, VRK:VRK + 4] = cols(np.asarray(inputs['r_k'][l]).reshape(-1))
        vecs[:, l, VQG:VQG + 3] = cols(inputs['q_norm_g'][l])
        vecs[:, l, VKVG:VKVG + 2] = cols(inputs['kv_norm_g'][l])
        rows[l, 0:512] = np.asarray(inputs['lnx_w'][l], np.float32)
        rows[l, 512:1024] = np.asarray(inputs['lnx_b'][l], np.float32)
    shared = {
        "consts": make_consts(), "vecs": vecs, "rows": rows, "masku": make_masku(),
        "final_g": f(inputs['final_g']).reshape(1, D),
        "w_ada": f(inputs['w_ada'])[:L], "w_in": f(inputs['w_in'])[:L],
        "w_vd": f(inputs['w_vmix_down'])[:max(L - 1, 1)],
        "w_dec": f(inputs['w_decay_up'])[:L], "w_icl": f(inputs['w_iclr_up'])[:L],
        "w_vup": f(inputs['w_vmix_up'])[:max(L - 1, 1)],
        "w_uq": f(inputs['w_uq'])[:L], "w_ukv": f(inputs['w_ukv'])[:L], "w_out": f(inputs['w_out'])[:L],
    }
    x = np.asarray(inputs['x'], np.float32)
    c = np.asarray(inputs['c'], np.float32)
    pos = np.asarray(inputs['positions']).astype(np.int32)
    B = x.shape[0]
    per = []
    for b in range(B):
        m = dict(shared)
        m["x"] = np.ascontiguousarray(x[b, :S])
        m["cT"] = np.ascontiguousarray(c[b].reshape(8, 128).T)
        m["pos"] = np.ascontiguousarray(pos[b, :S].reshape(1, S))
        per.append(m)
    return per


_NC_CACHE = {}


def kernel(**inputs):
    S, L = 4096, 4
    B = np.asarray(inputs['x']).shape[0]
    per = host_layout(inputs, S, L)
    if (S, L) not in _NC_CACHE:
        _NC_CACHE[(S, L)] = build(S, L)
    nc = _NC_CACHE[(S, L)]
    res = run_bass_kernel_spmd(nc, per, core_ids=list(range(B)))
    return np.stack([np.asarray(r["out"], np.float32) for r in res.results], axis=0)
```

```python
import numpy as np
import ml_dtypes
import concourse.bass as bass
import concourse.mybir as mybir
from concourse.bass_utils import run_bass_kernel_spmd
from contextlib import ExitStack

F32 = mybir.dt.float32
BF16 = mybir.dt.bfloat16
I32 = mybir.dt.int32
ALU = mybir.AluOpType
AF = mybir.ActivationFunctionType
AX = mybir.AxisListType

ENGS = ['pe', 'act', 'dve', 'pool', 'sp']
EPOCH = 20000
NDSEM = 24
DT_BYTES = {F32: 4, BF16: 2, I32: 4}


class Buf:
    __slots__ = ('name', 'last_w', 'readers', 'dma_ws', 'excl')

    def __init__(self, name, excl=False):
        self.name = name
        self.excl = excl
        self.last_w = None
        self.dma_ws = []
        self.readers = {}


class V:
    __slots__ = ('ap', 'buf')

    def __init__(self, ap, buf):
        self.ap = ap
        self.buf = buf

    def __getitem__(self, k):
        return V(self.ap[k], self.buf)

    def re(self, s, **kw):
        return V(self.ap.rearrange(s, **kw), self.buf)

    def bc(self, shape):
        return V(self.ap.to_broadcast(list(shape)), self.buf)

    def un(self, ax):
        return V(self.ap.unsqueeze(ax), self.buf)

    def cast(self, dt):
        return V(self.ap.bitcast(dt), self.buf)


class Op:
    __slots__ = ('eng', 'fn', 'deps', 'sig', 'is_dma', 'has_dep', 'gidx')

    def __init__(self, eng, fn, is_dma):
        self.eng = eng
        self.fn = fn
        self.is_dma = is_dma
        self.deps = []
        self.sig = None
        self.has_dep = False


class Prog:
    def __init__(self, nc, es, arena_bytes):
        self.nc = nc
        self.es = es
        self.ops = {e: [] for e in ENGS}
        self.n = 0
        self.final_dmas = []
        h = es.enter_context(nc.sbuf_tensor("arena", [128, arena_bytes // 2], BF16))
        self.arena = h[:]
        self.arena_bytes = arena_bytes
        self.live = []
        self.sp_ = 0
        self.banks = []
        for i in range(8):
            hb = es.enter_context(nc.psum_tensor(f"bank{i}", [128, 512], F32))
            self.banks.append(V(hb[:], Buf(f"bank{i}", excl=True)))
        self.bi = 0

    def sb(self, name, shape, dt=F32):
        h = self.es.enter_context(self.nc.sbuf_tensor("s_" + name, list(shape), dt))
        return V(h[:], Buf(name))

    def mark(self):
        return self.sp_

    def release(self, m):
        self.sp_ = m

    def al(self, name, shape, dt=F32):
        per = int(np.prod(shape[1:])) * DT_BYTES[dt]
        start = (self.sp_ + 63) // 64 * 64
        end = start + per
        assert end <= self.arena_bytes, (name, end, self.arena_bytes)
        self.sp_ = end
        b = Buf(name)
        keep = []
        for (s0, e0, ob) in self.live:
            if s0 < end and start < e0:
                cands = list(ob.readers.values()) + list(ob.dma_ws)
                if ob.last_w is not None:
                    cands.append(ob.last_w)
                for d in cands:
                    k = ('dma', id(d)) if d.is_dma else d.eng
                    if k not in b.readers or (not d.is_dma and b.readers[k].gidx < d.gidx):
                        b.readers[k] = d
                if not (start <= s0 and e0 <= end):
                    keep.append((s0, e0, ob))
            else:
                keep.append((s0, e0, ob))
        keep.append((start, end, b))
        self.live = keep
        ap = self.arena[0:shape[0], start // 2: end // 2]
        if dt != BF16:
            ap = ap.bitcast(dt)
        if len(shape) == 3:
            ap = ap.rearrange("p (a b) -> p a b", a=shape[1])
        elif len(shape) == 4:
            ap = ap.rearrange("p (a b c) -> p a b c", a=shape[1], b=shape[2])
        return V(ap, b)

    def bank(self):
        b = self.banks[self.bi % 8]
        self.bi += 1
        return b

    def dram(self, name, shape, dt, kind="Internal"):
        t = self.nc.dram_tensor(name, list(shape), dt, kind=kind)
        return V(t.ap(), Buf(name))

    def op(self, eng, fn, r=(), w=(), is_dma=False):
        o = Op(eng, fn, is_dma)
        o.gidx = self.n
        self.n += 1
        deps = {}

        def add(d):
            if d is None or d is o:
                return
            if d.is_dma:
                deps[('dma', id(d))] = d
            else:
                if d.eng == eng and eng == 'pe' and not is_dma:
                    return
                k = d.eng
                if k not in deps or deps[k].gidx < d.gidx:
                    deps[k] = d
        rb = [x.buf for x in r]
        wb = [x.buf for x in w]
        for b in rb:
            add(b.last_w)
            for d in b.dma_ws:
                add(d)
            if b.excl:
                for d in b.readers.values():
                    if d.eng != eng:
                        add(d)
        for b in wb:
            add(b.last_w)
            for d in b.readers.values():
                add(d)
            if not is_dma:
                for d in b.dma_ws:
                    add(d)
        for b in wb:
            if is_dma:
                if b.readers:
                    b.dma_ws = []
                    b.readers = {}
                b.dma_ws.append(o)
            else:
                b.last_w = o
                b.dma_ws = []
                b.readers = {}
        for b in rb:
            if b in wb:
                continue
            if is_dma:
                b.readers[('dma', id(o))] = o
            else:
                b.readers[eng] = o
        o.deps = list(deps.values())
        for d in o.deps:
            d.has_dep = True
        self.ops[eng].append(o)
        return o

    def dma(self, out, in_, eng='sp', final=False, **kw):
        o = self.op(eng, lambda e: e.dma_start(out=out.ap, in_=in_.ap, **kw), r=[in_], w=[out], is_dma=True)
        o.has_dep = True
        if final:
            self.final_dmas.append(o)
        return o

    def mm(self, out, lhsT, rhs, start=True, stop=True, **kw):
        return self.op('pe', lambda e: e.matmul(out.ap, lhsT.ap, rhs.ap, start=start, stop=stop, **kw),
                       r=[lhsT, rhs] + ([] if start else [out]), w=[out])

    def tr(self, out, in_, ident):
        return self.op('pe', lambda e: e.transpose(out.ap, in_.ap, ident.ap), r=[in_, ident], w=[out])

    def act(self, out, in_, func, bias=None, scale=None, accum=None):
        r = [in_]
        kw = {}
        if bias is not None:
            if isinstance(bias, V):
                r.append(bias)
                kw['bias'] = bias.ap
            else:
                kw['bias'] = float(bias)
        if scale is not None:
            if isinstance(scale, V):
                r.append(scale)
                kw['scale'] = scale.ap
            else:
                kw['scale'] = float(scale)
        w = [out]
        if accum is not None:
            kw['accum_out'] = accum.ap
            w.append(accum)
        return self.op('act', lambda e: e.activation(out.ap, in_.ap, func, **kw), r=r, w=w)

    def tt(self, eng, out, in0, in1, op):
        return self.op(eng, lambda e: e.tensor_tensor(out.ap, in0.ap, in1.ap, op), r=[in0, in1], w=[out])

    def ts(self, eng, out, in0, s1, s2=None, op0=ALU.mult, op1=None):
        r = [in0]
        a1 = s1.ap if isinstance(s1, V) else float(s1)
        if isinstance(s1, V):
            r.append(s1)
        a2 = None
        if s2 is not None:
            a2 = s2.ap if isinstance(s2, V) else float(s2)
            if isinstance(s2, V):
                r.append(s2)
        if op1 is None:
            return self.op(eng, lambda e: e.tensor_scalar(out.ap, in0.ap, a1, None, op0), r=r, w=[out])
        return self.op(eng, lambda e: e.tensor_scalar(out.ap, in0.ap, a1, a2, op0, op1), r=r, w=[out])

    def stt(self, out, in0, s, in1, op0, op1):
        r = [in0, in1]
        a = s.ap if isinstance(s, V) else float(s)
        if isinstance(s, V):
            r.append(s)
        return self.op('dve', lambda e: e.scalar_tensor_tensor(out.ap, in0.ap, a, in1.ap, op0, op1), r=r, w=[out])

    def cp(self, eng, out, in_):
        if eng == 'act':
            return self.op('act', lambda e: e.copy(out.ap, in_.ap), r=[in_], w=[out])
        return self.op(eng, lambda e: e.tensor_copy(out.ap, in_.ap), r=[in_], w=[out])

    def memset(self, eng, out, val):
        return self.op(eng, lambda e: e.memset(out.ap, val), r=[], w=[out])

    def scan(self, out, d0, d1, init, op0, op1):
        return self.op('dve', lambda e: e.tensor_tensor_scan(out.ap, d0.ap, d1.ap, float(init), op0, op1),
                       r=[d0, d1], w=[out])

    def recip(self, out, in_):
        return self.op('dve', lambda e: e.reciprocal(out.ap, in_.ap), r=[in_], w=[out])

    def reduce(self, out, in_, op=ALU.add):
        return self.op('dve', lambda e: e.tensor_reduce(out.ap, in_.ap, AX.X, op), r=[in_], w=[out])

    def emit(self):
        nc = self.nc
        sem_names = []
        for eng in ENGS:
            cnt = 0
            dcnt = 0
            for o in self.ops[eng]:
                if o.is_dma:
                    j = dcnt % NDSEM
                    name = f"d_{eng}_{j}"
                    o.sig = (name, 16 * (dcnt // NDSEM + 1), 16)
                    dcnt += 1
                elif o.has_dep:
                    ep = cnt // EPOCH
                    name = f"c_{eng}_{ep}"
                    o.sig = (name, cnt % EPOCH + 1, 1)
                    cnt += 1
                else:
                    continue
                if name not in sem_names:
                    sem_names.append(name)
        sems = {}
        for nm in sem_names:
            sems[nm] = self.es.enter_context(nc.semaphore(nm))
        self.nsem = len(sem_names)
        block = self.es.enter_context(nc.Block())
        prog = self

        def run(eng, e):
            known = {}
            for o in prog.ops[eng]:
                if o.is_dma:
                    nm, val, inc = o.sig
                    if val > 16 and known.get(nm, 0) < val - 16:
                        e.wait_ge(sems[nm], val - 16)
                        known[nm] = val - 16
                for d in o.deps:
                    nm, val, inc = d.sig
                    if known.get(nm, 0) < val:
                        e.wait_ge(sems[nm], val)
                        known[nm] = val
                inst = o.fn(e)
                if o.sig is not None:
                    nm, val, inc = o.sig
                    inst.then_inc(sems[nm], inc)
            if eng == 'sp':
                last = {}
                for en in ENGS:
                    for o in prog.ops[en]:
                        if o.is_dma:
                            nm, val, inc = o.sig
                            last[nm] = max(last.get(nm, 0), val)
                for nm, val in last.items():
                    if known.get(nm, 0) < val:
                        e.wait_ge(sems[nm], val)
                        known[nm] = val

        @block.tensor
        def _(e):
            run('pe', e)

        @block.scalar
        def _(e):
            run('act', e)

        @block.vector
        def _(e):
            run('dve', e)

        @block.gpsimd
        def _(e):
            run('pool', e)

        @block.sync
        def _(e):
            run('sp', e)


D = 1024
NKC = 8
NH = 8
NCH = 28
WCOLS = NCH * 128
C_DEC = float(np.exp(-0.5))
ATT_SCALE = float(96 ** -0.5)
NORM_EPS = 1e-6
GN_EPS = 64e-5
NV = 80
VG, VB, VMU, VMUV, VW0, VA0, VV0, VKK, VKA, VRK, VQG, VKVG = 0, 8, 32, 45, 46, 50, 54, 58, 62, 66, 70, 73
CI, CMT, CML, CTRI, CBO, CBC, CINV, CRST, NCC = 0, 128, 640, 768, 896, 1024, 1026, 1028, 1540
TWO_PI = float(2 * np.pi)
CW1 = 6.28125
CW2 = float(2 * np.pi - 6.28125)


def make_consts():
    c = np.zeros((128, NCC), np.float32)
    i = np.arange(128)
    c[:, CI:CI + 128] = np.eye(128)
    strict = (i[None, :] > i[:, None]).astype(np.float32)
    incl = (i[None, :] >= i[:, None]).astype(np.float32)
    c[:, CMT:CMT + 512] = np.concatenate([strict, incl, -strict, incl], axis=1)
    c[:, CML:CML + 128] = -(i[None, :] < i[:, None]).astype(np.float32)
    c[:, CTRI:CTRI + 128] = incl
    bo = np.zeros((128, 128), np.float32)
    bo[:64, :64] = 1
    bo[64:, 64:] = 1
    c[:, CBO:CBO + 128] = bo
    c[:64, CBC] = 1
    c[64:, CBC + 1] = 1
    inv = (10000.0 ** (-np.arange(0, 32, 2, dtype=np.float32) / 32)).astype(np.float32)
    c[:, CINV] = inv[i % 16]
    rs = np.ones(512, np.float32)
    rs[::128] = 0
    c[:, CRST:CRST + 512] = rs[None, :]
    return c


class _Stop(Exception):
    pass


def build(S, L, taps=(), stop=None):
    NB = S // 512
    NT = S // 128
    nc = bass.Bass("TRN2", target_bir_lowering=False)
    es = ExitStack()
    P = Prog(nc, es, arena_bytes=195 * 1024)
    tapset = set(taps)

    def din(name, shape, dt=F32):
        t = nc.dram_tensor(name, list(shape), dt, kind="ExternalInput")
        return V(t.ap(), Buf(name))

    def dscr(name, shape, dt):
        kind = "ExternalOutput" if name in tapset else "Internal"
        return P.dram(name, shape, dt, kind=kind)

    def finish():
        P.emit()
        es.close()
        return nc

    def chk(tag):
        if stop == tag:
            raise _Stop()

    x_d = din("x", [S, D])
    cT_d = din("cT", [128, 8])
    pos_d = din("pos", [1, S], I32)
    consts_d = din("consts", [128, NCC])
    vecs_d = din("vecs", [128, L, NV])
    rows_d = din("rows", [L, 1024])
    masku_d = din("masku", [128, 7 * 128], BF16)
    fing_d = din("final_g", [1, D])
    wada_d = din("w_ada", [L, D, 3 * D])
    win_d = din("w_in", [L, D, 3360])
    wvd_d = din("w_vd", [max(L - 1, 1), D, 32])
    wdec_d = din("w_dec", [L, 64, 512])
    wicl_d = din("w_icl", [L, 64, 512])
    wvup_d = din("w_vup", [max(L - 1, 1), 32, 512])
    wuq_d = din("w_uq", [L, 384, 768])
    wukv_d = din("w_ukv", [L, 256, 1024])
    wout_d = din("w_out", [L, D, D])
    out_d = P.dram("out", [S, D], F32, kind="ExternalOutput")

    xT_d = dscr("xT", [D, S], F32)
    cs_d = dscr("cs", [2, 128, S], F32)
    vfirst_d = dscr("vfirst", [512, S], F32)
    feat_d = dscr("feat", [NT, 64, NH, 4, 128], BF16)
    tokm_d = dscr("tokm", [S, 4, 512], BF16)
    gc_d = dscr("gc", [512, NT], F32)
    sbon_d = dscr("sbon", [S, NH], F32)
    gate_d = dscr("gate", [D, S], BF16)
    qT_d = dscr("qT", [NH, 96, S], BF16)
    kT_d = dscr("kT", [NH, 64, S], BF16)
    krT_d = dscr("krT", [32, S], BF16)
    v1_d = dscr("v1", [S, NH, 65], BF16)
    ytok_d = dscr("ytok", [S, D], BF16)
    xT_v = xT_d.re("(c p) t -> p c t", p=128)

    consts = P.sb("consts", [128, NCC])
    vecs = P.sb("vecs", [128, L, NV])
    mods = P.sb("mods", [128, L, 24])
    gs = P.sb("gs", [128, L, 8])
    omka = P.sb("omka", [128, L, 4])
    identb = P.sb("identb", [128, 128], BF16)
    onesb = P.sb("onesb", [128, 128], BF16)
    bonesb = P.sb("bonesb", [128, 128], BF16)
    bcolsb = P.sb("bcolsb", [128, 2], BF16)
    epsn = P.sb("epsn", [128, 1])
    epsg = P.sb("epsg", [128, 1])
    halfpi = P.sb("halfpi", [128, 1])
    masku = P.sb("masku", [128, 7, 128], BF16)
    sball = P.sb("sball", [128, NT, NH])
    gcall = P.sb("gcall", [128, 4, NT])
    ident = consts[:, CI:CI + 128]
    maskT = consts[:, CMT:CMT + 512]
    masklow = consts[:, CML:CML + 128]
    tri = consts[:, CTRI:CTRI + 128]
    invf = consts[:, CINV:CINV + 1]
    resetm = consts[:, CRST:CRST + 512]

    P.dma(consts, consts_d)
    P.dma(vecs, vecs_d)
    P.dma(masku, masku_d.re("p (j t) -> p j t", j=7))
    P.cp('dve', identb, ident)
    P.memset('pool', onesb, 1.0)
    P.memset('pool', epsn, NORM_EPS)
    P.memset('pool', epsg, GN_EPS)
    P.memset('pool', halfpi, float(np.pi / 2))
    P.cp('dve', bonesb, consts[:, CBO:CBO + 128])
    P.cp('dve', bcolsb, consts[:, CBC:CBC + 2])

    m0 = P.mark()
    cact = P.al("cact", [128, 8])
    P.dma(cact, cT_d)
    P.act(cact, cact, AF.Silu)
    wst_t = [P.al("wst0", [128, 8, 512]), P.al("wst1", [128, 8, 512])]
    mbank = P.bank()
    n = 0
    for l in range(L):
        for cg in range(6):
            wst = wst_t[n % 2]
            n += 1
            P.dma(wst, wada_d[l, :, cg * 512:(cg + 1) * 512].re("(kc p) n -> p kc n", p=128))
            for j in range(4):
                col = l * 24 + cg * 4 + j
                for kc in range(8):
                    P.mm(mbank[:, col:col + 1], wst[:, kc, j * 128:(j + 1) * 128], cact[:, kc:kc + 1],
                         start=(kc == 0), stop=(kc == 7))
    P.tt('dve', mods, mbank[:, 0:L * 24].re("p (l j) -> p l j", l=L), vecs[:, :, VB:VB + 24], ALU.add)
    P.stt(gs, mods[:, :, 8:16], 1.0, vecs[:, :, VG:VG + 8], ALU.add, ALU.mult)
    P.ts('dve', omka, vecs[:, :, VKA:VKA + 4], -1.0, 1.0, ALU.mult, ALU.add)

    xs = P.al("xs", [128, 4, D])
    xts = P.al("xts", [128, 8, 512])
    posi = P.al("posi", [128, 512], I32)
    ang = P.al("ang", [128, 512])
    rk_ = P.al("rk_", [128, 512])
    rki = P.al("rki", [128, 512], I32)
    rr = P.al("rr", [128, 512])
    mk = P.al("mk", [128, 512])
    for b in range(NB):
        t0 = b * 512
        P.dma(xs, x_d[t0:t0 + 512, :].re("(j p) d -> p j d", p=128))
        for c in range(8):
            bk = P.bank()
            for j in range(4):
                P.tr(bk[:, j * 128:(j + 1) * 128], xs[:, j, c * 128:(c + 1) * 128], ident)
            P.cp('act' if c % 2 else 'dve', xts[:, c, :], bk)
        P.dma(xT_v[:, :, t0:t0 + 512], xts)
        P.dma(posi, V(pos_d.ap[:, t0:t0 + 512].partition_broadcast(128), pos_d.buf))
        P.cp('pool', ang, posi)
        P.ts('dve', ang, ang, invf, None, ALU.mult)
        P.ts('dve', rk_, ang, 1.0 / TWO_PI, None, ALU.mult)
        P.cp('dve', rki, rk_)
        P.cp('dve', rk_, rki)
        P.stt(rr, rk_, -CW1, ang, ALU.mult, ALU.add)
        P.stt(rr, rk_, -CW2, rr, ALU.mult, ALU.add)
        P.ts('dve', mk, rr, float(np.pi), None, ALU.is_gt)
        P.stt(rr, mk, -TWO_PI, rr, ALU.mult, ALU.add)
        P.ts('dve', mk, rr, float(-np.pi), None, ALU.is_lt)
        P.stt(rr, mk, TWO_PI, rr, ALU.mult, ALU.add)
        P.ts('dve', rr, rr, float(np.pi), float(-np.pi), ALU.min, ALU.max)
        P.act(mk, rr, AF.Sin)
        P.dma(cs_d[1, :, t0:t0 + 512], mk)
        P.stt(rk_, rr, -1.0, rr, ALU.mult, ALU.max)
        P.act(ang, rk_, AF.Sin, bias=halfpi, scale=-1.0)
        P.dma(cs_d[0, :, t0:t0 + 512], ang)
    P.release(m0)
    if stop == 'p0':
        return finish()

    def layer(l):
        mL = P.mark()
        W = P.al("W", [128, 8, WCOLS], BF16)
        lora_up = P.al("lora_up", [128, 512], BF16)
        vmix_up = P.al("vmix_up", [128, 512], BF16)
        Wq = P.al("Wq", [128, 3, 1024], BF16)
        Wkv = P.al("Wkv", [128, 2, 1024], BF16)
        mW = P.mark()
        wstage = P.al("wstage", [128, WCOLS])
        P.memset('pool', wstage, 0.0)
        for kc in range(8):
            rows = slice(kc * 128, (kc + 1) * 128)
            P.dma(wstage[:, 0:2816], win_d[l, rows, 0:2816])
            P.dma(wstage[:, 2816:3328], win_d[l, rows, 2848:3360])
            P.dma(wstage[:, 3328:3360], win_d[l, rows, 2816:2848])
            if l > 0:
                P.dma(wstage[:, 3392:3424], wvd_d[l - 1, rows, :])
            P.dma(wstage[:, 3456:3472], win_d[l, rows, 2832:2848])
            P.dma(wstage[:, 3472:3488], win_d[l, rows, 2816:2832])
            P.cp('dve' if kc % 2 else 'pool', W[:, kc, :], wstage)
            P.ts('dve', W[:, kc, 3456:3472], W[:, kc, 3456:3472], -1.0, None, ALU.mult)
        lst = P.al("lst", [128, 512])
        P.dma(lst[0:64], wdec_d[l])
        P.dma(lst[64:128], wicl_d[l])
        P.cp('pool', lora_up, lst)
        if l > 0:
            vst = P.al("vst", [128, 512])
            P.dma(vst[64:96], wvup_d[l - 1])
            P.cp('pool', vmix_up[64:96], vst[64:96])
        qst = P.al("qst", [128, 1024])
        for kc in range(3):
            src = wuq_d[l, kc * 128:(kc + 1) * 128, :].re("p (h e) -> p h e", e=96)
            P.dma(qst[:, 0:512].re("p (h e) -> p h e", e=64), src[:, :, 0:64])
            P.dma(qst[:, 512:768].re("p (h e) -> p h e", e=32), src[:, :, 64:96])
            rot = qst[:, 768:1024].re("p (h e) -> p h e", e=32)
            P.dma(rot[:, :, 0:16], src[:, :, 80:96])
            P.dma(rot[:, :, 16:32], src[:, :, 64:80])
            P.ts('dve', Wq[:, kc, :], qst, vecs[:, l, VQG + kc:VQG + kc + 1], None, ALU.mult)
            wr = Wq[:, kc, 768:1024].re("p (h e) -> p h e", e=32)[:, :, 0:16]
            P.ts('dve', wr, wr, -1.0, None, ALU.mult)
        for kc in range(2):
            src = wukv_d[l, kc * 128:(kc + 1) * 128, :].re("p (h e) -> p h e", e=128)
            P.dma(qst[:, 0:512].re("p (h e) -> p h e", e=64), src[:, :, 0:64])
            P.dma(qst[:, 512:1024].re("p (h e) -> p h e", e=64), src[:, :, 64:128])
            P.ts('dve', Wkv[:, kc, :], qst, vecs[:, l, VKVG + kc:VKVG + kc + 1], None, ALU.mult)

        chk(f'w_{l}')
        P.release(mW)
        carry = {j: P.al(f"carry{j}", [128, 1]) for j in list(range(13)) + [26]}
        for j in carry:
            P.memset('pool', carry[j], 0.0)
        pe2 = [P.al("pe0", [128, 513]), P.al("pe1", [128, 513])]
        npe = [0]
        xt = P.al("xt", [128, 8, 512])
        sq = P.al("sq", [128, 8, 512], BF16)
        hT = P.al("hT", [128, 8, 512], BF16)
        rstd = P.al("rstd", [128, 512])
        u2 = [P.al("u0", [128, 512]), P.al("u1", [128, 512])]
        dtmp = P.al("dtmp", [128, 512])
        lsh = P.al("lsh", [128, 512])
        lora_in = P.al("lora_in", [128, 512], BF16)
        vlo_in = P.al("vlo_in", [128, 512], BF16)
        csb = P.al("csb", [128, 2, 512])
        F_ = {nm: P.al(nm, [128, 512]) for nm in
              ("rs", "ks", "vs", "sg", "aa", "gv", "vf", "kk", "nrm", "bb", "Gs", "g1")}
        B_ = {nm: P.al(nm, [128, 512], BF16) for nm in
              ("kk2", "Rt", "KKt", "Kt", "Bt", "Kh", "Bh", "Vb", "rkb")}
        nb4 = P.al("nb4", [128, 4])
        gC4 = P.al("gC4", [128, 4])
        sb4 = P.al("sb4", [128, 4, 2])
        tmo = P.al("tmo", [128, 4, 4, 128], BF16)
        gst = [P.al("gst0", [128, 512], BF16), P.al("gst1", [128, 512], BF16)]
        cq = P.al("cq", [128, 3, 512], BF16)
        sqq = P.al("sqq", [128, 3, 512], BF16)
        rq, crs, srs, a1, a2, rkv = (P.al(nm, [128, 512]) for nm in ("rq", "crs", "srs", "a1", "a2", "rkv"))
        qn = [P.al("qn0", [128, 512], BF16), P.al("qn1", [128, 512], BF16)]
        qrp = [P.al("qrp0", [128, 512], BF16), P.al("qrp1", [128, 512], BF16)]
        ckv = P.al("ckv", [128, 2, 512], BF16)
        sqk = P.al("sqk", [128, 2, 512], BF16)
        rcol = P.al("rcol", [128, 1])
        v1s = P.al("v1s", [128, 4, NH, 65], BF16)
        krb = P.al("krb", [128, 512], BF16)
        P.memset('pool', v1s[:, :, :, 64:65], 1.0)

        for b in range(NB):
            t0 = b * 512
            tsl = slice(t0, t0 + 512)
            P.dma(xt, xT_v[:, :, tsl])
            P.dma(csb, cs_d[:, :, tsl].re("w p t -> p w t"))
            for c in range(8):
                if c % 2:
                    P.act(sq[:, c, :], xt[:, c, :], AF.Square)
                else:
                    P.tt('pool', sq[:, c, :], xt[:, c, :], xt[:, c, :], ALU.mult)
            bk = P.bank()
            for c in range(8):
                P.mm(bk, onesb, sq[:, c, :], start=(c == 0), stop=(c == 7))
            P.act(rstd, bk, AF.Sqrt, bias=epsn, scale=1.0 / D)
            P.recip(rstd, rstd)
            for c in range(8):
                u = u2[c % 2]
                P.tt('pool' if c % 2 else 'dve', u, xt[:, c, :], rstd, ALU.mult)
                P.act(hT[:, c, :], u, AF.Identity, bias=mods[:, l, c:c + 1], scale=gs[:, l, c:c + 1])

            def proj(j, M=128):
                bk = P.bank()
                for kc in range(8):
                    P.mm(bk[0:M, :], W[:, kc, j * 128:j * 128 + M], hT[:, kc, :], start=(kc == 0), stop=(kc == 7))
                return bk

            def shifted(j, out, mucol, rows=slice(0, 128)):
                bk = proj(j)
                pe = pe2[npe[0] % 2]
                npe[0] += 1
                P.cp('pool', pe[rows, 0:1], carry[j][rows, :])
                P.cp('act', pe[:, 1:513], bk)
                P.tt('pool', dtmp[rows], pe[rows, 0:512], pe[rows, 1:513], ALU.subtract)
                P.stt(out[rows], dtmp[rows], vecs[rows, l, mucol:mucol + 1], pe[rows, 1:513], ALU.mult, ALU.add)
                P.cp('pool', carry[j][rows, :], pe[rows, 512:513])
                return pe

            chk(f'h_{l}')
            shifted(12, lsh, VMU + 12)
            P.act(lora_in[0:64], lsh[0:64], AF.Tanh)
            P.cp('pool', lora_in[64:128], lsh[64:128])
            if l > 0:
                pe26 = shifted(26, lsh, VMUV, rows=slice(64, 96))
                P.cp('pool', vlo_in[64:96], lsh[64:96])
            else:
                bk26 = proj(26)
                pe26 = pe2[npe[0] % 2]
                npe[0] += 1
                P.cp('act', pe26[:, 1:513], bk26)
            bk27 = proj(27, M=32)
            P.tt('dve', a1[0:32], pe26[0:32, 1:513], csb[0:32, 0, :], ALU.mult)
            P.tt('dve', a2[0:32], bk27[0:32, :], csb[0:32, 1, :], ALU.mult)
            P.tt('pool', krb[0:32], a1[0:32], a2[0:32], ALU.add)
            P.dma(krT_d[:, tsl], krb[0:32])

            def rw_gen():
                for hp in range(4):
                    rs, ks, vs, sg, aa, gv, vf = (F_[k] for k in ("rs", "ks", "vs", "sg", "aa", "gv", "vf"))
                    kk, nrm, bb, Gs, g1 = (F_[k] for k in ("kk", "nrm", "bb", "Gs", "g1"))
                    kkn = kk
                    ff = nrm
                    kmod = nrm
                    gi = e1 = ge = ginv = gcr = g1
                    rk = rs
                    shifted(hp, rs, VMU + hp)
                    shifted(4 + hp, ks, VMU + 4 + hp)
                    shifted(8 + hp, vs, VMU + 8 + hp)
                    yield
                    cols = slice(hp * 128, (hp + 1) * 128)
                    prow = slice(hp * 128, (hp + 1) * 128)
                    bkw = P.bank()
                    P.mm(bkw, lora_up[0:64, cols], lora_in[0:64, :])
                    P.act(sg, bkw, AF.Sigmoid, bias=vecs[:, l, VW0 + hp:VW0 + hp + 1])
                    bka = P.bank()
                    P.mm(bka, lora_up[64:128, cols], lora_in[64:128, :])
                    P.act(aa, bka, AF.Sigmoid, bias=vecs[:, l, VA0 + hp:VA0 + hp + 1])
                    yield
                    if l > 0:
                        bkv = P.bank()
                        P.mm(bkv, vmix_up[64:96, cols], vlo_in[64:96, :])
                        P.act(gv, bkv, AF.Sigmoid, bias=vecs[:, l, VV0 + hp:VV0 + hp + 1])
                        P.dma(vf, vfirst_d[prow, tsl])
                        P.tt('pool', vf, vf, vs, ALU.subtract)
                        P.tt('pool', vf, vf, gv, ALU.mult)
                        P.tt('pool', vs, vs, vf, ALU.add)
                    else:
                        P.dma(vfirst_d[prow, tsl], vs)
                    P.act(kk, ks, AF.Identity, scale=vecs[:, l, VKK + hp:VKK + hp + 1])
                    P.tt('dve', B_["kk2"], kk, kk, ALU.mult)
                    bks = P.bank()
                    P.mm(bks, bonesb, B_["kk2"])
                    P.act(nrm, bks, AF.Sqrt)
                    P.ts('dve', nrm, nrm, 1e-12, None, ALU.max)
                    P.recip(nrm, nrm)
                    yield
                    P.tt('dve', kkn, kk, nrm, ALU.mult)
                    P.ts('dve', ff, aa, vecs[:, l, VKA + hp:VKA + hp + 1], omka[:, l, hp:hp + 1], ALU.mult, ALU.add)
                    P.tt('dve', kmod, ff, ks, ALU.mult)
                    P.tt('dve', bb, kkn, aa, ALU.mult)
                    P.scan(Gs, resetm, sg, 0.0, ALU.mult, ALU.add)
                    yield
                    P.act(gi, Gs, AF.Exp, scale=-C_DEC)
                    P.tt('dve', B_["Rt"], rs, gi, ALU.mult)
                    P.tt('pool', e1, Gs, sg, ALU.subtract)
                    P.act(ge, e1, AF.Exp, scale=-C_DEC)
                    P.tt('dve', B_["KKt"], kkn, ge, ALU.mult)
                    yield
                    P.act(ginv, Gs, AF.Exp, scale=C_DEC)
                    P.tt('pool', B_["Kt"], kmod, ginv, ALU.mult)
                    P.tt('dve', B_["Bt"], bb, ginv, ALU.mult)
                    yield
                    GsC = Gs.re("p (q t) -> p q t", t=128)[:, :, 127]
                    P.ts('dve', nb4, GsC, -C_DEC, None, ALU.mult)
                    for q in range(4):
                        P.act(gcr[:, q * 128:(q + 1) * 128], Gs[:, q * 128:(q + 1) * 128], AF.Exp, scale=C_DEC,
                              bias=nb4[:, q:q + 1])
                    P.tt('dve', B_["Kh"], kmod, gcr, ALU.mult)
                    P.tt('pool', B_["Bh"], bb, gcr, ALU.mult)
                    yield
                    P.act(gcall[:, hp, b * 4:(b + 1) * 4], nb4, AF.Exp)
                    P.tt('pool', rk, rs, kmod, ALU.mult)
                    P.act(B_["rkb"], rk, AF.Identity, scale=vecs[:, l, VRK + hp:VRK + hp + 1])
                    bkb = P.bank()
                    for q in range(4):
                        P.mm(bkb[:, q * 2:(q + 1) * 2], B_["rkb"][:, q * 128:(q + 1) * 128], bcolsb)
                    P.cp('dve', sball[:, b * 4:(b + 1) * 4, 2 * hp:2 * hp + 2], bkb[:, 0:8].re("p (q h) -> p q h", h=2))
                    P.cp('pool', B_["Vb"], vs)
                    yield
                    for opi, nm in enumerate(("Kt", "Bt", "KKt", "Rt")):
                        for hh in range(2):
                            P.dma(feat_d[b * 4:(b + 1) * 4, :, 2 * hp + hh, opi, :].re("c k t -> k c t"),
                                  B_[nm][hh * 64:(hh + 1) * 64, :].re("k (c t) -> k c t", c=4))
                    for q in range(4):
                        bkt = P.bank().cast(BF16)
                        for opi, nm in enumerate(("KKt", "Vb", "Kh", "Bh")):
                            P.tr(bkt[:, opi * 128:(opi + 1) * 128], B_[nm][:, q * 128:(q + 1) * 128], identb)
                        P.cp('act' if q % 2 else 'dve', tmo[:, q, :, :], bkt[:, 0:512].re("p (o f) -> p o f", o=4))
                        yield
                    for q in range(4):
                        P.dma(tokm_d[t0 + q * 128:t0 + (q + 1) * 128, :, cols], tmo[:, q, :, :])

            def mla_gen():
                for gi_, j in enumerate([13, 14, 15, 16, 22, 23, 24, 25]):
                    bk = proj(j)
                    g = gst[gi_ % 2]
                    P.act(g, bk, AF.Silu)
                    P.dma(gate_d[gi_ * 128:(gi_ + 1) * 128, tsl], g)
                    yield
                for i in range(3):
                    bk = proj(17 + i)
                    P.cp('act', cq[:, i, :], bk)
                    P.act(sqq[:, i, :], bk, AF.Square)
                    yield
                bk = P.bank()
                for i in range(3):
                    P.mm(bk, onesb, sqq[:, i, :], start=(i == 0), stop=(i == 2))
                P.act(rq, bk, AF.Sqrt, bias=epsn, scale=1.0 / 384)
                P.recip(rq, rq)
                yield
                P.tt('pool', crs, csb[:, 0, :], rq, ALU.mult)
                P.tt('pool', srs, csb[:, 1, :], rq, ALU.mult)
                for i in range(4):
                    bk = P.bank()
                    for kc in range(3):
                        P.mm(bk, Wq[:, kc, i * 128:(i + 1) * 128], cq[:, kc, :], start=(kc == 0), stop=(kc == 2))
                    qq = qn[i % 2]
                    P.tt('dve', qq, bk, rq, ALU.mult)
                    for hh in range(2):
                        P.dma(qT_d[2 * i + hh, 0:64, tsl], qq[hh * 64:(hh + 1) * 64, :])
                    yield
                for i in range(2):
                    bk1 = P.bank()
                    for kc in range(3):
                        P.mm(bk1, Wq[:, kc, 512 + i * 128:512 + (i + 1) * 128], cq[:, kc, :], start=(kc == 0),
                             stop=(kc == 2))
                    bk2 = P.bank()
                    for kc in range(3):
                        P.mm(bk2, Wq[:, kc, 768 + i * 128:768 + (i + 1) * 128], cq[:, kc, :], start=(kc == 0),
                             stop=(kc == 2))
                    P.tt('dve', a1, bk1, crs, ALU.mult)
                    P.tt('dve', a2, bk2, srs, ALU.mult)
                    qq = qrp[i % 2]
                    P.tt('dve', qq, a1, a2, ALU.add)
                    for hh in range(4):
                        P.dma(qT_d[4 * i + hh, 64:96, tsl], qq[hh * 32:(hh + 1) * 32, :])
                    yield
                for i in range(2):
                    bk = proj(20 + i)
                    P.cp('act', ckv[:, i, :], bk)
                    P.act(sqk[:, i, :], bk, AF.Square)
                    yield
                bk = P.bank()
                for i in range(2):
                    P.mm(bk, onesb, sqk[:, i, :], start=(i == 0), stop=(i == 1))
                P.act(rkv, bk, AF.Sqrt, bias=epsn, scale=1.0 / 256)
                P.recip(rkv, rkv)
                for i in range(4):
                    bk = P.bank()
                    for kc in range(2):
                        P.mm(bk, Wkv[:, kc, i * 128:(i + 1) * 128], ckv[:, kc, :], start=(kc == 0), stop=(kc == 1))
                    qq = qn[i % 2]
                    P.tt('dve', qq, bk, rkv, ALU.mult)
                    for hh in range(2):
                        P.dma(kT_d[2 * i + hh, :, tsl], qq[hh * 64:(hh + 1) * 64, :])
                    yield
                for q in range(4):
                    qs = slice(q * 128, (q + 1) * 128)
                    bkc = P.bank()
                    for i in range(2):
                        P.mm(bkc[:, 0:1], sqk[:, i, qs], onesb[:, 0:1], start=(i == 0), stop=(i == 1))
                    P.act(rcol, bkc[:, 0:1], AF.Sqrt, bias=epsn, scale=1.0 / 256)
                    P.recip(rcol, rcol)
                    bkv = P.bank()
                    for i in range(2):
                        P.mm(bkv, ckv[:, i, qs], Wkv[:, i, 512:1024], start=(i == 0), stop=(i == 1))
                    P.ts('dve', v1s[:, q, :, 0:64], bkv.re("p (h e) -> p h e", e=64), rcol, None, ALU.mult)
                    yield
                for q in range(4):
                    P.dma(v1_d[t0 + q * 128:t0 + (q + 1) * 128], v1s[:, q, :, :])
            gens_ = [rw_gen(), mla_gen()]
            while gens_:
                nx_ = []
                for gg in gens_:
                    try:
                        next(gg)
                        nx_.append(gg)
                    except StopIteration:
                        pass
                gens_ = nx_
        for hp in range(4):
            P.dma(gc_d[hp * 128:(hp + 1) * 128, :], gcall[:, hp, :])
        if 'sbon' in tapset:
            P.dma(sbon_d.re("(c p) h -> p c h", p=128), sball)
        P.release(mL)
        if stop == f'p1_{l}':
            return True

        gcs = P.al("gcs", [64, NH, NT])
        P.dma(gcs, gc_d.re("(h k) c -> k h c", k=64))
        sbn = sball
        lnw = P.al("lnw", [128, 1024])
        P.dma(lnw, V(rows_d.ap[l:l + 1, :].partition_broadcast(128), rows_d.buf))
        lnw3 = lnw[:, 0:512].re("p (h e) -> p h e", e=64)
        lnb3 = lnw[:, 512:1024].re("p (h e) -> p h e", e=64)
        Sf = P.al("Sf", [64, NH, 64])
        Sb = P.al("Sb", [64, NH, 64], BF16)
        P.memset('pool', Sf, 0.0)
        P.memset('pool', Sb, 0.0)
        GL = 3

        def slot_tiles(i):
            d = {}
            d['F'] = P.al(f"F{i}", [64, NH, 4, 128], BF16)
            d['T'] = P.al(f"T{i}", [128, 4, 512], BF16)
            d['XK'] = P.al(f"XK{i}", [128, NH, 128], BF16)
            d['AT'] = P.al(f"AT{i}", [128, NH, 512], BF16)
            for nm in ('Nk', 'Dk', 'Wk', 'Y1', 'Dl'):
                d[nm] = [P.al(f"{nm}{i}_{g}", [128, 4, 128], BF16) for g in range(2)]
            d['MU'] = [P.al(f"MU{i}_{g}", [128, 4, 6, 128], BF16) for g in range(2)]
            d['nPQ'] = P.al(f"nPQ{i}", [128, NH, 128], BF16)
            d['nZ'] = P.al(f"nZ{i}", [64, NH, 64], BF16)
            d['Psi'] = P.al(f"Psi{i}", [64, NH, 64])
            d['Ry'] = P.al(f"Ry{i}", [64, NH, 128], BF16)
            d['t1'] = P.al(f"t1{i}", [64, NH, 64])
            d['y'] = P.al(f"y{i}", [128, NH, 64])
            d['yc'] = P.al(f"yc{i}", [128, NH, 64])
            d['ysq'] = P.al(f"ysq{i}", [128, NH, 64])
            d['mean'] = P.al(f"mean{i}", [128, NH])
            d['var'] = P.al(f"var{i}", [128, NH])
            d['ybf'] = P.al(f"ybf{i}", [128, 512], BF16)
            return d
        slots = [slot_tiles(i) for i in range(GL)]

        def chunk_gen(c):
            d = slots[c % GL]
            F, T, XK, AT, Nk, Dk, Wk, Y1, Dl, MU = (d[k] for k in ('F', 'T', 'XK', 'AT', 'Nk', 'Dk', 'Wk', 'Y1', 'Dl', 'MU'))
            nPQ, nZ, Psi, Ry, t1, y, yc, ysq, mean, var, ybf = (d[k] for k in
                                                               ('nPQ', 'nZ', 'Psi', 'Ry', 't1', 'y', 'yc', 'ysq', 'mean', 'var', 'ybf'))
            crow = slice(c * 128, (c + 1) * 128)
            P.dma(F, feat_d[c])
            P.dma(T, tokm_d[crow])
            P.dma(XK[:, :, 0:64], tokm_d[crow, 0, :].re("p (h e) -> p h e", e=64))
            yield
            for h in range(NH):
                bk = P.bank()
                KR = F[:, h, 2:4, :].re("k o t -> k (o t)")
                P.mm(bk[:, 0:256], F[:, h, 0, :], KR)
                P.mm(bk[:, 256:512], F[:, h, 1, :], KR)
                P.tt('dve', AT[:, h, :], bk, maskT, ALU.mult)
                if h % 4 == 3:
                    yield
            for g in range(2):
                bkn = P.bank()
                for hh in range(4):
                    h = 4 * g + hh
                    P.mm(bkn[:, hh * 128:(hh + 1) * 128], F[:, h, 2, :], F[:, h, 1, :])
                P.tt('dve', Nk[g], bkn.re("p (h s) -> p h s", h=4), masklow.un(1).bc([128, 4, 128]), ALU.mult)
                m0b = masku[:, 0, :].un(1).bc([128, 4, 128])
                idb = identb.un(1).bc([128, 4, 128])
                P.tt('pool', Dk[g], Nk[g], m0b, ALU.mult)
                P.tt('pool', Dk[g], Dk[g], idb, ALU.add)
                P.tt('pool', Wk[g], AT[:, 4 * g:4 * g + 4, 256:384], m0b, ALU.mult)
                P.tt('pool', Wk[g], Wk[g], idb, ALU.add)
                for hh in range(4):
                    P.tt('pool', MU[g][:, hh, :, :], AT[:, 4 * g + hh, 256:384].un(1).bc([128, 6, 128]),
                         masku[:, 1:7, :], ALU.mult)
                yield
            for j in range(1, 7):
                bkYs = []
                for g in range(2):
                    bkY = P.bank()
                    for hh in range(4):
                        P.mm(bkY[:, hh * 128:(hh + 1) * 128], MU[g][:, hh, j - 1, :], Dk[g][:, hh, :])
                    P.cp('act', Y1[g], bkY.re("p (h s) -> p h s", h=4))
                yield
                bkDs = []
                for g in range(2):
                    bkD = P.bank()
                    bkDs.append(bkD)
                    for hh in range(4):
                        P.mm(bkD[:, hh * 128:(hh + 1) * 128], Wk[g][:, hh, :], Y1[g][:, hh, :])
                    P.cp('act', Dl[g], bkD.re("p (h s) -> p h s", h=4))
                    if j < 6:
                        P.tt('dve', Dk[g], bkD.re("p (h s) -> p h s", h=4), Dk[g], ALU.add)
                yield
                for g in range(2):
                    bkT = P.bank().cast(BF16)
                    for hh in range(4):
                        P.tr(bkT[:, hh * 128:(hh + 1) * 128], Dl[g][:, hh, :], identb)
                    P.tt('dve', Wk[g], bkT[:, 0:512].re("p (h s) -> p h s", h=4), Wk[g], ALU.add)
                yield
            bkx = P.bank()
            for h in range(NH):
                hc = slice(h * 64, (h + 1) * 64)
                P.mm(bkx[:, hc], AT[:, h, 0:128], T[:, 1, hc])
            P.cp('act', XK[:, :, 64:128], bkx.re("p (h e) -> p h e", e=64))
            yield
            for g in range(2):
                bkp = P.bank()
                for hh in range(4):
                    h = 4 * g + hh
                    P.mm(bkp[:, hh * 128:(hh + 1) * 128], Wk[g][:, hh, :], XK[:, h, :])
                P.act(nPQ[:, 4 * g:4 * g + 4, :], bkp.re("p (h s) -> p h s", h=4), AF.Copy, scale=-1.0)
            yield
            bkz = P.bank()
            for h in range(NH):
                hc = slice(h * 64, (h + 1) * 64)
                P.mm(bkz[0:64, hc], nPQ[:, h, 0:64], T[:, 3, hc])
            P.cp('act', nZ, bkz[0:64, :].re("k (h e) -> k h e", e=64))
            bkpsi = P.bank()
            for h in range(NH):
                hc = slice(h * 64, (h + 1) * 64)
                P.mm(bkpsi[0:64, hc], T[:, 2, hc], T[:, 1, hc], start=True, stop=False)
                P.mm(bkpsi[0:64, hc], T[:, 3, hc], nPQ[:, h, 64:128], start=False, stop=True)
            P.cp('dve', Psi, bkpsi[0:64, :].re("k (h e) -> k h e", e=64))
            for g in range(2):
                bkr = P.bank()
                for hh in range(4):
                    h = 4 * g + hh
                    P.mm(bkr[0:64, hh * 128:(hh + 1) * 128], nPQ[:, h, 0:64], AT[:, h, 384:512])
                P.tt('dve', Ry[:, 4 * g:4 * g + 4, :], bkr[0:64, :].re("k (h t) -> k h t", h=4),
                     F[:, 4 * g:4 * g + 4, 3, :], ALU.add)
            yield
            bky = P.bank()
            for h in range(NH):
                hc = slice(h * 64, (h + 1) * 64)
                P.mm(bky[:, hc], AT[:, h, 128:256], T[:, 1, hc], start=True, stop=False)
                P.mm(bky[:, hc], AT[:, h, 384:512], nPQ[:, h, 64:128], start=False, stop=False)
                P.mm(bky[:, hc], Ry[:, h, :], Sb[:, h, :], start=False, stop=True)
            bku = P.bank()
            for h in range(NH):
                hc = slice(h * 64, (h + 1) * 64)
                P.mm(bku[0:64, hc], nZ[:, h, :], Sb[:, h, :])
            P.tt('pool', t1, Sf, gcs[:, :, c].un(2).bc([64, NH, 64]), ALU.mult)
            P.tt('pool', t1, t1, Psi, ALU.add)
            bku3 = bku[0:64, :].re("k (h e) -> k h e", e=64)
            P.tt('dve', Sb, bku3, t1, ALU.add)
            P.tt('dve', Sf, bku3, t1, ALU.add)
            yield
            P.cp('act', y, bky.re("p (h e) -> p h e", e=64))
            P.reduce(mean, y)
            P.ts('dve', mean, mean, 1.0 / 64, None, ALU.mult)
            P.tt('pool', yc, y, mean.un(2).bc([128, NH, 64]), ALU.subtract)
            P.tt('pool', ysq, yc, yc, ALU.mult)
            P.reduce(var, ysq)
            P.act(var, var, AF.Sqrt, bias=epsg, scale=1.0 / 64)
            P.recip(var, var)
            yield
            P.tt('pool', yc, yc, var.un(2).bc([128, NH, 64]), ALU.mult)
            P.tt('pool', yc, yc, lnw3, ALU.mult)
            P.tt('pool', yc, yc, lnb3, ALU.add)
            P.tt('pool', ysq, T[:, 1, :].re("p (h e) -> p h e", e=64), sbn[:, c, :].un(2).bc([128, NH, 64]), ALU.mult)
            P.tt('pool', ybf.re("p (h e) -> p h e", e=64), yc, ysq, ALU.add)
            P.dma(ytok_d[crow, 0:512], ybf)

        def run_lockstep(gens):
            live = list(gens)
            while live:
                nxt = []
                for gg in live:
                    try:
                        next(gg)
                        nxt.append(gg)
                    except StopIteration:
                        pass
                live = nxt
        for c0 in range(0, NT, GL):
            run_lockstep([chunk_gen(c) for c in range(c0, min(NT, c0 + GL))])
        P.release(mL)
        if stop == f'p2_{l}':
            return True

        v1a = P.al("v1a", [128, NT, NH * 65], BF16)
        P.dma(v1a, v1_d.re("(j p) h e -> p j (h e)", p=128))
        kTq = [P.al(f"kTh{i}", [96, S], BF16) for i in range(2)]
        qTq = [P.al(f"qTh{i}", [96, S], BF16) for i in range(2)]
        Et = [P.al(f"E{i}", [128, 512], BF16) for i in range(3)]
        trib = P.al("trib", [128, 128], BF16)
        P.cp('pool', trib, tri)
        rden = P.al("rden", [128, 1])
        yh = [P.al(f"yh{i}", [128, 64], BF16) for i in range(2)]
        Ob = P.banks[0:4]
        Sbk = P.banks[4:8]
        nsb = 0
        for h in range(NH):
            kTh = kTq[h % 2]
            qTh = qTq[h % 2]
            P.dma(kTh[0:64], kT_d[h])
            P.dma(kTh[64:96], krT_d)
            P.dma(qTh, qT_d[h])
            for Q in range(NB):
                nkb = 4 * Q + 4
                qend = (Q + 1) * 512

                def qk(j):
                    nonlocal nsb
                    qlo = max(Q * 512, j * 128)
                    N = qend - qlo
                    bs = Sbk[nsb % 4]
                    E = Et[nsb % 3]
                    nsb += 1
                    P.mm(bs[:, 0:N], kTh[:, j * 128:(j + 1) * 128], qTh[:, qlo:qend])
                    P.act(E[:, 0:N], bs[:, 0:N], AF.Exp, scale=ATT_SCALE)
                    if j >= 4 * Q:
                        P.tt('pool', E[:, 0:128], E[:, 0:128], trib, ALU.mult)
                    return E, qlo

                def pv(j, E, qlo):
                    for t in range(max(4 * Q, j), 4 * Q + 4):
                        off = t * 128 - qlo
                        P.mm(Ob[t - 4 * Q][:, 0:65], E[:, off:off + 128], v1a[:, j, h * 65:(h + 1) * 65],
                             start=(j == 0), stop=(j == t))
                pend = qk(0)
                for j in range(nkb):
                    nxt = qk(j + 1) if j + 1 < nkb else None
                    pv(j, *pend)
                    pend = nxt
                for tq in range(4):
                    t = 4 * Q + tq
                    yy = yh[tq % 2]
                    P.recip(rden, Ob[tq][:, 64:65])
                    P.ts('dve', yy, Ob[tq][:, 0:64], rden, None, ALU.mult)
                    P.dma(ytok_d[t * 128:(t + 1) * 128, 512 + h * 64:512 + (h + 1) * 64], yy)
        P.release(mL)
        if stop == f'p3_{l}':
            return True

        Wo = P.al("Wo", [128, 8, D], BF16)
        ost = P.al("ost", [128, D])
        for kc in range(8):
            P.dma(ost, wout_d[l, kc * 128:(kc + 1) * 128, :])
            P.cp('pool' if kc % 2 else 'dve', Wo[:, kc, :], ost)
        yt = P.al("yt", [128, 4, D], BF16)
        gt = P.al("gt", [128, 8, 512], BF16)
        xt4 = P.al("xt4", [128, 8, 512])
        yT = P.al("yT", [128, 8, 512], BF16)
        for b in range(NB):
            tsl = slice(b * 512, (b + 1) * 512)
            P.dma(yt, ytok_d[tsl].re("(q p) f -> p q f", p=128))
            P.dma(gt, gate_d.re("(c p) t -> p c t", p=128)[:, :, tsl])
            P.dma(xt4, xT_v[:, :, tsl])
            for f in range(8):
                bkt = P.bank().cast(BF16)
                for q in range(4):
                    P.tr(bkt[:, q * 128:(q + 1) * 128], yt[:, q, f * 128:(f + 1) * 128], identb)
                P.tt('dve', yT[:, f, :], bkt[:, 0:512], gt[:, f, :], ALU.mult)
            for dch in range(8):
                bk = P.bank()
                for f in range(8):
                    P.mm(bk, Wo[:, f, dch * 128:(dch + 1) * 128], yT[:, f, :], start=(f == 0), stop=(f == 7))
                P.stt(xt4[:, dch, :], bk, mods[:, l, 16 + dch:17 + dch], xt4[:, dch, :], ALU.mult, ALU.add)
            P.dma(xT_v[:, :, tsl], xt4)
        P.release(mL)
        return False

    try:
        for l in range(L):
            if layer(l):
                return finish()
    except _Stop:
        return finish()

    m0 = P.mark()
    fg = P.al("fg", [128, D])
    P.dma(fg, V(fing_d.ap.partition_broadcast(128), fing_d.buf))
    xtf = P.al("xtf", [128, 8, 512])
    xo = [P.al("xo0", [128, D]), P.al("xo1", [128, D])]
    junk = P.al("junkf", [128, D])
    ssq = [P.al("ssq0", [128, 1]), P.al("ssq1", [128, 1])]
    for b in range(NB):
        t0 = b * 512
        P.dma(xtf, xT_v[:, :, t0:t0 + 512])
        for j in range(4):
            bk0 = P.bank()
            bk1 = P.bank()
            for c in range(8):
                bk = bk0 if c < 4 else bk1
                P.tr(bk[:, (c % 4) * 128:(c % 4 + 1) * 128], xtf[:, c, j * 128:(j + 1) * 128], ident)
            o = xo[j % 2]
            s = ssq[j % 2]
            P.cp('dve', o[:, 0:512], bk0)
            P.cp('act', o[:, 512:1024], bk1)
            P.act(junk, o, AF.Square, accum=s)
            P.act(s, s, AF.Sqrt, bias=epsn, scale=1.0 / D)
            P.recip(s, s)
            P.stt(o, o, s, fg, ALU.mult, ALU.mult)
            P.dma(out_d[t0 + j * 128:t0 + (j + 1) * 128, :], o, final=True)
    P.release(m0)
    return finish()


def make_masku():
    i = np.arange(128)
    m = np.zeros((128, 7, 128), np.float32)
    for j in range(7):
        bs = 1 << j
        m[:, j, :] = ((i[:, None] // (2 * bs)) == (i[None, :] // (2 * bs))) & ((i[:, None] // bs) != (i[None, :] // bs))
    return m.reshape(128, 7 * 128).astype(ml_dtypes.bfloat16)


def host_layout(inputs, S, L):
    f = lambda a: np.ascontiguousarray(np.asarray(a, dtype=np.float32))

    def cols(v):
        v = np.asarray(v, np.float32)
        return v.reshape(-1, 128).T

    vecs = np.zeros((128, L, NV), np.float32)
    rows = np.zeros((L, 1024), np.float32)
    for l in range(L):
        vecs[:, l, VG:VG + 8] = cols(inputs['norm_g'][l])
        vecs[:, l, VB:VB + 24] = cols(inputs['b_ada'][l])
        vecs[:, l, VMU:VMU + 13] = cols(inputs['mu_shift'][l])
        if l > 0:
            vecs[64:96, l, VMUV] = np.asarray(inputs['mu_vmix'][l - 1], np.float32)
            vecs[:, l, VV0:VV0 + 4] = cols(inputs['v0'][l - 1])
        vecs[:, l, VW0:VW0 + 4] = cols(inputs['w0'][l])
        vecs[:, l, VA0:VA0 + 4] = cols(inputs['a0'][l])
        vecs[:, l, VKK:VKK + 4] = cols(inputs['k_k'][l])
        vecs[:, l, VKA:VKA + 4] = cols(inputs['k_a'][l])
        vecs[:, l, VRK:VRK + 4] = cols(np.asarray(inputs['r_k'][l]).reshape(-1))
        vecs[:, l, VQG:VQG + 3] = cols(inputs['q_norm_g'][l])
        vecs[:, l, VKVG:VKVG + 2] = cols(inputs['kv_norm_g'][l])
        rows[l, 0:512] = np.asarray(inputs['lnx_w'][l], np.float32)
        rows[l, 512:1024] = np.asarray(inputs['lnx_b'][l], np.float32)
    shared = {
        "consts": make_consts(), "vecs": vecs, "rows": rows, "masku": make_masku(),
        "final_g": f(inputs['final_g']).reshape(1, D),
        "w_ada": f(inputs['w_ada'])[:L], "w_in": f(inputs['w_in'])[:L],
        "w_vd": f(inputs['w_vmix_down'])[:max(L - 1, 1)],
        "w_dec": f(inputs['w_decay_up'])[:L], "w_icl": f(inputs['w_iclr_up'])[:L],
        "w_vup": f(inputs['w_vmix_up'])[:max(L - 1, 1)],
        "w_uq": f(inputs['w_uq'])[:L], "w_ukv": f(inputs['w_ukv'])[:L], "w_out": f(inputs['w_out'])[:L],
    }
    x = np.asarray(inputs['x'], np.float32)
    c = np.asarray(inputs['c'], np.float32)
    pos = np.asarray(inputs['positions']).astype(np.int32)
    B = x.shape[0]
    per = []
    for b in range(B):
        m = dict(shared)
        m["x"] = np.ascontiguousarray(x[b, :S])
        m["cT"] = np.ascontiguousarray(c[b].reshape(8, 128).T)
        m["pos"] = np.ascontiguousarray(pos[b, :S].reshape(1, S))
        per.append(m)
    return per


_NC_CACHE = {}


def kernel(**inputs):
    S, L = 4096, 4
    B = np.asarray(inputs['x']).shape[0]
    per = host_layout(inputs, S, L)
    if (S, L) not in _NC_CACHE:
        _NC_CACHE[(S, L)] = build(S, L)
    nc = _NC_CACHE[(S, L)]
    res = run_bass_kernel_spmd(nc, per, core_ids=list(range(B)))
    return np.stack([np.asarray(r["out"], np.float32) for r in res.results], axis=0)
```

```python
import numpy as np
import ml_dtypes
import concourse.bass as bass
import concourse.mybir as mybir
from concourse.bass_utils import run_bass_kernel_spmd
from contextlib import ExitStack

F32 = mybir.dt.float32
BF16 = mybir.dt.bfloat16
I32 = mybir.dt.int32
ALU = mybir.AluOpType
AF = mybir.ActivationFunctionType
AX = mybir.AxisListType

ENGS = ['pe', 'act', 'dve', 'pool', 'sp']
EPOCH = 20000
NDSEM = 24
DT_BYTES = {F32: 4, BF16: 2, I32: 4}


class Buf:
    __slots__ = ('name', 'last_w', 'readers', 'dma_ws', 'excl')

    def __init__(self, name, excl=False):
        self.name = name
        self.excl = excl
        self.last_w = None
        self.dma_ws = []
        self.readers = {}


class V:
    __slots__ = ('ap', 'buf')

    def __init__(self, ap, buf):
        self.ap = ap
        self.buf = buf

    def __getitem__(self, k):
        return V(self.ap[k], self.buf)

    def re(self, s, **kw):
        return V(self.ap.rearrange(s, **kw), self.buf)

    def bc(self, shape):
        return V(self.ap.to_broadcast(list(shape)), self.buf)

    def un(self, ax):
        return V(self.ap.unsqueeze(ax), self.buf)

    def cast(self, dt):
        return V(self.ap.bitcast(dt), self.buf)


class Op:
    __slots__ = ('eng', 'fn', 'deps', 'sig', 'is_dma', 'has_dep', 'gidx')

    def __init__(self, eng, fn, is_dma):
        self.eng = eng
        self.fn = fn
        self.is_dma = is_dma
        self.deps = []
        self.sig = None
        self.has_dep = False


class Prog:
    def __init__(self, nc, es, arena_bytes):
        self.nc = nc
        self.es = es
        self.ops = {e: [] for e in ENGS}
        self.n = 0
        self.final_dmas = []
        h = es.enter_context(nc.sbuf_tensor("arena", [128, arena_bytes // 2], BF16))
        self.arena = h[:]
        self.arena_bytes = arena_bytes
        self.live = []
        self.sp_ = 0
        self.banks = []
        for i in range(8):
            hb = es.enter_context(nc.psum_tensor(f"bank{i}", [128, 512], F32))
            self.banks.append(V(hb[:], Buf(f"bank{i}", excl=True)))
        self.bi = 0

    def sb(self, name, shape, dt=F32):
        h = self.es.enter_context(self.nc.sbuf_tensor("s_" + name, list(shape), dt))
        return V(h[:], Buf(name))

    def mark(self):
        return self.sp_

    def release(self, m):
        self.sp_ = m

    def al(self, name, shape, dt=F32):
        per = int(np.prod(shape[1:])) * DT_BYTES[dt]
        start = (self.sp_ + 63) // 64 * 64
        end = start + per
        assert end <= self.arena_bytes, (name, end, self.arena_bytes)
        self.sp_ = end
        b = Buf(name)
        keep = []
        for (s0, e0, ob) in self.live:
            if s0 < end and start < e0:
                cands = list(ob.readers.values()) + list(ob.dma_ws)
                if ob.last_w is not None:
                    cands.append(ob.last_w)
                for d in cands:
                    k = ('dma', id(d)) if d.is_dma else d.eng
                    if k not in b.readers or (not d.is_dma and b.readers[k].gidx < d.gidx):
                        b.readers[k] = d
                if not (start <= s0 and e0 <= end):
                    keep.append((s0, e0, ob))
            else:
                keep.append((s0, e0, ob))
        keep.append((start, end, b))
        self.live = keep
        ap = self.arena[0:shape[0], start // 2: end // 2]
        if dt != BF16:
            ap = ap.bitcast(dt)
        if len(shape) == 3:
            ap = ap.rearrange("p (a b) -> p a b", a=shape[1])
        elif len(shape) == 4:
            ap = ap.rearrange("p (a b c) -> p a b c", a=shape[1], b=shape[2])
        return V(ap, b)

    def bank(self):
        b = self.banks[self.bi % 8]
        self.bi += 1
        return b

    def dram(self, name, shape, dt, kind="Internal"):
        t = self.nc.dram_tensor(name, list(shape), dt, kind=kind)
        return V(t.ap(), Buf(name))

    def op(self, eng, fn, r=(), w=(), is_dma=False):
        o = Op(eng, fn, is_dma)
        o.gidx = self.n
        self.n += 1
        deps = {}

        def add(d):
            if d is None or d is o:
                return
            if d.is_dma:
                deps[('dma', id(d))] = d
            else:
                if d.eng == eng and eng == 'pe' and not is_dma:
                    return
                k = d.eng
                if k not in deps or deps[k].gidx < d.gidx:
                    deps[k] = d
        rb = [x.buf for x in r]
        wb = [x.buf for x in w]
        for b in rb:
            add(b.last_w)
            for d in b.dma_ws:
                add(d)
            if b.excl:
                for d in b.readers.values():
                    if d.eng != eng:
                        add(d)
        for b in wb:
            add(b.last_w)
            for d in b.readers.values():
                add(d)
            if not is_dma:
                for d in b.dma_ws:
                    add(d)
        for b in wb:
            if is_dma:
                if b.readers:
                    b.dma_ws = []
                    b.readers = {}
                b.dma_ws.append(o)
            else:
                b.last_w = o
                b.dma_ws = []
                b.readers = {}
        for b in rb:
            if b in wb:
                continue
            if is_dma:
                b.readers[('dma', id(o))] = o
            else:
                b.readers[eng] = o
        o.deps = list(deps.values())
        for d in o.deps:
            d.has_dep = True
        self.ops[eng].append(o)
        return o

    def dma(self, out, in_, eng='sp', final=False, **kw):
        o = self.op(eng, lambda e: e.dma_start(out=out.ap, in_=in_.ap, **kw), r=[in_], w=[out], is_dma=True)
        o.has_dep = True
        if final:
            self.final_dmas.append(o)
        return o

    def mm(self, out, lhsT, rhs, start=True, stop=True, **kw):
        return self.op('pe', lambda e: e.matmul(out.ap, lhsT.ap, rhs.ap, start=start, stop=stop, **kw),
                       r=[lhsT, rhs] + ([] if start else [out]), w=[out])

    def tr(self, out, in_, ident):
        return self.op('pe', lambda e: e.transpose(out.ap, in_.ap, ident.ap), r=[in_, ident], w=[out])

    def act(self, out, in_, func, bias=None, scale=None, accum=None):
        r = [in_]
        kw = {}
        if bias is not None:
            if isinstance(bias, V):
                r.append(bias)
                kw['bias'] = bias.ap
            else:
                kw['bias'] = float(bias)
        if scale is not None:
            if isinstance(scale, V):
                r.append(scale)
                kw['scale'] = scale.ap
            else:
                kw['scale'] = float(scale)
        w = [out]
        if accum is not None:
            kw['accum_out'] = accum.ap
            w.append(accum)
        return self.op('act', lambda e: e.activation(out.ap, in_.ap, func, **kw), r=r, w=w)

    def tt(self, eng, out, in0, in1, op):
        return self.op(eng, lambda e: e.tensor_tensor(out.ap, in0.ap, in1.ap, op), r=[in0, in1], w=[out])

    def ts(self, eng, out, in0, s1, s2=None, op0=ALU.mult, op1=None):
        r = [in0]
        a1 = s1.ap if isinstance(s1, V) else float(s1)
        if isinstance(s1, V):
            r.append(s1)
        a2 = None
        if s2 is not None:
            a2 = s2.ap if isinstance(s2, V) else float(s2)
            if isinstance(s2, V):
                r.append(s2)
        if op1 is None:
            return self.op(eng, lambda e: e.tensor_scalar(out.ap, in0.ap, a1, None, op0), r=r, w=[out])
        return self.op(eng, lambda e: e.tensor_scalar(out.ap, in0.ap, a1, a2, op0, op1), r=r, w=[out])

    def stt(self, out, in0, s, in1, op0, op1):
        r = [in0, in1]
        a = s.ap if isinstance(s, V) else float(s)
        if isinstance(s, V):
            r.append(s)
        return self.op('dve', lambda e: e.scalar_tensor_tensor(out.ap, in0.ap, a, in1.ap, op0, op1), r=r, w=[out])

    def cp(self, eng, out, in_):
        if eng == 'act':
            return self.op('act', lambda e: e.copy(out.ap, in_.ap), r=[in_], w=[out])
        return self.op(eng, lambda e: e.tensor_copy(out.ap, in_.ap), r=[in_], w=[out])

    def memset(self, eng, out, val):
        return self.op(eng, lambda e: e.memset(out.ap, val), r=[], w=[out])

    def scan(self, out, d0, d1, init, op0, op1):
        return self.op('dve', lambda e: e.tensor_tensor_scan(out.ap, d0.ap, d1.ap, float(init), op0, op1),
                       r=[d0, d1], w=[out])

    def recip(self, out, in_):
        return self.op('dve', lambda e: e.reciprocal(out.ap, in_.ap), r=[in_], w=[out])

    def reduce(self, out, in_, op=ALU.add):
        return self.op('dve', lambda e: e.tensor_reduce(out.ap, in_.ap, AX.X, op), r=[in_], w=[out])

    def emit(self):
        nc = self.nc
        sem_names = []
        for eng in ENGS:
            cnt = 0
            dcnt = 0
            for o in self.ops[eng]:
                if o.is_dma:
                    j = dcnt % NDSEM
                    name = f"d_{eng}_{j}"
                    o.sig = (name, 16 * (dcnt // NDSEM + 1), 16)
                    dcnt += 1
                elif o.has_dep:
                    ep = cnt // EPOCH
                    name = f"c_{eng}_{ep}"
                    o.sig = (name, cnt % EPOCH + 1, 1)
                    cnt += 1
                else:
                    continue
                if name not in sem_names:
                    sem_names.append(name)
        sems = {}
        for nm in sem_names:
            sems[nm] = self.es.enter_context(nc.semaphore(nm))
        self.nsem = len(sem_names)
        block = self.es.enter_context(nc.Block())
        prog = self

        def run(eng, e):
            known = {}
            for o in prog.ops[eng]:
                if o.is_dma:
                    nm, val, inc = o.sig
                    if val > 16 and known.get(nm, 0) < val - 16:
                        e.wait_ge(sems[nm], val - 16)
                        known[nm] = val - 16
                for d in o.deps:
                    nm, val, inc = d.sig
                    if known.get(nm, 0) < val:
                        e.wait_ge(sems[nm], val)
                        known[nm] = val
                inst = o.fn(e)
                if o.sig is not None:
                    nm, val, inc = o.sig
                    inst.then_inc(sems[nm], inc)
            if eng == 'sp':
                last = {}
                for en in ENGS:
                    for o in prog.ops[en]:
                        if o.is_dma:
                            nm, val, inc = o.sig
                            last[nm] = max(last.get(nm, 0), val)
                for nm, val in last.items():
                    if known.get(nm, 0) < val:
                        e.wait_ge(sems[nm], val)
                        known[nm] = val

        @block.tensor
        def _(e):
            run('pe', e)

        @block.scalar
        def _(e):
            run('act', e)

        @block.vector
        def _(e):
            run('dve', e)

        @block.gpsimd
        def _(e):
            run('pool', e)

        @block.sync
        def _(e):
            run('sp', e)


D = 1024
NKC = 8
NH = 8
NCH = 28
WCOLS = NCH * 128
C_DEC = float(np.exp(-0.5))
ATT_SCALE = float(96 ** -0.5)
NORM_EPS = 1e-6
GN_EPS = 64e-5
NV = 80
VG, VB, VMU, VMUV, VW0, VA0, VV0, VKK, VKA, VRK, VQG, VKVG = 0, 8, 32, 45, 46, 50, 54, 58, 62, 66, 70, 73
CI, CMT, CML, CTRI, CBO, CBC, CINV, CRST, NCC = 0, 128, 640, 768, 896, 1024, 1026, 1028, 1540
TWO_PI = float(2 * np.pi)
CW1 = 6.28125
CW2 = float(2 * np.pi - 6.28125)


def make_consts():
    c = np.zeros((128, NCC), np.float32)
    i = np.arange(128)
    c[:, CI:CI + 128] = np.eye(128)
    strict = (i[None, :] > i[:, None]).astype(np.float32)
    incl = (i[None, :] >= i[:, None]).astype(np.float32)
    c[:, CMT:CMT + 512] = np.concatenate([strict, incl, -strict, incl], axis=1)
    c[:, CML:CML + 128] = -(i[None, :] < i[:, None]).astype(np.float32)
    c[:, CTRI:CTRI + 128] = incl
    bo = np.zeros((128, 128), np.float32)
    bo[:64, :64] = 1
    bo[64:, 64:] = 1
    c[:, CBO:CBO + 128] = bo
    c[:64, CBC] = 1
    c[64:, CBC + 1] = 1
    inv = (10000.0 ** (-np.arange(0, 32, 2, dtype=np.float32) / 32)).astype(np.float32)
    c[:, CINV] = inv[i % 16]
    rs = np.ones(512, np.float32)
    rs[::128] = 0
    c[:, CRST:CRST + 512] = rs[None, :]
    return c


class _Stop(Exception):
    pass


def build(S, L, taps=(), stop=None):
    NB = S // 512
    NT = S // 128
    nc = bass.Bass("TRN2", target_bir_lowering=False)
    es = ExitStack()
    P = Prog(nc, es, arena_bytes=195 * 1024)
    tapset = set(taps)

    def din(name, shape, dt=F32):
        t = nc.dram_tensor(name, list(shape), dt, kind="ExternalInput")
        return V(t.ap(), Buf(name))

    def dscr(name, shape, dt):
        kind = "ExternalOutput" if name in tapset else "Internal"
        return P.dram(name, shape, dt, kind=kind)

    def finish():
        P.emit()
        es.close()
        return nc

    def chk(tag):
        if stop == tag:
            raise _Stop()

    x_d = din("x", [S, D])
    cT_d = din("cT", [128, 8])
    pos_d = din("pos", [1, S], I32)
    consts_d = din("consts", [128, NCC])
    vecs_d = din("vecs", [128, L, NV])
    rows_d = din("rows", [L, 1024])
    masku_d = din("masku", [128, 7 * 128], BF16)
    fing_d = din("final_g", [1, D])
    wada_d = din("w_ada", [L, D, 3 * D])
    win_d = din("w_in", [L, D, 3360])
    wvd_d = din("w_vd", [max(L - 1, 1), D, 32])
    wdec_d = din("w_dec", [L, 64, 512])
    wicl_d = din("w_icl", [L, 64, 512])
    wvup_d = din("w_vup", [max(L - 1, 1), 32, 512])
    wuq_d = din("w_uq", [L, 384, 768])
    wukv_d = din("w_ukv", [L, 256, 1024])
    wout_d = din("w_out", [L, D, D])
    out_d = P.dram("out", [S, D], F32, kind="ExternalOutput")

    xT_d = dscr("xT", [D, S], F32)
    cs_d = dscr("cs", [2, 128, S], F32)
    vfirst_d = dscr("vfirst", [512, S], F32)
    feat_d = dscr("feat", [NT, 64, NH, 4, 128], BF16)
    tokm_d = dscr("tokm", [S, 4, 512], BF16)
    gc_d = dscr("gc", [512, NT], F32)
    sbon_d = dscr("sbon", [S, NH], F32)
    gate_d = dscr("gate", [D, S], BF16)
    qT_d = dscr("qT", [NH, 96, S], BF16)
    kT_d = dscr("kT", [NH, 64, S], BF16)
    krT_d = dscr("krT", [32, S], BF16)
    v1_d = dscr("v1", [S, NH, 65], BF16)
    ytok_d = dscr("ytok", [S, D], BF16)
    xT_v = xT_d.re("(c p) t -> p c t", p=128)

    consts = P.sb("consts", [128, NCC])
    vecs = P.sb("vecs", [128, L, NV])
    mods = P.sb("mods", [128, L, 24])
    gs = P.sb("gs", [128, L, 8])
    omka = P.sb("omka", [128, L, 4])
    identb = P.sb("identb", [128, 128], BF16)
    onesb = P.sb("onesb", [128, 128], BF16)
    bonesb = P.sb("bonesb", [128, 128], BF16)
    bcolsb = P.sb("bcolsb", [128, 2], BF16)
    epsn = P.sb("epsn", [128, 1])
    epsg = P.sb("epsg", [128, 1])
    halfpi = P.sb("halfpi", [128, 1])
    masku = P.sb("masku", [128, 7, 128], BF16)
    sball = P.sb("sball", [128, NT, NH])
    gcall = P.sb("gcall", [128, 4, NT])
    ident = consts[:, CI:CI + 128]
    maskT = consts[:, CMT:CMT + 512]
    masklow = consts[:, CML:CML + 128]
    tri = consts[:, CTRI:CTRI + 128]
    invf = consts[:, CINV:CINV + 1]
    resetm = consts[:, CRST:CRST + 512]

    P.dma(consts, consts_d)
    P.dma(vecs, vecs_d)
    P.dma(masku, masku_d.re("p (j t) -> p j t", j=7))
    P.cp('dve', identb, ident)
    P.memset('pool', onesb, 1.0)
    P.memset('pool', epsn, NORM_EPS)
    P.memset('pool', epsg, GN_EPS)
    P.memset('pool', halfpi, float(np.pi / 2))
    P.cp('dve', bonesb, consts[:, CBO:CBO + 128])
    P.cp('dve', bcolsb, consts[:, CBC:CBC + 2])

    m0 = P.mark()
    cact = P.al("cact", [128, 8])
    P.dma(cact, cT_d)
    P.act(cact, cact, AF.Silu)
    wst_t = [P.al("wst0", [128, 8, 512]), P.al("wst1", [128, 8, 512])]
    mbank = P.bank()
    n = 0
    for l in range(L):
        for cg in range(6):
            wst = wst_t[n % 2]
            n += 1
            P.dma(wst, wada_d[l, :, cg * 512:(cg + 1) * 512].re("(kc p) n -> p kc n", p=128))
            for j in range(4):
                col = l * 24 + cg * 4 + j
                for kc in range(8):
                    P.mm(mbank[:, col:col + 1], wst[:, kc, j * 128:(j + 1) * 128], cact[:, kc:kc + 1],
                         start=(kc == 0), stop=(kc == 7))
    P.tt('dve', mods, mbank[:, 0:L * 24].re("p (l j) -> p l j", l=L), vecs[:, :, VB:VB + 24], ALU.add)
    P.stt(gs, mods[:, :, 8:16], 1.0, vecs[:, :, VG:VG + 8], ALU.add, ALU.mult)
    P.ts('dve', omka, vecs[:, :, VKA:VKA + 4], -1.0, 1.0, ALU.mult, ALU.add)

    xs = P.al("xs", [128, 4, D])
    xts = P.al("xts", [128, 8, 512])
    posi = P.al("posi", [128, 512], I32)
    ang = P.al("ang", [128, 512])
    rk_ = P.al("rk_", [128, 512])
    rki = P.al("rki", [128, 512], I32)
    rr = P.al("rr", [128, 512])
    mk = P.al("mk", [128, 512])
    for b in range(NB):
        t0 = b * 512
        P.dma(xs, x_d[t0:t0 + 512, :].re("(j p) d -> p j d", p=128))
        for c in range(8):
            bk = P.bank()
            for j in range(4):
                P.tr(bk[:, j * 128:(j + 1) * 128], xs[:, j, c * 128:(c + 1) * 128], ident)
            P.cp('act' if c % 2 else 'dve', xts[:, c, :], bk)
        P.dma(xT_v[:, :, t0:t0 + 512], xts)
        P.dma(posi, V(pos_d.ap[:, t0:t0 + 512].partition_broadcast(128), pos_d.buf))
        P.cp('pool', ang, posi)
        P.ts('dve', ang, ang, invf, None, ALU.mult)
        P.ts('dve', rk_, ang, 1.0 / TWO_PI, None, ALU.mult)
        P.cp('dve', rki, rk_)
        P.cp('dve', rk_, rki)
        P.stt(rr, rk_, -CW1, ang, ALU.mult, ALU.add)
        P.stt(rr, rk_, -CW2, rr, ALU.mult, ALU.add)
        P.ts('dve', mk, rr, float(np.pi), None, ALU.is_gt)
        P.stt(rr, mk, -TWO_PI, rr, ALU.mult, ALU.add)
        P.ts('dve', mk, rr, float(-np.pi), None, ALU.is_lt)
        P.stt(rr, mk, TWO_PI, rr, ALU.mult, ALU.add)
        P.ts('dve', rr, rr, float(np.pi), float(-np.pi), ALU.min, ALU.max)
        P.act(mk, rr, AF.Sin)
        P.dma(cs_d[1, :, t0:t0 + 512], mk)
        P.stt(rk_, rr, -1.0, rr, ALU.mult, ALU.max)
        P.act(ang, rk_, AF.Sin, bias=halfpi, scale=-1.0)
        P.dma(cs_d[0, :, t0:t0 + 512], ang)
    P.release(m0)
    if stop == 'p0':
        return finish()

    def layer(l):
        mL = P.mark()
        W = P.al("W", [128, 8, WCOLS], BF16)
        lora_up = P.al("lora_up", [128, 512], BF16)
        vmix_up = P.al("vmix_up", [128, 512], BF16)
        Wq = P.al("Wq", [128, 3, 1024], BF16)
        Wkv = P.al("Wkv", [128, 2, 1024], BF16)
        mW = P.mark()
        wstage = P.al("wstage", [128, WCOLS])
        P.memset('pool', wstage, 0.0)
        for kc in range(8):
            rows = slice(kc * 128, (kc + 1) * 128)
            P.dma(wstage[:, 0:2816], win_d[l, rows, 0:2816])
            P.dma(wstage[:, 2816:3328], win_d[l, rows, 2848:3360])
            P.dma(wstage[:, 3328:3360], win_d[l, rows, 2816:2848])
            if l > 0:
                P.dma(wstage[:, 3392:3424], wvd_d[l - 1, rows, :])
            P.dma(wstage[:, 3456:3472], win_d[l, rows, 2832:2848])
            P.dma(wstage[:, 3472:3488], win_d[l, rows, 2816:2832])
            P.cp('dve' if kc % 2 else 'pool', W[:, kc, :], wstage)
            P.ts('dve', W[:, kc, 3456:3472], W[:, kc, 3456:3472], -1.0, None, ALU.mult)
        lst = P.al("lst", [128, 512])
        P.dma(lst[0:64], wdec_d[l])
        P.dma(lst[64:128], wicl_d[l])
        P.cp('pool', lora_up, lst)
        if l > 0:
            vst = P.al("vst", [128, 512])
            P.dma(vst[64:96], wvup_d[l - 1])
            P.cp('pool', vmix_up[64:96], vst[64:96])
        qst = P.al("qst", [128, 1024])
        for kc in range(3):
            src = wuq_d[l, kc * 128:(kc + 1) * 128, :].re("p (h e) -> p h e", e=96)
            P.dma(qst[:, 0:512].re("p (h e) -> p h e", e=64), src[:, :, 0:64])
            P.dma(qst[:, 512:768].re("p (h e) -> p h e", e=32), src[:, :, 64:96])
            rot = qst[:, 768:1024].re("p (h e) -> p h e", e=32)
            P.dma(rot[:, :, 0:16], src[:, :, 80:96])
            P.dma(rot[:, :, 16:32], src[:, :, 64:80])
            P.ts('dve', Wq[:, kc, :], qst, vecs[:, l, VQG + kc:VQG + kc + 1], None, ALU.mult)
            wr = Wq[:, kc, 768:1024].re("p (h e) -> p h e", e=32)[:, :, 0:16]
            P.ts('dve', wr, wr, -1.0, None, ALU.mult)
        for kc in range(2):
            src = wukv_d[l, kc * 128:(kc + 1) * 128, :].re("p (h e) -> p h e", e=128)
            P.dma(qst[:, 0:512].re("p (h e) -> p h e", e=64), src[:, :, 0:64])
            P.dma(qst[:, 512:1024].re("p (h e) -> p h e", e=64), src[:, :, 64:128])
            P.ts('dve', Wkv[:, kc, :], qst, vecs[:, l, VKVG + kc:VKVG + kc + 1], None, ALU.mult)

        chk(f'w_{l}')
        P.release(mW)
        carry = {j: P.al(f"carry{j}", [128, 1]) for j in list(range(13)) + [26]}
        for j in carry:
            P.memset('pool', carry[j], 0.0)
        pe2 = [P.al("pe0", [128, 513]), P.al("pe1", [128, 513])]
        npe = [0]
        xt = P.al("xt", [128, 8, 512])
        sq = P.al("sq", [128, 8, 512], BF16)
        hT = P.al("hT", [128, 8, 512], BF16)
        rstd = P.al("rstd", [128, 512])
        dtmp = P.al("dtmp", [128, 512])
        lsh = P.al("lsh", [128, 512])
        u2 = [dtmp, lsh]
        lora_in = P.al("lora_in", [128, 512], BF16)
        vlo_in = P.al("vlo_in", [128, 512], BF16)
        csb = P.al("csb", [128, 2, 512])
        F_ = {nm: P.al(nm, [128, 512]) for nm in
              ("rs0", "ks0", "vs0", "rs1", "ks1", "vs1", "sg", "aa", "gv", "vf", "kk", "nrm", "bb", "Gs", "g1")}
        B_ = {nm: P.al(nm, [128, 512], BF16) for nm in
              ("kk2", "Rt", "KKt", "Kt", "Bt", "Kh", "Bh", "Vb", "rkb")}
        nb4 = P.al("nb4", [128, 4])
        gC4 = P.al("gC4", [128, 4])
        sb4 = P.al("sb4", [128, 4, 2])
        tmo = P.al("tmo", [128, 4, 4, 128], BF16)
        gst = [P.al("gst0", [128, 512], BF16), P.al("gst1", [128, 512], BF16)]
        cq = P.al("cq", [128, 3, 512], BF16)
        sqq = P.al("sqq", [128, 3, 512], BF16)
        rq, crs, srs, a1, a2, rkv = (P.al(nm, [128, 512]) for nm in ("rq", "crs", "srs", "a1", "a2", "rkv"))
        qn = [P.al("qn0", [128, 512], BF16), P.al("qn1", [128, 512], BF16)]
        qrp = [P.al("qrp0", [128, 512], BF16), P.al("qrp1", [128, 512], BF16)]
        ckv = P.al("ckv", [128, 2, 512], BF16)
        sqk = P.al("sqk", [128, 2, 512], BF16)
        rcol = P.al("rcol", [128, 1])
        v1s = P.al("v1s", [128, 4, NH, 65], BF16)
        krb = P.al("krb", [128, 512], BF16)
        P.memset('pool', v1s[:, :, :, 64:65], 1.0)

        for b in range(NB):
            t0 = b * 512
            tsl = slice(t0, t0 + 512)
            P.dma(xt, xT_v[:, :, tsl])
            P.dma(csb, cs_d[:, :, tsl].re("w p t -> p w t"))
            for c in range(8):
                if c % 2:
                    P.act(sq[:, c, :], xt[:, c, :], AF.Square)
                else:
                    P.tt('pool', sq[:, c, :], xt[:, c, :], xt[:, c, :], ALU.mult)
            bk = P.bank()
            for c in range(8):
                P.mm(bk, onesb, sq[:, c, :], start=(c == 0), stop=(c == 7))
            P.act(rstd, bk, AF.Sqrt, bias=epsn, scale=1.0 / D)
            P.recip(rstd, rstd)
            for c in range(8):
                u = u2[c % 2]
                P.tt('pool' if c % 2 else 'dve', u, xt[:, c, :], rstd, ALU.mult)
                P.act(hT[:, c, :], u, AF.Identity, bias=mods[:, l, c:c + 1], scale=gs[:, l, c:c + 1])

            def proj(j, M=128):
                bk = P.bank()
                for kc in range(8):
                    P.mm(bk[0:M, :], W[:, kc, j * 128:j * 128 + M], hT[:, kc, :], start=(kc == 0), stop=(kc == 7))
                return bk

            def shifted(j, out, mucol, rows=slice(0, 128)):
                bk = proj(j)
                pe = pe2[npe[0] % 2]
                npe[0] += 1
                P.cp('pool', pe[rows, 0:1], carry[j][rows, :])
                P.cp('act', pe[:, 1:513], bk)
                P.tt('pool', dtmp[rows], pe[rows, 0:512], pe[rows, 1:513], ALU.subtract)
                P.stt(out[rows], dtmp[rows], vecs[rows, l, mucol:mucol + 1], pe[rows, 1:513], ALU.mult, ALU.add)
                P.cp('pool', carry[j][rows, :], pe[rows, 512:513])
                return pe

            chk(f'h_{l}')
            shifted(12, lsh, VMU + 12)
            P.act(lora_in[0:64], lsh[0:64], AF.Tanh)
            P.cp('pool', lora_in[64:128], lsh[64:128])
            if l > 0:
                pe26 = shifted(26, lsh, VMUV, rows=slice(64, 96))
                P.cp('pool', vlo_in[64:96], lsh[64:96])
            else:
                bk26 = proj(26)
                pe26 = pe2[npe[0] % 2]
                npe[0] += 1
                P.cp('act', pe26[:, 1:513], bk26)
            bk27 = proj(27, M=32)
            P.tt('dve', a1[0:32], pe26[0:32, 1:513], csb[0:32, 0, :], ALU.mult)
            P.tt('dve', a2[0:32], bk27[0:32, :], csb[0:32, 1, :], ALU.mult)
            P.tt('pool', krb[0:32], a1[0:32], a2[0:32], ALU.add)
            P.dma(krT_d[:, tsl], krb[0:32])

            def proj_shift(hp_):
                shifted(hp_, F_["rs" + str(hp_ % 2)], VMU + hp_)
                yield
                shifted(4 + hp_, F_["ks" + str(hp_ % 2)], VMU + 4 + hp_)
                yield
                shifted(8 + hp_, F_["vs" + str(hp_ % 2)], VMU + 8 + hp_)
                yield

            def rw_gen():
                for hp in range(4):
                    rs, ks, vs = (F_[k + str(hp % 2)] for k in ("rs", "ks", "vs"))
                    sg, aa, gv, vf = (F_[k] for k in ("sg", "aa", "gv", "vf"))
                    kk, nrm, bb, Gs, g1 = (F_[k] for k in ("kk", "nrm", "bb", "Gs", "g1"))
                    kkn = kk
                    ff = nrm
                    kmod = nrm
                    gi = e1 = ge = ginv = gcr = g1
                    rk = rs
                    if hp == 0:
                        for _ in proj_shift(0):
                            yield
                    if hp + 1 < 4:
                        for _ in proj_shift(hp + 1):
                            yield
                    cols = slice(hp * 128, (hp + 1) * 128)
                    prow = slice(hp * 128, (hp + 1) * 128)
                    bkw = P.bank()
                    P.mm(bkw, lora_up[0:64, cols], lora_in[0:64, :])
                    P.act(sg, bkw, AF.Sigmoid, bias=vecs[:, l, VW0 + hp:VW0 + hp + 1])
                    bka = P.bank()
                    P.mm(bka, lora_up[64:128, cols], lora_in[64:128, :])
                    P.act(aa, bka, AF.Sigmoid, bias=vecs[:, l, VA0 + hp:VA0 + hp + 1])
                    yield
                    if l > 0:
                        bkv = P.bank()
                        P.mm(bkv, vmix_up[64:96, cols], vlo_in[64:96, :])
                        P.act(gv, bkv, AF.Sigmoid, bias=vecs[:, l, VV0 + hp:VV0 + hp + 1])
                        P.dma(vf, vfirst_d[prow, tsl])
                        P.tt('pool', vf, vf, vs, ALU.subtract)
                        P.tt('pool', vf, vf, gv, ALU.mult)
                        P.tt('pool', vs, vs, vf, ALU.add)
                    else:
                        P.dma(vfirst_d[prow, tsl], vs)
                    P.act(kk, ks, AF.Identity, scale=vecs[:, l, VKK + hp:VKK + hp + 1])
                    P.tt('dve', B_["kk2"], kk, kk, ALU.mult)
                    bks = P.bank()
                    P.mm(bks, bonesb, B_["kk2"])
                    P.act(nrm, bks, AF.Sqrt)
                    P.ts('dve', nrm, nrm, 1e-12, None, ALU.max)
                    P.recip(nrm, nrm)
                    yield
                    P.tt('dve', kkn, kk, nrm, ALU.mult)
                    P.ts('dve', ff, aa, vecs[:, l, VKA + hp:VKA + hp + 1], omka[:, l, hp:hp + 1], ALU.mult, ALU.add)
                    P.tt('dve', kmod, ff, ks, ALU.mult)
                    P.tt('dve', bb, kkn, aa, ALU.mult)
                    P.scan(Gs, resetm, sg, 0.0, ALU.mult, ALU.add)
                    yield
                    P.act(gi, Gs, AF.Exp, scale=-C_DEC)
                    P.tt('dve', B_["Rt"], rs, gi, ALU.mult)
                    P.tt('pool', e1, Gs, sg, ALU.subtract)
                    P.act(ge, e1, AF.Exp, scale=-C_DEC)
                    P.tt('dve', B_["KKt"], kkn, ge, ALU.mult)
                    yield
                    P.act(ginv, Gs, AF.Exp, scale=C_DEC)
                    P.tt('pool', B_["Kt"], kmod, ginv, ALU.mult)
                    P.tt('dve', B_["Bt"], bb, ginv, ALU.mult)
                    yield
                    GsC = Gs.re("p (q t) -> p q t", t=128)[:, :, 127]
                    P.ts('dve', nb4, GsC, -C_DEC, None, ALU.mult)
                    for q in range(4):
                        P.act(gcr[:, q * 128:(q + 1) * 128], Gs[:, q * 128:(q + 1) * 128], AF.Exp, scale=C_DEC,
                              bias=nb4[:, q:q + 1])
                    P.tt('dve', B_["Kh"], kmod, gcr, ALU.mult)
                    P.tt('pool', B_["Bh"], bb, gcr, ALU.mult)
                    yield
                    P.act(gcall[:, hp, b * 4:(b + 1) * 4], nb4, AF.Exp)
                    P.tt('pool', rk, rs, kmod, ALU.mult)
                    P.act(B_["rkb"], rk, AF.Identity, scale=vecs[:, l, VRK + hp:VRK + hp + 1])
                    bkb = P.bank()
                    for q in range(4):
                        P.mm(bkb[:, q * 2:(q + 1) * 2], B_["rkb"][:, q * 128:(q + 1) * 128], bcolsb)
                    P.cp('dve', sball[:, b * 4:(b + 1) * 4, 2 * hp:2 * hp + 2], bkb[:, 0:8].re("p (q h) -> p q h", h=2))
                    P.cp('pool', B_["Vb"], vs)
                    yield
                    for opi, nm in enumerate(("Kt", "Bt", "KKt", "Rt")):
                        for hh in range(2):
                            P.dma(feat_d[b * 4:(b + 1) * 4, :, 2 * hp + hh, opi, :].re("c k t -> k c t"),
                                  B_[nm][hh * 64:(hh + 1) * 64, :].re("k (c t) -> k c t", c=4))
                    for q in range(4):
                        bkt = P.bank().cast(BF16)
                        for opi, nm in enumerate(("KKt", "Vb", "Kh", "Bh")):
                            P.tr(bkt[:, opi * 128:(opi + 1) * 128], B_[nm][:, q * 128:(q + 1) * 128], identb)
                        P.cp('act' if q % 2 else 'dve', tmo[:, q, :, :], bkt[:, 0:512].re("p (o f) -> p o f", o=4))
                        yield
                    for q in range(4):
                        P.dma(tokm_d[t0 + q * 128:t0 + (q + 1) * 128, :, cols], tmo[:, q, :, :])

            def mla_gen():
                for gi_, j in enumerate([13, 14, 15, 16, 22, 23, 24, 25]):
                    bk = proj(j)
                    g = gst[gi_ % 2]
                    P.act(g, bk, AF.Silu)
                    P.dma(gate_d[gi_ * 128:(gi_ + 1) * 128, tsl], g)
                    yield
                for i in range(3):
                    bk = proj(17 + i)
                    P.cp('act', cq[:, i, :], bk)
                    P.act(sqq[:, i, :], bk, AF.Square)
                    yield
                bk = P.bank()
                for i in range(3):
                    P.mm(bk, onesb, sqq[:, i, :], start=(i == 0), stop=(i == 2))
                P.act(rq, bk, AF.Sqrt, bias=epsn, scale=1.0 / 384)
                P.recip(rq, rq)
                yield
                P.tt('pool', crs, csb[:, 0, :], rq, ALU.mult)
                P.tt('pool', srs, csb[:, 1, :], rq, ALU.mult)
                for i in range(4):
                    bk = P.bank()
                    for kc in range(3):
                        P.mm(bk, Wq[:, kc, i * 128:(i + 1) * 128], cq[:, kc, :], start=(kc == 0), stop=(kc == 2))
                    qq = qn[i % 2]
                    P.tt('dve', qq, bk, rq, ALU.mult)
                    for hh in range(2):
                        P.dma(qT_d[2 * i + hh, 0:64, tsl], qq[hh * 64:(hh + 1) * 64, :])
                    yield
                for i in range(2):
                    bk1 = P.bank()
                    for kc in range(3):
                        P.mm(bk1, Wq[:, kc, 512 + i * 128:512 + (i + 1) * 128], cq[:, kc, :], start=(kc == 0),
                             stop=(kc == 2))
                    bk2 = P.bank()
                    for kc in range(3):
                        P.mm(bk2, Wq[:, kc, 768 + i * 128:768 + (i + 1) * 128], cq[:, kc, :], start=(kc == 0),
                             stop=(kc == 2))
                    P.tt('dve', a1, bk1, crs, ALU.mult)
                    P.tt('dve', a2, bk2, srs, ALU.mult)
                    qq = qrp[i % 2]
                    P.tt('dve', qq, a1, a2, ALU.add)
                    for hh in range(4):
                        P.dma(qT_d[4 * i + hh, 64:96, tsl], qq[hh * 32:(hh + 1) * 32, :])
                    yield
                for i in range(2):
                    bk = proj(20 + i)
                    P.cp('act', ckv[:, i, :], bk)
                    P.act(sqk[:, i, :], bk, AF.Square)
                    yield
                bk = P.bank()
                for i in range(2):
                    P.mm(bk, onesb, sqk[:, i, :], start=(i == 0), stop=(i == 1))
                P.act(rkv, bk, AF.Sqrt, bias=epsn, scale=1.0 / 256)
                P.recip(rkv, rkv)
                for i in range(4):
                    bk = P.bank()
                    for kc in range(2):
                        P.mm(bk, Wkv[:, kc, i * 128:(i + 1) * 128], ckv[:, kc, :], start=(kc == 0), stop=(kc == 1))
                    qq = qn[i % 2]
                    P.tt('dve', qq, bk, rkv, ALU.mult)
                    for hh in range(2):
                        P.dma(kT_d[2 * i + hh, :, tsl], qq[hh * 64:(hh + 1) * 64, :])
                    yield
                for q in range(4):
                    qs = slice(q * 128, (q + 1) * 128)
                    bkc = P.bank()
                    for i in range(2):
                        P.mm(bkc[:, 0:1], sqk[:, i, qs], onesb[:, 0:1], start=(i == 0), stop=(i == 1))
                    P.act(rcol, bkc[:, 0:1], AF.Sqrt, bias=epsn, scale=1.0 / 256)
                    P.recip(rcol, rcol)
                    bkv = P.bank()
                    for i in range(2):
                        P.mm(bkv, ckv[:, i, qs], Wkv[:, i, 512:1024], start=(i == 0), stop=(i == 1))
                    P.ts('dve', v1s[:, q, :, 0:64], bkv.re("p (h e) -> p h e", e=64), rcol, None, ALU.mult)
                    yield
                for q in range(4):
                    P.dma(v1_d[t0 + q * 128:t0 + (q + 1) * 128], v1s[:, q, :, :])
            gens_ = [rw_gen(), mla_gen()]
            while gens_:
                nx_ = []
                for gg in gens_:
                    try:
                        next(gg)
                        nx_.append(gg)
                    except StopIteration:
                        pass
                gens_ = nx_
        for hp in range(4):
            P.dma(gc_d[hp * 128:(hp + 1) * 128, :], gcall[:, hp, :])
        if 'sbon' in tapset:
            P.dma(sbon_d.re("(c p) h -> p c h", p=128), sball)
        P.release(mL)
        if stop == f'p1_{l}':
            return True

        gcs = P.al("gcs", [64, NH, NT])
        P.dma(gcs, gc_d.re("(h k) c -> k h c", k=64))
        sbn = sball
        lnw = P.al("lnw", [128, 1024])
        P.dma(lnw, V(rows_d.ap[l:l + 1, :].partition_broadcast(128), rows_d.buf))
        lnw3 = lnw[:, 0:512].re("p (h e) -> p h e", e=64)
        lnb3 = lnw[:, 512:1024].re("p (h e) -> p h e", e=64)
        Sf = P.al("Sf", [64, NH, 64])
        Sb = P.al("Sb", [64, NH, 64], BF16)
        P.memset('pool', Sf, 0.0)
        P.memset('pool', Sb, 0.0)
        GL = 3

        def slot_tiles(i):
            d = {}
            d['F'] = P.al(f"F{i}", [64, NH, 4, 128], BF16)
            d['T'] = P.al(f"T{i}", [128, 4, 512], BF16)
            d['XK'] = P.al(f"XK{i}", [128, NH, 128], BF16)
            d['AT'] = P.al(f"AT{i}", [128, NH, 512], BF16)
            for nm in ('Nk', 'Dk', 'Wk', 'Y1', 'Dl'):
                d[nm] = [P.al(f"{nm}{i}_{g}", [128, 4, 128], BF16) for g in range(2)]
            d['MU'] = [P.al(f"MU{i}_{g}", [128, 4, 6, 128], BF16) for g in range(2)]
            d['nPQ'] = P.al(f"nPQ{i}", [128, NH, 128], BF16)
            d['nZ'] = P.al(f"nZ{i}", [64, NH, 64], BF16)
            d['Psi'] = P.al(f"Psi{i}", [64, NH, 64])
            d['Ry'] = P.al(f"Ry{i}", [64, NH, 128], BF16)
            d['t1'] = P.al(f"t1{i}", [64, NH, 64])
            d['y'] = P.al(f"y{i}", [128, NH, 64])
            d['yc'] = P.al(f"yc{i}", [128, NH, 64])
            d['ysq'] = P.al(f"ysq{i}", [128, NH, 64])
            d['mean'] = P.al(f"mean{i}", [128, NH])
            d['var'] = P.al(f"var{i}", [128, NH])
            d['ybf'] = P.al(f"ybf{i}", [128, 512], BF16)
            return d
        slots = [slot_tiles(i) for i in range(GL)]

        def chunk_gen(c):
            d = slots[c % GL]
            F, T, XK, AT, Nk, Dk, Wk, Y1, Dl, MU = (d[k] for k in ('F', 'T', 'XK', 'AT', 'Nk', 'Dk', 'Wk', 'Y1', 'Dl', 'MU'))
            nPQ, nZ, Psi, Ry, t1, y, yc, ysq, mean, var, ybf = (d[k] for k in
                                                               ('nPQ', 'nZ', 'Psi', 'Ry', 't1', 'y', 'yc', 'ysq', 'mean', 'var', 'ybf'))
            crow = slice(c * 128, (c + 1) * 128)
            P.dma(F, feat_d[c])
            P.dma(T, tokm_d[crow])
            P.dma(XK[:, :, 0:64], tokm_d[crow, 0, :].re("p (h e) -> p h e", e=64))
            yield
            for h in range(NH):
                bk = P.bank()
                KR = F[:, h, 2:4, :].re("k o t -> k (o t)")
                P.mm(bk[:, 0:256], F[:, h, 0, :], KR)
                P.mm(bk[:, 256:512], F[:, h, 1, :], KR)
                P.tt('dve', AT[:, h, :], bk, maskT, ALU.mult)
                if h % 4 == 3:
                    yield
            for g in range(2):
                bkn = P.bank()
                for hh in range(4):
                    h = 4 * g + hh
                    P.mm(bkn[:, hh * 128:(hh + 1) * 128], F[:, h, 2, :], F[:, h, 1, :])
                P.tt('dve', Nk[g], bkn.re("p (h s) -> p h s", h=4), masklow.un(1).bc([128, 4, 128]), ALU.mult)
                m0b = masku[:, 0, :].un(1).bc([128, 4, 128])
                idb = identb.un(1).bc([128, 4, 128])
                P.tt('pool', Dk[g], Nk[g], m0b, ALU.mult)
                P.tt('pool', Dk[g], Dk[g], idb, ALU.add)
                P.tt('pool', Wk[g], AT[:, 4 * g:4 * g + 4, 256:384], m0b, ALU.mult)
                P.tt('pool', Wk[g], Wk[g], idb, ALU.add)
                for hh in range(4):
                    P.tt('pool', MU[g][:, hh, :, :], AT[:, 4 * g + hh, 256:384].un(1).bc([128, 6, 128]),
                         masku[:, 1:7, :], ALU.mult)
                yield
            for j in range(1, 7):
                bkYs = []
                for g in range(2):
                    bkY = P.bank()
                    for hh in range(4):
                        P.mm(bkY[:, hh * 128:(hh + 1) * 128], MU[g][:, hh, j - 1, :], Dk[g][:, hh, :])
                    P.cp('act', Y1[g], bkY.re("p (h s) -> p h s", h=4))
                yield
                bkDs = []
                for g in range(2):
                    bkD = P.bank()
                    bkDs.append(bkD)
                    for hh in range(4):
                        P.mm(bkD[:, hh * 128:(hh + 1) * 128], Wk[g][:, hh, :], Y1[g][:, hh, :])
                    P.cp('act', Dl[g], bkD.re("p (h s) -> p h s", h=4))
                    if j < 6:
                        P.tt('dve', Dk[g], bkD.re("p (h s) -> p h s", h=4), Dk[g], ALU.add)
                yield
                for g in range(2):
                    bkT = P.bank().cast(BF16)
                    for hh in range(4):
                        P.tr(bkT[:, hh * 128:(hh + 1) * 128], Dl[g][:, hh, :], identb)
                    P.tt('dve', Wk[g], bkT[:, 0:512].re("p (h s) -> p h s", h=4), Wk[g], ALU.add)
                yield
            bkx = P.bank()
            for h in range(NH):
                hc = slice(h * 64, (h + 1) * 64)
                P.mm(bkx[:, hc], AT[:, h, 0:128], T[:, 1, hc])
            P.cp('act', XK[:, :, 64:128], bkx.re("p (h e) -> p h e", e=64))
            yield
            for g in range(2):
                bkp = P.bank()
                for hh in range(4):
                    h = 4 * g + hh
                    P.mm(bkp[:, hh * 128:(hh + 1) * 128], Wk[g][:, hh, :], XK[:, h, :])
                P.act(nPQ[:, 4 * g:4 * g + 4, :], bkp.re("p (h s) -> p h s", h=4), AF.Copy, scale=-1.0)
            yield
            bkz = P.bank()
            for h in range(NH):
                hc = slice(h * 64, (h + 1) * 64)
                P.mm(bkz[0:64, hc], nPQ[:, h, 0:64], T[:, 3, hc])
            P.cp('act', nZ, bkz[0:64, :].re("k (h e) -> k h e", e=64))
            bkpsi = P.bank()
            for h in range(NH):
                hc = slice(h * 64, (h + 1) * 64)
                P.mm(bkpsi[0:64, hc], T[:, 2, hc], T[:, 1, hc], start=True, stop=False)
                P.mm(bkpsi[0:64, hc], T[:, 3, hc], nPQ[:, h, 64:128], start=False, stop=True)
            P.cp('dve', Psi, bkpsi[0:64, :].re("k (h e) -> k h e", e=64))
            for g in range(2):
                bkr = P.bank()
                for hh in range(4):
                    h = 4 * g + hh
                    P.mm(bkr[0:64, hh * 128:(hh + 1) * 128], nPQ[:, h, 0:64], AT[:, h, 384:512])
                P.tt('dve', Ry[:, 4 * g:4 * g + 4, :], bkr[0:64, :].re("k (h t) -> k h t", h=4),
                     F[:, 4 * g:4 * g + 4, 3, :], ALU.add)
            yield
            bky = P.bank()
            for h in range(NH):
                hc = slice(h * 64, (h + 1) * 64)
                P.mm(bky[:, hc], AT[:, h, 128:256], T[:, 1, hc], start=True, stop=False)
                P.mm(bky[:, hc], AT[:, h, 384:512], nPQ[:, h, 64:128], start=False, stop=False)
                P.mm(bky[:, hc], Ry[:, h, :], Sb[:, h, :], start=False, stop=True)
            bku = P.bank()
            for h in range(NH):
                hc = slice(h * 64, (h + 1) * 64)
                P.mm(bku[0:64, hc], nZ[:, h, :], Sb[:, h, :])
            P.tt('pool', t1, Sf, gcs[:, :, c].un(2).bc([64, NH, 64]), ALU.mult)
            P.tt('pool', t1, t1, Psi, ALU.add)
            bku3 = bku[0:64, :].re("k (h e) -> k h e", e=64)
            P.tt('dve', Sb, bku3, t1, ALU.add)
            P.tt('dve', Sf, bku3, t1, ALU.add)
            yield
            P.cp('act', y, bky.re("p (h e) -> p h e", e=64))
            P.reduce(mean, y)
            P.ts('dve', mean, mean, 1.0 / 64, None, ALU.mult)
            P.tt('pool', yc, y, mean.un(2).bc([128, NH, 64]), ALU.subtract)
            P.tt('pool', ysq, yc, yc, ALU.mult)
            P.reduce(var, ysq)
            P.act(var, var, AF.Sqrt, bias=epsg, scale=1.0 / 64)
            P.recip(var, var)
            yield
            P.tt('pool', yc, yc, var.un(2).bc([128, NH, 64]), ALU.mult)
            P.tt('pool', yc, yc, lnw3, ALU.mult)
            P.tt('pool', yc, yc, lnb3, ALU.add)
            P.tt('pool', ysq, T[:, 1, :].re("p (h e) -> p h e", e=64), sbn[:, c, :].un(2).bc([128, NH, 64]), ALU.mult)
            P.tt('pool', ybf.re("p (h e) -> p h e", e=64), yc, ysq, ALU.add)
            P.dma(ytok_d[crow, 0:512], ybf)

        def run_lockstep(gens):
            live = list(gens)
            while live:
                nxt = []
                for gg in live:
                    try:
                        next(gg)
                        nxt.append(gg)
                    except StopIteration:
                        pass
                live = nxt
        for c0 in range(0, NT, GL):
            run_lockstep([chunk_gen(c) for c in range(c0, min(NT, c0 + GL))])
        P.release(mL)
        if stop == f'p2_{l}':
            return True

        v1a = P.al("v1a", [128, NT, NH * 65], BF16)
        P.dma(v1a, v1_d.re("(j p) h e -> p j (h e)", p=128))
        kTq = [P.al(f"kTh{i}", [96, S], BF16) for i in range(2)]
        qTq = [P.al(f"qTh{i}", [96, S], BF16) for i in range(2)]
        Et = [P.al(f"E{i}", [128, 512], BF16) for i in range(4)]
        trib = P.al("trib", [128, 128], BF16)
        P.cp('pool', trib, tri)
        rden = P.al("rden", [128, 1])
        yh = [P.al(f"yh{i}", [128, 64], BF16) for i in range(2)]
        Ob = P.banks[0:4]
        Sbk = P.banks[4:8]
        nsb = 0
        for h in range(NH):
            kTh = kTq[h % 2]
            qTh = qTq[h % 2]
            P.dma(kTh[0:64], kT_d[h])
            P.dma(kTh[64:96], krT_d)
            P.dma(qTh, qT_d[h])
            for Q in range(NB):
                nkb = 4 * Q + 4
                qend = (Q + 1) * 512

                def qk(j):
                    nonlocal nsb
                    qlo = max(Q * 512, j * 128)
                    N = qend - qlo
                    bs = Sbk[nsb % 4]
                    E = Et[nsb % 4]
                    nsb += 1
                    P.mm(bs[:, 0:N], kTh[:, j * 128:(j + 1) * 128], qTh[:, qlo:qend])
                    P.act(E[:, 0:N], bs[:, 0:N], AF.Exp, scale=ATT_SCALE)
                    if j >= 4 * Q:
                        P.tt('pool', E[:, 0:128], E[:, 0:128], trib, ALU.mult)
                    return E, qlo

                def pv(j, E, qlo):
                    for t in range(max(4 * Q, j), 4 * Q + 4):
                        off = t * 128 - qlo
                        P.mm(Ob[t - 4 * Q][:, 0:65], E[:, off:off + 128], v1a[:, j, h * 65:(h + 1) * 65],
                             start=(j == 0), stop=(j == t))
                pq_ = [qk(0)]
                if nkb > 1:
                    pq_.append(qk(1))
                for j in range(nkb):
                    if j + 2 < nkb:
                        pq_.append(qk(j + 2))
                    pv(j, *pq_.pop(0))
                for tq in range(4):
                    t = 4 * Q + tq
                    yy = yh[tq % 2]
                    P.recip(rden, Ob[tq][:, 64:65])
                    P.ts('dve', yy, Ob[tq][:, 0:64], rden, None, ALU.mult)
                    P.dma(ytok_d[t * 128:(t + 1) * 128, 512 + h * 64:512 + (h + 1) * 64], yy)
        P.release(mL)
        if stop == f'p3_{l}':
            return True

        Wo = P.al("Wo", [128, 8, D], BF16)
        ost = P.al("ost", [128, D])
        for kc in range(8):
            P.dma(ost, wout_d[l, kc * 128:(kc + 1) * 128, :])
            P.cp('pool' if kc % 2 else 'dve', Wo[:, kc, :], ost)
        ytq = [P.al(f"yt{i}", [128, 4, D], BF16) for i in range(2)]
        gtq = [P.al(f"gt{i}", [128, 8, 512], BF16) for i in range(2)]
        xtq = [P.al(f"xt4{i}", [128, 8, 512]) for i in range(2)]
        yT = P.al("yT", [128, 8, 512], BF16)

        def p4_load(b):
            tsl = slice(b * 512, (b + 1) * 512)
            P.dma(ytq[b % 2], ytok_d[tsl].re("(q p) f -> p q f", p=128))
            P.dma(gtq[b % 2], gate_d.re("(c p) t -> p c t", p=128)[:, :, tsl])
            P.dma(xtq[b % 2], xT_v[:, :, tsl])
        p4_load(0)
        for b in range(NB):
            tsl = slice(b * 512, (b + 1) * 512)
            yt, gt, xt4 = ytq[b % 2], gtq[b % 2], xtq[b % 2]
            if b + 1 < NB:
                p4_load(b + 1)
            for f in range(8):
                bkt = P.bank().cast(BF16)
                for q in range(4):
                    P.tr(bkt[:, q * 128:(q + 1) * 128], yt[:, q, f * 128:(f + 1) * 128], identb)
                P.tt('dve', yT[:, f, :], bkt[:, 0:512], gt[:, f, :], ALU.mult)
            for dch in range(8):
                bk = P.bank()
                for f in range(8):
                    P.mm(bk, Wo[:, f, dch * 128:(dch + 1) * 128], yT[:, f, :], start=(f == 0), stop=(f == 7))
                P.stt(xt4[:, dch, :], bk, mods[:, l, 16 + dch:17 + dch], xt4[:, dch, :], ALU.mult, ALU.add)
            P.dma(xT_v[:, :, tsl], xt4)
        P.release(mL)
        return False

    try:
        for l in range(L):
            if layer(l):
                return finish()
    except _Stop:
        return finish()

    m0 = P.mark()
    fg = P.al("fg", [128, D])
    P.dma(fg, V(fing_d.ap.partition_broadcast(128), fing_d.buf))
    xtf = P.al("xtf", [128, 8, 512])
    xo = [P.al("xo0", [128, D]), P.al("xo1", [128, D])]
    junk = P.al("junkf", [128, D])
    ssq = [P.al("ssq0", [128, 1]), P.al("ssq1", [128, 1])]
    for b in range(NB):
        t0 = b * 512
        P.dma(xtf, xT_v[:, :, t0:t0 + 512])
        for j in range(4):
            bk0 = P.bank()
            bk1 = P.bank()
            for c in range(8):
                bk = bk0 if c < 4 else bk1
                P.tr(bk[:, (c % 4) * 128:(c % 4 + 1) * 128], xtf[:, c, j * 128:(j + 1) * 128], ident)
            o = xo[j % 2]
            s = ssq[j % 2]
            P.cp('dve', o[:, 0:512], bk0)
            P.cp('act', o[:, 512:1024], bk1)
            P.act(junk, o, AF.Square, accum=s)
            P.act(s, s, AF.Sqrt, bias=epsn, scale=1.0 / D)
            P.recip(s, s)
            P.stt(o, o, s, fg, ALU.mult, ALU.mult)
            P.dma(out_d[t0 + j * 128:t0 + (j + 1) * 128, :], o, final=True)
    P.release(m0)
    return finish()


def make_masku():
    i = np.arange(128)
    m = np.zeros((128, 7, 128), np.float32)
    for j in range(7):
        bs = 1 << j
        m[:, j, :] = ((i[:, None] // (2 * bs)) == (i[None, :] // (2 * bs))) & ((i[:, None] // bs) != (i[None, :] // bs))
    return m.reshape(128, 7 * 128).astype(ml_dtypes.bfloat16)


def host_layout(inputs, S, L):
    f = lambda a: np.ascontiguousarray(np.asarray(a, dtype=np.float32))

    def cols(v):
        v = np.asarray(v, np.float32)
        return v.reshape(-1, 128).T

    vecs = np.zeros((128, L, NV), np.float32)
    rows = np.zeros((L, 1024), np.float32)
    for l in range(L):
        vecs[:, l, VG:VG + 8] = cols(inputs['norm_g'][l])
        vecs[:, l, VB:VB + 24] = cols(inputs['b_ada'][l])
        vecs[:, l, VMU:VMU + 13] = cols(inputs['mu_shift'][l])
        if l > 0:
            vecs[64:96, l, VMUV] = np.asarray(inputs['mu_vmix'][l - 1], np.float32)
            vecs[:, l, VV0:VV0 + 4] = cols(inputs['v0'][l - 1])
        vecs[:, l, VW0:VW0 + 4] = cols(inputs['w0'][l])
        vecs[:, l, VA0:VA0 + 4] = cols(inputs['a0'][l])
        vecs[:, l, VKK:VKK + 4] = cols(inputs['k_k'][l])
        vecs[:, l, VKA:VKA + 4] = cols(inputs['k_a'][l])
        vecs[:, l, VRK:VRK + 4] = cols(np.asarray(inputs['r_k'][l]).reshape(-1))
        vecs[:, l, VQG:VQG + 3] = cols(inputs['q_norm_g'][l])
        vecs[:, l, VKVG:VKVG + 2] = cols(inputs['kv_norm_g'][l])
        rows[l, 0:512] = np.asarray(inputs['lnx_w'][l], np.float32)
        rows[l, 512:1024] = np.asarray(inputs['lnx_b'][l], np.float32)
    shared = {
        "consts": make_consts(), "vecs": vecs, "rows": rows, "masku": make_masku(),
        "final_g": f(inputs['final_g']).reshape(1, D),
        "w_ada": f(inputs['w_ada'])[:L], "w_in": f(inputs['w_in'])[:L],
        "w_vd": f(inputs['w_vmix_down'])[:max(L - 1, 1)],
        "w_dec": f(inputs['w_decay_up'])[:L], "w_icl": f(inputs['w_iclr_up'])[:L],
        "w_vup": f(inputs['w_vmix_up'])[:max(L - 1, 1)],
        "w_uq": f(inputs['w_uq'])[:L], "w_ukv": f(inputs['w_ukv'])[:L], "w_out": f(inputs['w_out'])[:L],
    }
    x = np.asarray(inputs['x'], np.float32)
    c = np.asarray(inputs['c'], np.float32)
    pos = np.asarray(inputs['positions']).astype(np.int32)
    B = x.shape[0]
    per = []
    for b in range(B):
        m = dict(shared)
        m["x"] = np.ascontiguousarray(x[b, :S])
        m["cT"] = np.ascontiguousarray(c[b].reshape(8, 128).T)
        m["pos"] = np.ascontiguousarray(pos[b, :S].reshape(1, S))
        per.append(m)
    return per


_NC_CACHE = {}


def kernel(**inputs):
    S, L = 4096, 4
    B = np.asarray(inputs['x']).shape[0]
    per = host_layout(inputs, S, L)
    if (S, L) not in _NC_CACHE:
        _NC_CACHE[(S, L)] = build(S, L)
    nc = _NC_CACHE[(S, L)]
    res = run_bass_kernel_spmd(nc, per, core_ids=list(range(B)))
    return np.stack([np.asarray(r["out"], np.float32) for r in res.results], axis=0)
```

```python
import numpy as np
import ml_dtypes
import concourse.bass as bass
import concourse.mybir as mybir
from concourse.bass_utils import run_bass_kernel_spmd
from contextlib import ExitStack

F32 = mybir.dt.float32
BF16 = mybir.dt.bfloat16
I32 = mybir.dt.int32
ALU = mybir.AluOpType
AF = mybir.ActivationFunctionType
AX = mybir.AxisListType

ENGS = ['pe', 'act', 'dve', 'pool', 'sp']
EPOCH = 20000
NDSEM = 24
DT_BYTES = {F32: 4, BF16: 2, I32: 4}


class Buf:
    __slots__ = ('name', 'last_w', 'readers', 'dma_ws', 'excl')

    def __init__(self, name, excl=False):
        self.name = name
        self.excl = excl
        self.last_w = None
        self.dma_ws = []
        self.readers = {}


class V:
    __slots__ = ('ap', 'buf')

    def __init__(self, ap, buf):
        self.ap = ap
        self.buf = buf

    def __getitem__(self, k):
        return V(self.ap[k], self.buf)

    def re(self, s, **kw):
        return V(self.ap.rearrange(s, **kw), self.buf)

    def bc(self, shape):
        return V(self.ap.to_broadcast(list(shape)), self.buf)

    def un(self, ax):
        return V(self.ap.unsqueeze(ax), self.buf)

    def cast(self, dt):
        return V(self.ap.bitcast(dt), self.buf)


class Op:
    __slots__ = ('eng', 'fn', 'deps', 'sig', 'is_dma', 'has_dep', 'gidx')

    def __init__(self, eng, fn, is_dma):
        self.eng = eng
        self.fn = fn
        self.is_dma = is_dma
        self.deps = []
        self.sig = None
        self.has_dep = False


class Prog:
    def __init__(self, nc, es, arena_bytes):
        self.nc = nc
        self.es = es
        self.ops = {e: [] for e in ENGS}
        self.n = 0
        self.final_dmas = []
        h = es.enter_context(nc.sbuf_tensor("arena", [128, arena_bytes // 2], BF16))
        self.arena = h[:]
        self.arena_bytes = arena_bytes
        self.live = []
        self.sp_ = 0
        self.banks = []
        for i in range(8):
            hb = es.enter_context(nc.psum_tensor(f"bank{i}", [128, 512], F32))
            self.banks.append(V(hb[:], Buf(f"bank{i}", excl=True)))
        self.bi = 0

    def sb(self, name, shape, dt=F32):
        h = self.es.enter_context(self.nc.sbuf_tensor("s_" + name, list(shape), dt))
        return V(h[:], Buf(name))

    def mark(self):
        return self.sp_

    def release(self, m):
        self.sp_ = m

    def al(self, name, shape, dt=F32):
        per = int(np.prod(shape[1:])) * DT_BYTES[dt]
        start = (self.sp_ + 63) // 64 * 64
        end = start + per
        assert end <= self.arena_bytes, (name, end, self.arena_bytes)
        self.sp_ = end
        b = Buf(name)
        keep = []
        for (s0, e0, ob) in self.live:
            if s0 < end and start < e0:
                cands = list(ob.readers.values()) + list(ob.dma_ws)
                if ob.last_w is not None:
                    cands.append(ob.last_w)
                for d in cands:
                    k = ('dma', id(d)) if d.is_dma else d.eng
                    if k not in b.readers or (not d.is_dma and b.readers[k].gidx < d.gidx):
                        b.readers[k] = d
                if not (start <= s0 and e0 <= end):
                    keep.append((s0, e0, ob))
            else:
                keep.append((s0, e0, ob))
        keep.append((start, end, b))
        self.live = keep
        ap = self.arena[0:shape[0], start // 2: end // 2]
        if dt != BF16:
            ap = ap.bitcast(dt)
        if len(shape) == 3:
            ap = ap.rearrange("p (a b) -> p a b", a=shape[1])
        elif len(shape) == 4:
            ap = ap.rearrange("p (a b c) -> p a b c", a=shape[1], b=shape[2])
        return V(ap, b)

    def bank(self):
        b = self.banks[self.bi % 8]
        self.bi += 1
        return b

    def dram(self, name, shape, dt, kind="Internal"):
        t = self.nc.dram_tensor(name, list(shape), dt, kind=kind)
        return V(t.ap(), Buf(name))

    def op(self, eng, fn, r=(), w=(), is_dma=False):
        o = Op(eng, fn, is_dma)
        o.gidx = self.n
        self.n += 1
        deps = {}

        def add(d):
            if d is None or d is o:
                return
            if d.is_dma:
                deps[('dma', id(d))] = d
            else:
                if d.eng == eng and eng == 'pe' and not is_dma:
                    return
                k = d.eng
                if k not in deps or deps[k].gidx < d.gidx:
                    deps[k] = d
        rb = [x.buf for x in r]
        wb = [x.buf for x in w]
        for b in rb:
            add(b.last_w)
            for d in b.dma_ws:
                add(d)
            if b.excl:
                for d in b.readers.values():
                    if d.eng != eng:
                        add(d)
        for b in wb:
            add(b.last_w)
            for d in b.readers.values():
                add(d)
            if not is_dma:
                for d in b.dma_ws:
                    add(d)
        for b in wb:
            if is_dma:
                if b.readers:
                    b.dma_ws = []
                    b.readers = {}
                b.dma_ws.append(o)
            else:
                b.last_w = o
                b.dma_ws = []
                b.readers = {}
        for b in rb:
            if b in wb:
                continue
            if is_dma:
                b.readers[('dma', id(o))] = o
            else:
                b.readers[eng] = o
        o.deps = list(deps.values())
        for d in o.deps:
            d.has_dep = True
        self.ops[eng].append(o)
        return o

    def dma(self, out, in_, eng='sp', final=False, **kw):
        o = self.op(eng, lambda e: e.dma_start(out=out.ap, in_=in_.ap, **kw), r=[in_], w=[out], is_dma=True)
        o.has_dep = True
        if final:
            self.final_dmas.append(o)
        return o

    def mm(self, out, lhsT, rhs, start=True, stop=True, **kw):
        return self.op('pe', lambda e: e.matmul(out.ap, lhsT.ap, rhs.ap, start=start, stop=stop, **kw),
                       r=[lhsT, rhs] + ([] if start else [out]), w=[out])

    def tr(self, out, in_, ident):
        return self.op('pe', lambda e: e.transpose(out.ap, in_.ap, ident.ap), r=[in_, ident], w=[out])

    def act(self, out, in_, func, bias=None, scale=None, accum=None):
        r = [in_]
        kw = {}
        if bias is not None:
            if isinstance(bias, V):
                r.append(bias)
                kw['bias'] = bias.ap
            else:
                kw['bias'] = float(bias)
        if scale is not None:
            if isinstance(scale, V):
                r.append(scale)
                kw['scale'] = scale.ap
            else:
                kw['scale'] = float(scale)
        w = [out]
        if accum is not None:
            kw['accum_out'] = accum.ap
            w.append(accum)
        return self.op('act', lambda e: e.activation(out.ap, in_.ap, func, **kw), r=r, w=w)

    def tt(self, eng, out, in0, in1, op):
        return self.op(eng, lambda e: e.tensor_tensor(out.ap, in0.ap, in1.ap, op), r=[in0, in1], w=[out])

    def ts(self, eng, out, in0, s1, s2=None, op0=ALU.mult, op1=None):
        r = [in0]
        a1 = s1.ap if isinstance(s1, V) else float(s1)
        if isinstance(s1, V):
            r.append(s1)
        a2 = None
        if s2 is not None:
            a2 = s2.ap if isinstance(s2, V) else float(s2)
            if isinstance(s2, V):
                r.append(s2)
        if op1 is None:
            return self.op(eng, lambda e: e.tensor_scalar(out.ap, in0.ap, a1, None, op0), r=r, w=[out])
        return self.op(eng, lambda e: e.tensor_scalar(out.ap, in0.ap, a1, a2, op0, op1), r=r, w=[out])

    def stt(self, out, in0, s, in1, op0, op1):
        r = [in0, in1]
        a = s.ap if isinstance(s, V) else float(s)
        if isinstance(s, V):
            r.append(s)
        return self.op('dve', lambda e: e.scalar_tensor_tensor(out.ap, in0.ap, a, in1.ap, op0, op1), r=r, w=[out])

    def cp(self, eng, out, in_):
        if eng == 'act':
            return self.op('act', lambda e: e.copy(out.ap, in_.ap), r=[in_], w=[out])
        return self.op(eng, lambda e: e.tensor_copy(out.ap, in_.ap), r=[in_], w=[out])

    def memset(self, eng, out, val):
        return self.op(eng, lambda e: e.memset(out.ap, val), r=[], w=[out])

    def scan(self, out, d0, d1, init, op0, op1):
        return self.op('dve', lambda e: e.tensor_tensor_scan(out.ap, d0.ap, d1.ap, float(init), op0, op1),
                       r=[d0, d1], w=[out])

    def recip(self, out, in_):
        return self.op('dve', lambda e: e.reciprocal(out.ap, in_.ap), r=[in_], w=[out])

    def reduce(self, out, in_, op=ALU.add):
        return self.op('dve', lambda e: e.tensor_reduce(out.ap, in_.ap, AX.X, op), r=[in_], w=[out])

    def emit(self):
        nc = self.nc
        sem_names = []
        for eng in ENGS:
            cnt = 0
            dcnt = 0
            for o in self.ops[eng]:
                if o.is_dma:
                    j = dcnt % NDSEM
                    name = f"d_{eng}_{j}"
                    o.sig = (name, 16 * (dcnt // NDSEM + 1), 16)
                    dcnt += 1
                elif o.has_dep:
                    ep = cnt // EPOCH
                    name = f"c_{eng}_{ep}"
                    o.sig = (name, cnt % EPOCH + 1, 1)
                    cnt += 1
                else:
                    continue
                if name not in sem_names:
                    sem_names.append(name)
        sems = {}
        for nm in sem_names:
            sems[nm] = self.es.enter_context(nc.semaphore(nm))
        self.nsem = len(sem_names)
        block = self.es.enter_context(nc.Block())
        prog = self

        def run(eng, e):
            known = {}
            for o in prog.ops[eng]:
                if o.is_dma:
                    nm, val, inc = o.sig
                    if val > 16 and known.get(nm, 0) < val - 16:
                        e.wait_ge(sems[nm], val - 16)
                        known[nm] = val - 16
                for d in o.deps:
                    nm, val, inc = d.sig
                    if known.get(nm, 0) < val:
                        e.wait_ge(sems[nm], val)
                        known[nm] = val
                inst = o.fn(e)
                if o.sig is not None:
                    nm, val, inc = o.sig
                    inst.then_inc(sems[nm], inc)
            if eng == 'sp':
                last = {}
                for en in ENGS:
                    for o in prog.ops[en]:
                        if o.is_dma:
                            nm, val, inc = o.sig
                            last[nm] = max(last.get(nm, 0), val)
                for nm, val in last.items():
                    if known.get(nm, 0) < val:
                        e.wait_ge(sems[nm], val)
                        known[nm] = val

        @block.tensor
        def _(e):
            run('pe', e)

        @block.scalar
        def _(e):
            run('act', e)

        @block.vector
        def _(e):
            run('dve', e)

        @block.gpsimd
        def _(e):
            run('pool', e)

        @block.sync
        def _(e):
            run('sp', e)


D = 1024
NKC = 8
NH = 8
NCH = 28
WCOLS = NCH * 128
C_DEC = float(np.exp(-0.5))
ATT_SCALE = float(96 ** -0.5)
NORM_EPS = 1e-6
GN_EPS = 64e-5
NV = 80
VG, VB, VMU, VMUV, VW0, VA0, VV0, VKK, VKA, VRK, VQG, VKVG = 0, 8, 32, 45, 46, 50, 54, 58, 62, 66, 70, 73
CI, CMT, CML, CTRI, CBO, CBC, CINV, CRST, NCC = 0, 128, 640, 768, 896, 1024, 1026, 1028, 1540
TWO_PI = float(2 * np.pi)
CW1 = 6.28125
CW2 = float(2 * np.pi - 6.28125)


def make_consts():
    c = np.zeros((128, NCC), np.float32)
    i = np.arange(128)
    c[:, CI:CI + 128] = np.eye(128)
    strict = (i[None, :] > i[:, None]).astype(np.float32)
    incl = (i[None, :] >= i[:, None]).astype(np.float32)
    c[:, CMT:CMT + 512] = np.concatenate([strict, incl, -strict, incl], axis=1)
    c[:, CML:CML + 128] = -(i[None, :] < i[:, None]).astype(np.float32)
    c[:, CTRI:CTRI + 128] = incl
    bo = np.zeros((128, 128), np.float32)
    bo[:64, :64] = 1
    bo[64:, 64:] = 1
    c[:, CBO:CBO + 128] = bo
    c[:64, CBC] = 1
    c[64:, CBC + 1] = 1
    inv = (10000.0 ** (-np.arange(0, 32, 2, dtype=np.float32) / 32)).astype(np.float32)
    c[:, CINV] = inv[i % 16]
    rs = np.ones(512, np.float32)
    rs[::128] = 0
    c[:, CRST:CRST + 512] = rs[None, :]
    return c


class _Stop(Exception):
    pass


def build(S, L, taps=(), stop=None):
    NB = S // 512
    NT = S // 128
    nc = bass.Bass("TRN2", target_bir_lowering=False)
    es = ExitStack()
    P = Prog(nc, es, arena_bytes=195 * 1024)
    tapset = set(taps)

    def din(name, shape, dt=F32):
        t = nc.dram_tensor(name, list(shape), dt, kind="ExternalInput")
        return V(t.ap(), Buf(name))

    def dscr(name, shape, dt):
        kind = "ExternalOutput" if name in tapset else "Internal"
        return P.dram(name, shape, dt, kind=kind)

    def finish():
        P.emit()
        es.close()
        return nc

    def chk(tag):
        if stop == tag:
            raise _Stop()

    x_d = din("x", [S, D])
    cT_d = din("cT", [128, 8])
    pos_d = din("pos", [1, S], I32)
    consts_d = din("consts", [128, NCC])
    vecs_d = din("vecs", [128, L, NV])
    rows_d = din("rows", [L, 1024])
    masku_d = din("masku", [128, 7 * 128], BF16)
    fing_d = din("final_g", [1, D])
    wada_d = din("w_ada", [L, D, 3 * D])
    win_d = din("w_in", [L, D, 3360])
    wvd_d = din("w_vd", [max(L - 1, 1), D, 32])
    wdec_d = din("w_dec", [L, 64, 512])
    wicl_d = din("w_icl", [L, 64, 512])
    wvup_d = din("w_vup", [max(L - 1, 1), 32, 512])
    wuq_d = din("w_uq", [L, 384, 768])
    wukv_d = din("w_ukv", [L, 256, 1024])
    wout_d = din("w_out", [L, D, D])
    out_d = P.dram("out", [S, D], F32, kind="ExternalOutput")

    xT_d = dscr("xT", [D, S], F32)
    cs_d = dscr("cs", [2, 128, S], F32)
    vfirst_d = dscr("vfirst", [512, S], F32)
    feat_d = dscr("feat", [NT, 64, NH, 4, 128], BF16)
    tokm_d = dscr("tokm", [S, 4, 512], BF16)
    gc_d = dscr("gc", [512, NT], F32)
    sbon_d = dscr("sbon", [S, NH], F32)
    gate_d = dscr("gate", [D, S], BF16)
    qT_d = dscr("qT", [NH, 96, S], BF16)
    kT_d = dscr("kT", [NH, 64, S], BF16)
    krT_d = dscr("krT", [32, S], BF16)
    v1_d = dscr("v1", [S, NH, 65], BF16)
    ytok_d = dscr("ytok", [S, D], BF16)
    xT_v = xT_d.re("(c p) t -> p c t", p=128)

    consts = P.sb("consts", [128, NCC])
    vecs = P.sb("vecs", [128, L, NV])
    mods = P.sb("mods", [128, L, 24])
    gs = P.sb("gs", [128, L, 8])
    omka = P.sb("omka", [128, L, 4])
    identb = P.sb("identb", [128, 128], BF16)
    onesb = P.sb("onesb", [128, 128], BF16)
    bonesb = P.sb("bonesb", [128, 128], BF16)
    bcolsb = P.sb("bcolsb", [128, 2], BF16)
    epsn = P.sb("epsn", [128, 1])
    epsg = P.sb("epsg", [128, 1])
    halfpi = P.sb("halfpi", [128, 1])
    masku = P.sb("masku", [128, 7, 128], BF16)
    sball = P.sb("sball", [128, NT, NH])
    gcall = P.sb("gcall", [128, 4, NT])
    ident = consts[:, CI:CI + 128]
    maskT = consts[:, CMT:CMT + 512]
    masklow = consts[:, CML:CML + 128]
    tri = consts[:, CTRI:CTRI + 128]
    invf = consts[:, CINV:CINV + 1]
    resetm = consts[:, CRST:CRST + 512]

    P.dma(consts, consts_d)
    P.dma(vecs, vecs_d)
    P.dma(masku, masku_d.re("p (j t) -> p j t", j=7))
    P.cp('dve', identb, ident)
    P.memset('pool', onesb, 1.0)
    P.memset('pool', epsn, NORM_EPS)
    P.memset('pool', epsg, GN_EPS)
    P.memset('pool', halfpi, float(np.pi / 2))
    P.cp('dve', bonesb, consts[:, CBO:CBO + 128])
    P.cp('dve', bcolsb, consts[:, CBC:CBC + 2])

    m0 = P.mark()
    cact = P.al("cact", [128, 8])
    P.dma(cact, cT_d)
    P.act(cact, cact, AF.Silu)
    wst_t = [P.al("wst0", [128, 8, 512]), P.al("wst1", [128, 8, 512])]
    mbank = P.bank()
    n = 0
    for l in range(L):
        for cg in range(6):
            wst = wst_t[n % 2]
            n += 1
            P.dma(wst, wada_d[l, :, cg * 512:(cg + 1) * 512].re("(kc p) n -> p kc n", p=128))
            for j in range(4):
                col = l * 24 + cg * 4 + j
                for kc in range(8):
                    P.mm(mbank[:, col:col + 1], wst[:, kc, j * 128:(j + 1) * 128], cact[:, kc:kc + 1],
                         start=(kc == 0), stop=(kc == 7))
    P.tt('dve', mods, mbank[:, 0:L * 24].re("p (l j) -> p l j", l=L), vecs[:, :, VB:VB + 24], ALU.add)
    P.stt(gs, mods[:, :, 8:16], 1.0, vecs[:, :, VG:VG + 8], ALU.add, ALU.mult)
    P.ts('dve', omka, vecs[:, :, VKA:VKA + 4], -1.0, 1.0, ALU.mult, ALU.add)

    xs = P.al("xs", [128, 4, D])
    xts = P.al("xts", [128, 8, 512])
    posi = P.al("posi", [128, 512], I32)
    ang = P.al("ang", [128, 512])
    rk_ = P.al("rk_", [128, 512])
    rki = P.al("rki", [128, 512], I32)
    rr = P.al("rr", [128, 512])
    mk = P.al("mk", [128, 512])
    for b in range(NB):
        t0 = b * 512
        P.dma(xs, x_d[t0:t0 + 512, :].re("(j p) d -> p j d", p=128))
        for c in range(8):
            bk = P.bank()
            for j in range(4):
                P.tr(bk[:, j * 128:(j + 1) * 128], xs[:, j, c * 128:(c + 1) * 128], ident)
            P.cp('act' if c % 2 else 'dve', xts[:, c, :], bk)
        P.dma(xT_v[:, :, t0:t0 + 512], xts)
        P.dma(posi, V(pos_d.ap[:, t0:t0 + 512].partition_broadcast(128), pos_d.buf))
        P.cp('pool', ang, posi)
        P.ts('dve', ang, ang, invf, None, ALU.mult)
        P.ts('dve', rk_, ang, 1.0 / TWO_PI, None, ALU.mult)
        P.cp('dve', rki, rk_)
        P.cp('dve', rk_, rki)
        P.stt(rr, rk_, -CW1, ang, ALU.mult, ALU.add)
        P.stt(rr, rk_, -CW2, rr, ALU.mult, ALU.add)
        P.ts('dve', mk, rr, float(np.pi), None, ALU.is_gt)
        P.stt(rr, mk, -TWO_PI, rr, ALU.mult, ALU.add)
        P.ts('dve', mk, rr, float(-np.pi), None, ALU.is_lt)
        P.stt(rr, mk, TWO_PI, rr, ALU.mult, ALU.add)
        P.ts('dve', rr, rr, float(np.pi), float(-np.pi), ALU.min, ALU.max)
        P.act(mk, rr, AF.Sin)
        P.dma(cs_d[1, :, t0:t0 + 512], mk)
        P.stt(rk_, rr, -1.0, rr, ALU.mult, ALU.max)
        P.act(ang, rk_, AF.Sin, bias=halfpi, scale=-1.0)
        P.dma(cs_d[0, :, t0:t0 + 512], ang)
    P.release(m0)
    if stop == 'p0':
        return finish()

    def layer(l):
        mL = P.mark()
        W = P.al("W", [128, 8, WCOLS], BF16)
        lora_up = P.al("lora_up", [128, 512], BF16)
        vmix_up = P.al("vmix_up", [128, 512], BF16)
        Wq = P.al("Wq", [128, 3, 1024], BF16)
        Wkv = P.al("Wkv", [128, 2, 1024], BF16)
        mW = P.mark()
        wstage = P.al("wstage", [128, WCOLS])
        P.memset('pool', wstage, 0.0)
        for kc in range(8):
            rows = slice(kc * 128, (kc + 1) * 128)
            P.dma(wstage[:, 0:2816], win_d[l, rows, 0:2816])
            P.dma(wstage[:, 2816:3328], win_d[l, rows, 2848:3360])
            P.dma(wstage[:, 3328:3360], win_d[l, rows, 2816:2848])
            if l > 0:
                P.dma(wstage[:, 3392:3424], wvd_d[l - 1, rows, :])
            P.dma(wstage[:, 3456:3472], win_d[l, rows, 2832:2848])
            P.dma(wstage[:, 3472:3488], win_d[l, rows, 2816:2832])
            P.cp('dve' if kc % 2 else 'pool', W[:, kc, :], wstage)
            P.ts('dve', W[:, kc, 3456:3472], W[:, kc, 3456:3472], -1.0, None, ALU.mult)
        lst = P.al("lst", [128, 512])
        P.dma(lst[0:64], wdec_d[l])
        P.dma(lst[64:128], wicl_d[l])
        P.cp('pool', lora_up, lst)
        if l > 0:
            vst = P.al("vst", [128, 512])
            P.dma(vst[64:96], wvup_d[l - 1])
            P.cp('pool', vmix_up[64:96], vst[64:96])
        qst = P.al("qst", [128, 1024])
        for kc in range(3):
            src = wuq_d[l, kc * 128:(kc + 1) * 128, :].re("p (h e) -> p h e", e=96)
            P.dma(qst[:, 0:512].re("p (h e) -> p h e", e=64), src[:, :, 0:64])
            P.dma(qst[:, 512:768].re("p (h e) -> p h e", e=32), src[:, :, 64:96])
            rot = qst[:, 768:1024].re("p (h e) -> p h e", e=32)
            P.dma(rot[:, :, 0:16], src[:, :, 80:96])
            P.dma(rot[:, :, 16:32], src[:, :, 64:80])
            P.ts('dve', Wq[:, kc, :], qst, vecs[:, l, VQG + kc:VQG + kc + 1], None, ALU.mult)
            wr = Wq[:, kc, 768:1024].re("p (h e) -> p h e", e=32)[:, :, 0:16]
            P.ts('dve', wr, wr, -1.0, None, ALU.mult)
        for kc in range(2):
            src = wukv_d[l, kc * 128:(kc + 1) * 128, :].re("p (h e) -> p h e", e=128)
            P.dma(qst[:, 0:512].re("p (h e) -> p h e", e=64), src[:, :, 0:64])
            P.dma(qst[:, 512:1024].re("p (h e) -> p h e", e=64), src[:, :, 64:128])
            P.ts('dve', Wkv[:, kc, :], qst, vecs[:, l, VKVG + kc:VKVG + kc + 1], None, ALU.mult)

        chk(f'w_{l}')
        P.release(mW)
        carry = {j: P.al(f"carry{j}", [128, 1]) for j in list(range(13)) + [26]}
        for j in carry:
            P.memset('pool', carry[j], 0.0)
        pe2 = [P.al("pe0", [128, 513]), P.al("pe1", [128, 513])]
        npe = [0]
        xt = P.al("xt", [128, 8, 512])
        sq = P.al("sq", [128, 8, 512], BF16)
        hT = P.al("hT", [128, 8, 512], BF16)
        rstd = P.al("rstd", [128, 512])
        dtmp = P.al("dtmp", [128, 512])
        lsh = P.al("lsh", [128, 512])
        u2 = [dtmp, lsh]
        lora_in = P.al("lora_in", [128, 512], BF16)
        vlo_in = P.al("vlo_in", [128, 512], BF16)
        csb = P.al("csb", [128, 2, 512])
        F_ = {nm: P.al(nm, [128, 512]) for nm in
              ("rs0", "ks0", "vs0", "rs1", "ks1", "vs1", "sg", "aa", "gv", "vf", "kk", "nrm", "bb", "Gs", "g1")}
        B_ = {nm: P.al(nm, [128, 512], BF16) for nm in
              ("kk2", "Rt", "KKt", "Kt", "Bt", "Kh", "Bh", "Vb", "rkb")}
        nb4 = P.al("nb4", [128, 4])
        gC4 = P.al("gC4", [128, 4])
        sb4 = P.al("sb4", [128, 4, 2])
        tmo = P.al("tmo", [128, 4, 4, 128], BF16)
        gst = [P.al("gst0", [128, 512], BF16), P.al("gst1", [128, 512], BF16)]
        cq = P.al("cq", [128, 3, 512], BF16)
        sqq = P.al("sqq", [128, 3, 512], BF16)
        rq, crs, srs, a1, a2, rkv = (P.al(nm, [128, 512]) for nm in ("rq", "crs", "srs", "a1", "a2", "rkv"))
        qn = [P.al("qn0", [128, 512], BF16), P.al("qn1", [128, 512], BF16)]
        qrp = [P.al("qrp0", [128, 512], BF16), P.al("qrp1", [128, 512], BF16)]
        ckv = P.al("ckv", [128, 2, 512], BF16)
        sqk = P.al("sqk", [128, 2, 512], BF16)
        rcol = P.al("rcol", [128, 1])
        v1s = P.al("v1s", [128, 4, NH, 65], BF16)
        krb = P.al("krb", [128, 512], BF16)
        P.memset('pool', v1s[:, :, :, 64:65], 1.0)

        for b in range(NB):
            t0 = b * 512
            tsl = slice(t0, t0 + 512)
            P.dma(xt, xT_v[:, :, tsl])
            P.dma(csb, cs_d[:, :, tsl].re("w p t -> p w t"))
            for c in range(8):
                if c % 2:
                    P.act(sq[:, c, :], xt[:, c, :], AF.Square)
                else:
                    P.tt('pool', sq[:, c, :], xt[:, c, :], xt[:, c, :], ALU.mult)
            bk = P.bank()
            for c in range(8):
                P.mm(bk, onesb, sq[:, c, :], start=(c == 0), stop=(c == 7))
            P.act(rstd, bk, AF.Sqrt, bias=epsn, scale=1.0 / D)
            P.recip(rstd, rstd)
            for c in range(8):
                u = u2[c % 2]
                P.tt('pool' if c % 2 else 'dve', u, xt[:, c, :], rstd, ALU.mult)
                P.act(hT[:, c, :], u, AF.Identity, bias=mods[:, l, c:c + 1], scale=gs[:, l, c:c + 1])

            def proj(j, M=128):
                bk = P.bank()
                for kc in range(8):
                    P.mm(bk[0:M, :], W[:, kc, j * 128:j * 128 + M], hT[:, kc, :], start=(kc == 0), stop=(kc == 7))
                return bk

            def shifted(j, out, mucol, rows=slice(0, 128)):
                bk = proj(j)
                pe = pe2[npe[0] % 2]
                npe[0] += 1
                P.cp('pool', pe[rows, 0:1], carry[j][rows, :])
                P.cp('act', pe[:, 1:513], bk)
                P.tt('pool', dtmp[rows], pe[rows, 0:512], pe[rows, 1:513], ALU.subtract)
                P.stt(out[rows], dtmp[rows], vecs[rows, l, mucol:mucol + 1], pe[rows, 1:513], ALU.mult, ALU.add)
                P.cp('pool', carry[j][rows, :], pe[rows, 512:513])
                return pe

            chk(f'h_{l}')
            shifted(12, lsh, VMU + 12)
            P.act(lora_in[0:64], lsh[0:64], AF.Tanh)
            P.cp('pool', lora_in[64:128], lsh[64:128])
            if l > 0:
                pe26 = shifted(26, lsh, VMUV, rows=slice(64, 96))
                P.cp('pool', vlo_in[64:96], lsh[64:96])
            else:
                bk26 = proj(26)
                pe26 = pe2[npe[0] % 2]
                npe[0] += 1
                P.cp('act', pe26[:, 1:513], bk26)
            bk27 = proj(27, M=32)
            P.tt('dve', a1[0:32], pe26[0:32, 1:513], csb[0:32, 0, :], ALU.mult)
            P.tt('dve', a2[0:32], bk27[0:32, :], csb[0:32, 1, :], ALU.mult)
            P.tt('pool', krb[0:32], a1[0:32], a2[0:32], ALU.add)
            P.dma(krT_d[:, tsl], krb[0:32])

            def proj_shift(hp_):
                shifted(hp_, F_["rs" + str(hp_ % 2)], VMU + hp_)
                yield
                shifted(4 + hp_, F_["ks" + str(hp_ % 2)], VMU + 4 + hp_)
                yield
                shifted(8 + hp_, F_["vs" + str(hp_ % 2)], VMU + 8 + hp_)
                yield

            def rw_gen():
                for hp in range(4):
                    rs, ks, vs = (F_[k + str(hp % 2)] for k in ("rs", "ks", "vs"))
                    sg, aa, gv, vf = (F_[k] for k in ("sg", "aa", "gv", "vf"))
                    kk, nrm, bb, Gs, g1 = (F_[k] for k in ("kk", "nrm", "bb", "Gs", "g1"))
                    kkn = kk
                    ff = nrm
                    kmod = nrm
                    gi = e1 = ge = ginv = gcr = g1
                    rk = rs
                    if hp == 0:
                        for _ in proj_shift(0):
                            yield
                    if hp + 1 < 4:
                        for _ in proj_shift(hp + 1):
                            yield
                    cols = slice(hp * 128, (hp + 1) * 128)
                    prow = slice(hp * 128, (hp + 1) * 128)
                    bkw = P.bank()
                    P.mm(bkw, lora_up[0:64, cols], lora_in[0:64, :])
                    P.act(sg, bkw, AF.Sigmoid, bias=vecs[:, l, VW0 + hp:VW0 + hp + 1])
                    bka = P.bank()
                    P.mm(bka, lora_up[64:128, cols], lora_in[64:128, :])
                    P.act(aa, bka, AF.Sigmoid, bias=vecs[:, l, VA0 + hp:VA0 + hp + 1])
                    yield
                    if l > 0:
                        bkv = P.bank()
                        P.mm(bkv, vmix_up[64:96, cols], vlo_in[64:96, :])
                        P.act(gv, bkv, AF.Sigmoid, bias=vecs[:, l, VV0 + hp:VV0 + hp + 1])
                        P.dma(vf, vfirst_d[prow, tsl])
                        P.tt('pool', vf, vf, vs, ALU.subtract)
                        P.tt('pool', vf, vf, gv, ALU.mult)
                        P.tt('pool', vs, vs, vf, ALU.add)
                    else:
                        P.dma(vfirst_d[prow, tsl], vs)
                    P.act(kk, ks, AF.Identity, scale=vecs[:, l, VKK + hp:VKK + hp + 1])
                    P.tt('dve', B_["kk2"], kk, kk, ALU.mult)
                    bks = P.bank()
                    P.mm(bks, bonesb, B_["kk2"])
                    P.act(nrm, bks, AF.Sqrt)
                    P.ts('dve', nrm, nrm, 1e-12, None, ALU.max)
                    P.recip(nrm, nrm)
                    yield
                    P.tt('dve', kkn, kk, nrm, ALU.mult)
                    P.ts('dve', ff, aa, vecs[:, l, VKA + hp:VKA + hp + 1], omka[:, l, hp:hp + 1], ALU.mult, ALU.add)
                    P.tt('dve', kmod, ff, ks, ALU.mult)
                    P.tt('dve', bb, kkn, aa, ALU.mult)
                    P.scan(Gs, resetm, sg, 0.0, ALU.mult, ALU.add)
                    yield
                    P.act(gi, Gs, AF.Exp, scale=-C_DEC)
                    P.tt('dve', B_["Rt"], rs, gi, ALU.mult)
                    P.tt('pool', e1, Gs, sg, ALU.subtract)
                    P.act(ge, e1, AF.Exp, scale=-C_DEC)
                    P.tt('dve', B_["KKt"], kkn, ge, ALU.mult)
                    yield
                    P.act(ginv, Gs, AF.Exp, scale=C_DEC)
                    P.tt('pool', B_["Kt"], kmod, ginv, ALU.mult)
                    P.tt('dve', B_["Bt"], bb, ginv, ALU.mult)
                    yield
                    GsC = Gs.re("p (q t) -> p q t", t=128)[:, :, 127]
                    P.ts('dve', nb4, GsC, -C_DEC, None, ALU.mult)
                    for q in range(4):
                        P.act(gcr[:, q * 128:(q + 1) * 128], Gs[:, q * 128:(q + 1) * 128], AF.Exp, scale=C_DEC,
                              bias=nb4[:, q:q + 1])
                    P.tt('dve', B_["Kh"], kmod, gcr, ALU.mult)
                    P.tt('pool', B_["Bh"], bb, gcr, ALU.mult)
                    yield
                    P.act(gcall[:, hp, b * 4:(b + 1) * 4], nb4, AF.Exp)
                    P.tt('pool', rk, rs, kmod, ALU.mult)
                    P.act(B_["rkb"], rk, AF.Identity, scale=vecs[:, l, VRK + hp:VRK + hp + 1])
                    bkb = P.bank()
                    for q in range(4):
                        P.mm(bkb[:, q * 2:(q + 1) * 2], B_["rkb"][:, q * 128:(q + 1) * 128], bcolsb)
                    P.cp('dve', sball[:, b * 4:(b + 1) * 4, 2 * hp:2 * hp + 2], bkb[:, 0:8].re("p (q h) -> p q h", h=2))
                    P.cp('pool', B_["Vb"], vs)
                    yield
                    for opi, nm in enumerate(("Kt", "Bt", "KKt", "Rt")):
                        for hh in range(2):
                            P.dma(feat_d[b * 4:(b + 1) * 4, :, 2 * hp + hh, opi, :].re("c k t -> k c t"),
                                  B_[nm][hh * 64:(hh + 1) * 64, :].re("k (c t) -> k c t", c=4))
                    for q in range(4):
                        bkt = P.bank().cast(BF16)
                        for opi, nm in enumerate(("KKt", "Vb", "Kh", "Bh")):
                            P.tr(bkt[:, opi * 128:(opi + 1) * 128], B_[nm][:, q * 128:(q + 1) * 128], identb)
                        P.cp('act' if q % 2 else 'dve', tmo[:, q, :, :], bkt[:, 0:512].re("p (o f) -> p o f", o=4))
                        yield
                    for q in range(4):
                        P.dma(tokm_d[t0 + q * 128:t0 + (q + 1) * 128, :, cols], tmo[:, q, :, :])

            def mla_gen():
                for gi_, j in enumerate([13, 14, 15, 16, 22, 23, 24, 25]):
                    bk = proj(j)
                    g = gst[gi_ % 2]
                    P.act(g, bk, AF.Silu)
                    P.dma(gate_d[gi_ * 128:(gi_ + 1) * 128, tsl], g)
                    yield
                for i in range(3):
                    bk = proj(17 + i)
                    P.cp('act', cq[:, i, :], bk)
                    P.act(sqq[:, i, :], bk, AF.Square)
                    yield
                bk = P.bank()
                for i in range(3):
                    P.mm(bk, onesb, sqq[:, i, :], start=(i == 0), stop=(i == 2))
                P.act(rq, bk, AF.Sqrt, bias=epsn, scale=1.0 / 384)
                P.recip(rq, rq)
                yield
                P.tt('pool', crs, csb[:, 0, :], rq, ALU.mult)
                P.tt('pool', srs, csb[:, 1, :], rq, ALU.mult)
                for i in range(4):
                    bk = P.bank()
                    for kc in range(3):
                        P.mm(bk, Wq[:, kc, i * 128:(i + 1) * 128], cq[:, kc, :], start=(kc == 0), stop=(kc == 2))
                    qq = qn[i % 2]
                    P.tt('dve', qq, bk, rq, ALU.mult)
                    for hh in range(2):
                        P.dma(qT_d[2 * i + hh, 0:64, tsl], qq[hh * 64:(hh + 1) * 64, :])
                    yield
                for i in range(2):
                    bk1 = P.bank()
                    for kc in range(3):
                        P.mm(bk1, Wq[:, kc, 512 + i * 128:512 + (i + 1) * 128], cq[:, kc, :], start=(kc == 0),
                             stop=(kc == 2))
                    bk2 = P.bank()
                    for kc in range(3):
                        P.mm(bk2, Wq[:, kc, 768 + i * 128:768 + (i + 1) * 128], cq[:, kc, :], start=(kc == 0),
                             stop=(kc == 2))
                    P.tt('dve', a1, bk1, crs, ALU.mult)
                    P.tt('dve', a2, bk2, srs, ALU.mult)
                    qq = qrp[i % 2]
                    P.tt('dve', qq, a1, a2, ALU.add)
                    for hh in range(4):
                        P.dma(qT_d[4 * i + hh, 64:96, tsl], qq[hh * 32:(hh + 1) * 32, :])
                    yield
                for i in range(2):
                    bk = proj(20 + i)
                    P.cp('act', ckv[:, i, :], bk)
                    P.act(sqk[:, i, :], bk, AF.Square)
                    yield
                bk = P.bank()
                for i in range(2):
                    P.mm(bk, onesb, sqk[:, i, :], start=(i == 0), stop=(i == 1))
                P.act(rkv, bk, AF.Sqrt, bias=epsn, scale=1.0 / 256)
                P.recip(rkv, rkv)
                for i in range(4):
                    bk = P.bank()
                    for kc in range(2):
                        P.mm(bk, Wkv[:, kc, i * 128:(i + 1) * 128], ckv[:, kc, :], start=(kc == 0), stop=(kc == 1))
                    qq = qn[i % 2]
                    P.tt('dve', qq, bk, rkv, ALU.mult)
                    for hh in range(2):
                        P.dma(kT_d[2 * i + hh, :, tsl], qq[hh * 64:(hh + 1) * 64, :])
                    yield
                for q in range(4):
                    qs = slice(q * 128, (q + 1) * 128)
                    bkc = P.bank()
                    for i in range(2):
                        P.mm(bkc[:, 0:1], sqk[:, i, qs], onesb[:, 0:1], start=(i == 0), stop=(i == 1))
                    P.act(rcol, bkc[:, 0:1], AF.Sqrt, bias=epsn, scale=1.0 / 256)
                    P.recip(rcol, rcol)
                    bkv = P.bank()
                    for i in range(2):
                        P.mm(bkv, ckv[:, i, qs], Wkv[:, i, 512:1024], start=(i == 0), stop=(i == 1))
                    P.ts('dve', v1s[:, q, :, 0:64], bkv.re("p (h e) -> p h e", e=64), rcol, None, ALU.mult)
                    yield
                for q in range(4):
                    P.dma(v1_d[t0 + q * 128:t0 + (q + 1) * 128], v1s[:, q, :, :])
            gens_ = [rw_gen(), mla_gen()]
            while gens_:
                nx_ = []
                for gg in gens_:
                    try:
                        next(gg)
                        nx_.append(gg)
                    except StopIteration:
                        pass
                gens_ = nx_
        for hp in range(4):
            P.dma(gc_d[hp * 128:(hp + 1) * 128, :], gcall[:, hp, :])
        if 'sbon' in tapset:
            P.dma(sbon_d.re("(c p) h -> p c h", p=128), sball)
        P.release(mL)
        if stop == f'p1_{l}':
            return True

        gcs = P.al("gcs", [64, NH, NT])
        P.dma(gcs, gc_d.re("(h k) c -> k h c", k=64))
        sbn = sball
        lnw = P.al("lnw", [128, 1024])
        P.dma(lnw, V(rows_d.ap[l:l + 1, :].partition_broadcast(128), rows_d.buf))
        lnw3 = lnw[:, 0:512].re("p (h e) -> p h e", e=64)
        lnb3 = lnw[:, 512:1024].re("p (h e) -> p h e", e=64)
        Sf = P.al("Sf", [64, NH, 64])
        Sb = P.al("Sb", [64, NH, 64], BF16)
        P.memset('pool', Sf, 0.0)
        P.memset('pool', Sb, 0.0)
        GL = 4

        def slot_tiles(i):
            d = {}
            d['F'] = P.al(f"F{i}", [64, NH, 4, 128], BF16)
            d['T'] = P.al(f"T{i}", [128, 4, 512], BF16)
            d['XK'] = P.al(f"XK{i}", [128, NH, 128], BF16)
            d['AT'] = P.al(f"AT{i}", [128, NH, 512], BF16)
            for nm in ('Nk', 'Dk', 'Wk', 'Y1', 'Dl'):
                d[nm] = [P.al(f"{nm}{i}_{g}", [128, 4, 128], BF16) for g in range(2)]
            d['MU'] = [P.al(f"MU{i}_{g}", [128, 4, 128], BF16) for g in range(2)]
            d['nPQ'] = P.al(f"nPQ{i}", [128, NH, 128], BF16)
            d['nZ'] = P.al(f"nZ{i}", [64, NH, 64], BF16)
            d['Psi'] = P.al(f"Psi{i}", [64, NH, 64])
            d['Ry'] = P.al(f"Ry{i}", [64, NH, 128], BF16)
            d['y'] = P.al(f"y{i}", [128, NH, 64])
            d['t1'] = d['y'][0:64]
            d['yc'] = P.al(f"yc{i}", [128, NH, 64])
            d['ysq'] = d['y']
            d['mean'] = P.al(f"mean{i}", [128, NH])
            d['var'] = P.al(f"var{i}", [128, NH])
            d['ybf'] = P.al(f"ybf{i}", [128, 512], BF16)
            return d
        slots = [slot_tiles(i) for i in range(GL)]

        def chunk_gen(c):
            d = slots[c % GL]
            F, T, XK, AT, Nk, Dk, Wk, Y1, Dl, MU = (d[k] for k in ('F', 'T', 'XK', 'AT', 'Nk', 'Dk', 'Wk', 'Y1', 'Dl', 'MU'))
            nPQ, nZ, Psi, Ry, t1, y, yc, ysq, mean, var, ybf = (d[k] for k in
                                                               ('nPQ', 'nZ', 'Psi', 'Ry', 't1', 'y', 'yc', 'ysq', 'mean', 'var', 'ybf'))
            crow = slice(c * 128, (c + 1) * 128)
            P.dma(F, feat_d[c])
            P.dma(T, tokm_d[crow])
            P.dma(XK[:, :, 0:64], tokm_d[crow, 0, :].re("p (h e) -> p h e", e=64))
            yield
            for h in range(NH):
                bk = P.bank()
                KR = F[:, h, 2:4, :].re("k o t -> k (o t)")
                P.mm(bk[:, 0:256], F[:, h, 0, :], KR)
                P.mm(bk[:, 256:512], F[:, h, 1, :], KR)
                P.tt('dve', AT[:, h, :], bk, maskT, ALU.mult)
                if h % 4 == 3:
                    yield
            for g in range(2):
                bkn = P.bank()
                for hh in range(4):
                    h = 4 * g + hh
                    P.mm(bkn[:, hh * 128:(hh + 1) * 128], F[:, h, 2, :], F[:, h, 1, :])
                P.tt('dve', Nk[g], bkn.re("p (h s) -> p h s", h=4), masklow.un(1).bc([128, 4, 128]), ALU.mult)
                m0b = masku[:, 0, :].un(1).bc([128, 4, 128])
                idb = identb.un(1).bc([128, 4, 128])
                P.tt('pool', Dk[g], Nk[g], m0b, ALU.mult)
                P.tt('dve', Dk[g], Dk[g], idb, ALU.add)
                P.tt('pool', Wk[g], AT[:, 4 * g:4 * g + 4, 256:384], m0b, ALU.mult)
                P.tt('dve', Wk[g], Wk[g], idb, ALU.add)
                yield
            for j in range(1, 7):
                bkYs = []
                for g in range(2):
                    P.tt('pool', MU[g], AT[:, 4 * g:4 * g + 4, 256:384], masku[:, j, :].un(1).bc([128, 4, 128]), ALU.mult)
                    bkY = P.bank()
                    for hh in range(4):
                        P.mm(bkY[:, hh * 128:(hh + 1) * 128], MU[g][:, hh, :], Dk[g][:, hh, :])
                    P.cp('act', Y1[g], bkY.re("p (h s) -> p h s", h=4))
                yield
                bkDs = []
                for g in range(2):
                    bkD = P.bank()
                    bkDs.append(bkD)
                    for hh in range(4):
                        P.mm(bkD[:, hh * 128:(hh + 1) * 128], Wk[g][:, hh, :], Y1[g][:, hh, :])
                    P.cp('act', Dl[g], bkD.re("p (h s) -> p h s", h=4))
                    if j < 6:
                        P.tt('dve', Dk[g], bkD.re("p (h s) -> p h s", h=4), Dk[g], ALU.add)
                yield
                for g in range(2):
                    bkT = P.bank().cast(BF16)
                    for hh in range(4):
                        P.tr(bkT[:, hh * 128:(hh + 1) * 128], Dl[g][:, hh, :], identb)
                    P.tt('dve', Wk[g], bkT[:, 0:512].re("p (h s) -> p h s", h=4), Wk[g], ALU.add)
                yield
            bkx = P.bank()
            for h in range(NH):
                hc = slice(h * 64, (h + 1) * 64)
                P.mm(bkx[:, hc], AT[:, h, 0:128], T[:, 1, hc])
            P.cp('act', XK[:, :, 64:128], bkx.re("p (h e) -> p h e", e=64))
            yield
            for g in range(2):
                bkp = P.bank()
                for hh in range(4):
                    h = 4 * g + hh
                    P.mm(bkp[:, hh * 128:(hh + 1) * 128], Wk[g][:, hh, :], XK[:, h, :])
                P.act(nPQ[:, 4 * g:4 * g + 4, :], bkp.re("p (h s) -> p h s", h=4), AF.Copy, scale=-1.0)
            yield
            bkz = P.bank()
            for h in range(NH):
                hc = slice(h * 64, (h + 1) * 64)
                P.mm(bkz[0:64, hc], nPQ[:, h, 0:64], T[:, 3, hc])
            P.cp('act', nZ, bkz[0:64, :].re("k (h e) -> k h e", e=64))
            bkpsi = P.bank()
            for h in range(NH):
                hc = slice(h * 64, (h + 1) * 64)
                P.mm(bkpsi[0:64, hc], T[:, 2, hc], T[:, 1, hc], start=True, stop=False)
                P.mm(bkpsi[0:64, hc], T[:, 3, hc], nPQ[:, h, 64:128], start=False, stop=True)
            P.cp('dve', Psi, bkpsi[0:64, :].re("k (h e) -> k h e", e=64))
            for g in range(2):
                bkr = P.bank()
                for hh in range(4):
                    h = 4 * g + hh
                    P.mm(bkr[0:64, hh * 128:(hh + 1) * 128], nPQ[:, h, 0:64], AT[:, h, 384:512])
                P.tt('dve', Ry[:, 4 * g:4 * g + 4, :], bkr[0:64, :].re("k (h t) -> k h t", h=4),
                     F[:, 4 * g:4 * g + 4, 3, :], ALU.add)
            yield
            bky = P.bank()
            for h in range(NH):
                hc = slice(h * 64, (h + 1) * 64)
                P.mm(bky[:, hc], AT[:, h, 128:256], T[:, 1, hc], start=True, stop=False)
                P.mm(bky[:, hc], AT[:, h, 384:512], nPQ[:, h, 64:128], start=False, stop=False)
                P.mm(bky[:, hc], Ry[:, h, :], Sb[:, h, :], start=False, stop=True)
            bku = P.bank()
            for h in range(NH):
                hc = slice(h * 64, (h + 1) * 64)
                P.mm(bku[0:64, hc], nZ[:, h, :], Sb[:, h, :])
            P.tt('pool', t1, Sf, gcs[:, :, c].un(2).bc([64, NH, 64]), ALU.mult)
            P.tt('pool', t1, t1, Psi, ALU.add)
            bku3 = bku[0:64, :].re("k (h e) -> k h e", e=64)
            P.tt('dve', Sb, bku3, t1, ALU.add)
            P.tt('dve', Sf, bku3, t1, ALU.add)
            yield
            P.cp('act', y, bky.re("p (h e) -> p h e", e=64))
            P.reduce(mean, y)
            P.ts('dve', mean, mean, 1.0 / 64, None, ALU.mult)
            P.tt('dve', yc, y, mean.un(2).bc([128, NH, 64]), ALU.subtract)
            P.tt('dve', ysq, yc, yc, ALU.mult)
            P.reduce(var, ysq)
            P.act(var, var, AF.Sqrt, bias=epsg, scale=1.0 / 64)
            P.recip(var, var)
            yield
            P.tt('pool', yc, yc, var.un(2).bc([128, NH, 64]), ALU.mult)
            P.tt('pool', yc, yc, lnw3, ALU.mult)
            P.tt('pool', yc, yc, lnb3, ALU.add)
            P.tt('pool', ysq, T[:, 1, :].re("p (h e) -> p h e", e=64), sbn[:, c, :].un(2).bc([128, NH, 64]), ALU.mult)
            P.tt('dve', ybf.re("p (h e) -> p h e", e=64), yc, ysq, ALU.add)
            P.dma(ytok_d[crow, 0:512], ybf)

        def run_lockstep(gens):
            live = list(gens)
            while live:
                nxt = []
                for gg in live:
                    try:
                        next(gg)
                        nxt.append(gg)
                    except StopIteration:
                        pass
                live = nxt
        for c0 in range(0, NT, GL):
            run_lockstep([chunk_gen(c) for c in range(c0, min(NT, c0 + GL))])
        P.release(mL)
        if stop == f'p2_{l}':
            return True

        v1a = P.al("v1a", [128, NT, NH * 65], BF16)
        P.dma(v1a, v1_d.re("(j p) h e -> p j (h e)", p=128))
        kTq = [P.al(f"kTh{i}", [96, S], BF16) for i in range(2)]
        qTq = [P.al(f"qTh{i}", [96, S], BF16) for i in range(2)]
        Et = [P.al(f"E{i}", [128, 512], BF16) for i in range(4)]
        trib = P.al("trib", [128, 128], BF16)
        P.cp('pool', trib, tri)
        rden = P.al("rden", [128, 1])
        yh = [P.al(f"yh{i}", [128, 64], BF16) for i in range(2)]
        Ob = P.banks[0:4]
        Sbk = P.banks[4:8]
        nsb = 0
        for h in range(NH):
            kTh = kTq[h % 2]
            qTh = qTq[h % 2]
            P.dma(kTh[0:64], kT_d[h])
            P.dma(kTh[64:96], krT_d)
            P.dma(qTh, qT_d[h])
            for Q in range(NB):
                nkb = 4 * Q + 4
                qend = (Q + 1) * 512

                def qk(j):
                    nonlocal nsb
                    qlo = max(Q * 512, j * 128)
                    N = qend - qlo
                    bs = Sbk[nsb % 4]
                    E = Et[nsb % 4]
                    nsb += 1
                    P.mm(bs[:, 0:N], kTh[:, j * 128:(j + 1) * 128], qTh[:, qlo:qend])
                    P.act(E[:, 0:N], bs[:, 0:N], AF.Exp, scale=ATT_SCALE)
                    if j >= 4 * Q:
                        P.tt('pool', E[:, 0:128], E[:, 0:128], trib, ALU.mult)
                    return E, qlo

                def pv(j, E, qlo):
                    for t in range(max(4 * Q, j), 4 * Q + 4):
                        off = t * 128 - qlo
                        P.mm(Ob[t - 4 * Q][:, 0:65], E[:, off:off + 128], v1a[:, j, h * 65:(h + 1) * 65],
                             start=(j == 0), stop=(j == t))
                pq_ = [qk(0)]
                if nkb > 1:
                    pq_.append(qk(1))
                for j in range(nkb):
                    if j + 2 < nkb:
                        pq_.append(qk(j + 2))
                    pv(j, *pq_.pop(0))
                for tq in range(4):
                    t = 4 * Q + tq
                    yy = yh[tq % 2]
                    P.recip(rden, Ob[tq][:, 64:65])
                    P.ts('dve', yy, Ob[tq][:, 0:64], rden, None, ALU.mult)
                    P.dma(ytok_d[t * 128:(t + 1) * 128, 512 + h * 64:512 + (h + 1) * 64], yy)
        P.release(mL)
        if stop == f'p3_{l}':
            return True

        Wo = P.al("Wo", [128, 8, D], BF16)
        ost = P.al("ost", [128, D])
        for kc in range(8):
            P.dma(ost, wout_d[l, kc * 128:(kc + 1) * 128, :])
            P.cp('pool' if kc % 2 else 'dve', Wo[:, kc, :], ost)
        ytq = [P.al(f"yt{i}", [128, 4, D], BF16) for i in range(2)]
        gtq = [P.al(f"gt{i}", [128, 8, 512], BF16) for i in range(2)]
        xtq = [P.al(f"xt4{i}", [128, 8, 512]) for i in range(2)]
        yT = P.al("yT", [128, 8, 512], BF16)

        def p4_load(b):
            tsl = slice(b * 512, (b + 1) * 512)
            P.dma(ytq[b % 2], ytok_d[tsl].re("(q p) f -> p q f", p=128))
            P.dma(gtq[b % 2], gate_d.re("(c p) t -> p c t", p=128)[:, :, tsl])
            P.dma(xtq[b % 2], xT_v[:, :, tsl])
        p4_load(0)
        for b in range(NB):
            tsl = slice(b * 512, (b + 1) * 512)
            yt, gt, xt4 = ytq[b % 2], gtq[b % 2], xtq[b % 2]
            if b + 1 < NB:
                p4_load(b + 1)
            for f in range(8):
                bkt = P.bank().cast(BF16)
                for q in range(4):
                    P.tr(bkt[:, q * 128:(q + 1) * 128], yt[:, q, f * 128:(f + 1) * 128], identb)
                P.tt('dve', yT[:, f, :], bkt[:, 0:512], gt[:, f, :], ALU.mult)
            for dch in range(8):
                bk = P.bank()
                for f in range(8):
                    P.mm(bk, Wo[:, f, dch * 128:(dch + 1) * 128], yT[:, f, :], start=(f == 0), stop=(f == 7))
                P.stt(xt4[:, dch, :], bk, mods[:, l, 16 + dch:17 + dch], xt4[:, dch, :], ALU.mult, ALU.add)
            P.dma(xT_v[:, :, tsl], xt4)
        P.release(mL)
        return False

    try:
        for l in range(L):
            if layer(l):
                return finish()
    except _Stop:
        return finish()

    m0 = P.mark()
    fg = P.al("fg", [128, D])
    P.dma(fg, V(fing_d.ap.partition_broadcast(128), fing_d.buf))
    xtf = P.al("xtf", [128, 8, 512])
    xo = [P.al("xo0", [128, D]), P.al("xo1", [128, D])]
    junk = P.al("junkf", [128, D])
    ssq = [P.al("ssq0", [128, 1]), P.al("ssq1", [128, 1])]
    for b in range(NB):
        t0 = b * 512
        P.dma(xtf, xT_v[:, :, t0:t0 + 512])
        for j in range(4):
            bk0 = P.bank()
            bk1 = P.bank()
            for c in range(8):
                bk = bk0 if c < 4 else bk1
                P.tr(bk[:, (c % 4) * 128:(c % 4 + 1) * 128], xtf[:, c, j * 128:(j + 1) * 128], ident)
            o = xo[j % 2]
            s = ssq[j % 2]
            P.cp('dve', o[:, 0:512], bk0)
            P.cp('act', o[:, 512:1024], bk1)
            P.act(junk, o, AF.Square, accum=s)
            P.act(s, s, AF.Sqrt, bias=epsn, scale=1.0 / D)
            P.recip(s, s)
            P.stt(o, o, s, fg, ALU.mult, ALU.mult)
            P.dma(out_d[t0 + j * 128:t0 + (j + 1) * 128, :], o, final=True)
    P.release(m0)
    return finish()


def make_masku():
    i = np.arange(128)
    m = np.zeros((128, 7, 128), np.float32)
    for j in range(7):
        bs = 1 << j
        m[:, j, :] = ((i[:, None] // (2 * bs)) == (i[None, :] // (2 * bs))) & ((i[:, None] // bs) != (i[None, :] // bs))
    return m.reshape(128, 7 * 128).astype(ml_dtypes.bfloat16)


def host_layout(inputs, S, L):
    f = lambda a: np.ascontiguousarray(np.asarray(a, dtype=np.float32))

    def cols(v):
        v = np.asarray(v, np.float32)
        return v.reshape(-1, 128).T

    vecs = np.zeros((128, L, NV), np.float32)
    rows = np.zeros((L, 1024), np.float32)
    for l in range(L):
        vecs[:, l, VG:VG + 8] = cols(inputs['norm_g'][l])
        vecs[:, l, VB:VB + 24] = cols(inputs['b_ada'][l])
        vecs[:, l, VMU:VMU + 13] = cols(inputs['mu_shift'][l])
        if l > 0:
            vecs[64:96, l, VMUV] = np.asarray(inputs['mu_vmix'][l - 1], np.float32)
            vecs[:, l, VV0:VV0 + 4] = cols(inputs['v0'][l - 1])
        vecs[:, l, VW0:VW0 + 4] = cols(inputs['w0'][l])
        vecs[:, l, VA0:VA0 + 4] = cols(inputs['a0'][l])
        vecs[:, l, VKK:VKK + 4] = cols(inputs['k_k'][l])
        vecs[:, l, VKA:VKA + 4] = cols(inputs['k_a'][l])
        vecs[:, l, VRK:VRK + 4] = cols(np.asarray(inputs['r_k'][l]).reshape(-1))
        vecs[:, l, VQG:VQG + 3] = cols(inputs['q_norm_g'][l])
        vecs[:, l, VKVG:VKVG + 2] = cols(inputs['kv_norm_g'][l])
        rows[l, 0:512] = np.asarray(inputs['lnx_w'][l], np.float32)
        rows[l, 512:1024] = np.asarray(inputs['lnx_b'][l], np.float32)
    shared = {
        "consts": make_consts(), "vecs": vecs, "rows": rows, "masku": make_masku(),
        "final_g": f(inputs['final_g']).reshape(1, D),
        "w_ada": f(inputs['w_ada'])[:L], "w_in": f(inputs['w_in'])[:L],
        "w_vd": f(inputs['w_vmix_down'])[:max(L - 1, 1)],
        "w_dec": f(inputs['w_decay_up'])[:L], "w_icl": f(inputs['w_iclr_up'])[:L],
        "w_vup": f(inputs['w_vmix_up'])[:max(L - 1, 1)],
        "w_uq": f(inputs['w_uq'])[:L], "w_ukv": f(inputs['w_ukv'])[:L], "w_out": f(inputs['w_out'])[:L],
    }
    x = np.asarray(inputs['x'], np.float32)
    c = np.asarray(inputs['c'], np.float32)
    pos = np.asarray(inputs['positions']).astype(np.int32)
    B = x.shape[0]
    per = []
    for b in range(B):
        m = dict(shared)
        m["x"] = np.ascontiguousarray(x[b, :S])
        m["cT"] = np.ascontiguousarray(c[b].reshape(8, 128).T)
        m["pos"] = np.ascontiguousarray(pos[b, :S].reshape(1, S))
        per.append(m)
    return per


_NC_CACHE = {}


def kernel(**inputs):
    S, L = 4096, 4
    B = np.asarray(inputs['x']).shape[0]
    per = host_layout(inputs, S, L)
    if (S, L) not in _NC_CACHE:
        _NC_CACHE[(S, L)] = build(S, L)
    nc = _NC_CACHE[(S, L)]
    res = run_bass_kernel_spmd(nc, per, core_ids=list(range(B)))
    return np.stack([np.asarray(r["out"], np.float32) for r in res.results], axis=0)
```

```python
import numpy as np
import ml_dtypes
import concourse.bass as bass
import concourse.mybir as mybir
from concourse.bass_utils import run_bass_kernel_spmd
from contextlib import ExitStack

F32 = mybir.dt.float32
BF16 = mybir.dt.bfloat16
I32 = mybir.dt.int32
ALU = mybir.AluOpType
AF = mybir.ActivationFunctionType
AX = mybir.AxisListType

ENGS = ['pe', 'act', 'dve', 'pool', 'sp']
EPOCH = 20000
NDSEM = 24
DT_BYTES = {F32: 4, BF16: 2, I32: 4}


class Buf:
    __slots__ = ('name', 'last_w', 'readers', 'dma_ws', 'excl')

    def __init__(self, name, excl=False):
        self.name = name
        self.excl = excl
        self.last_w = None
        self.dma_ws = []
        self.readers = {}


class V:
    __slots__ = ('ap', 'buf')

    def __init__(self, ap, buf):
        self.ap = ap
        self.buf = buf

    def __getitem__(self, k):
        return V(self.ap[k], self.buf)

    def re(self, s, **kw):
        return V(self.ap.rearrange(s, **kw), self.buf)

    def bc(self, shape):
        return V(self.ap.to_broadcast(list(shape)), self.buf)

    def un(self, ax):
        return V(self.ap.unsqueeze(ax), self.buf)

    def cast(self, dt):
        return V(self.ap.bitcast(dt), self.buf)


class Op:
    __slots__ = ('eng', 'fn', 'deps', 'sig', 'is_dma', 'has_dep', 'gidx')

    def __init__(self, eng, fn, is_dma):
        self.eng = eng
        self.fn = fn
        self.is_dma = is_dma
        self.deps = []
        self.sig = None
        self.has_dep = False


class Prog:
    def __init__(self, nc, es, arena_bytes):
        self.nc = nc
        self.es = es
        self.ops = {e: [] for e in ENGS}
        self.n = 0
        self.final_dmas = []
        h = es.enter_context(nc.sbuf_tensor("arena", [128, arena_bytes // 2], BF16))
        self.arena = h[:]
        self.arena_bytes = arena_bytes
        self.live = []
        self.sp_ = 0
        self.banks = []
        for i in range(8):
            hb = es.enter_context(nc.psum_tensor(f"bank{i}", [128, 512], F32))
            self.banks.append(V(hb[:], Buf(f"bank{i}", excl=True)))
        self.bi = 0

    def sb(self, name, shape, dt=F32):
        h = self.es.enter_context(self.nc.sbuf_tensor("s_" + name, list(shape), dt))
        return V(h[:], Buf(name))

    def mark(self):
        return self.sp_

    def release(self, m):
        self.sp_ = m

    def al(self, name, shape, dt=F32):
        per = int(np.prod(shape[1:])) * DT_BYTES[dt]
        start = (self.sp_ + 63) // 64 * 64
        end = start + per
        assert end <= self.arena_bytes, (name, end, self.arena_bytes)
        self.sp_ = end
        b = Buf(name)
        keep = []
        for (s0, e0, ob) in self.live:
            if s0 < end and start < e0:
                cands = list(ob.readers.values()) + list(ob.dma_ws)
                if ob.last_w is not None:
                    cands.append(ob.last_w)
                for d in cands:
                    k = ('dma', id(d)) if d.is_dma else d.eng
                    if k not in b.readers or (not d.is_dma and b.readers[k].gidx < d.gidx):
                        b.readers[k] = d
                if not (start <= s0 and e0 <= end):
                    keep.append((s0, e0, ob))
            else:
                keep.append((s0, e0, ob))
        keep.append((start, end, b))
        self.live = keep
        ap = self.arena[0:shape[0], start // 2: end // 2]
        if dt != BF16:
            ap = ap.bitcast(dt)
        if len(shape) == 3:
            ap = ap.rearrange("p (a b) -> p a b", a=shape[1])
        elif len(shape) == 4:
            ap = ap.rearrange("p (a b c) -> p a b c", a=shape[1], b=shape[2])
        return V(ap, b)

    def bank(self):
        b = self.banks[self.bi % 8]
        self.bi += 1
        return b

    def dram(self, name, shape, dt, kind="Internal"):
        t = self.nc.dram_tensor(name, list(shape), dt, kind=kind)
        return V(t.ap(), Buf(name))

    def op(self, eng, fn, r=(), w=(), is_dma=False):
        o = Op(eng, fn, is_dma)
        o.gidx = self.n
        self.n += 1
        deps = {}

        def add(d):
            if d is None or d is o:
                return
            if d.is_dma:
                deps[('dma', id(d))] = d
            else:
                if d.eng == eng and eng == 'pe' and not is_dma:
                    return
                k = d.eng
                if k not in deps or deps[k].gidx < d.gidx:
                    deps[k] = d
        rb = [x.buf for x in r]
        wb = [x.buf for x in w]
        for b in rb:
            add(b.last_w)
            for d in b.dma_ws:
                add(d)
            if b.excl:
                for d in b.readers.values():
                    if d.eng != eng:
                        add(d)
        for b in wb:
            add(b.last_w)
            for d in b.readers.values():
                add(d)
            if not is_dma:
                for d in b.dma_ws:
                    add(d)
        for b in wb:
            if is_dma:
                if b.readers:
                    b.dma_ws = []
                    b.readers = {}
                b.dma_ws.append(o)
            else:
                b.last_w = o
                b.dma_ws = []
                b.readers = {}
        for b in rb:
            if b in wb:
                continue
            if is_dma:
                b.readers[('dma', id(o))] = o
            else:
                b.readers[eng] = o
        o.deps = list(deps.values())
        for d in o.deps:
            d.has_dep = True
        self.ops[eng].append(o)
        return o

    def dma(self, out, in_, eng='sp', final=False, **kw):
        o = self.op(eng, lambda e: e.dma_start(out=out.ap, in_=in_.ap, **kw), r=[in_], w=[out], is_dma=True)
        o.has_dep = True
        if final:
            self.final_dmas.append(o)
        return o

    def mm(self, out, lhsT, rhs, start=True, stop=True, **kw):
        return self.op('pe', lambda e: e.matmul(out.ap, lhsT.ap, rhs.ap, start=start, stop=stop, **kw),
                       r=[lhsT, rhs] + ([] if start else [out]), w=[out])

    def tr(self, out, in_, ident):
        return self.op('pe', lambda e: e.transpose(out.ap, in_.ap, ident.ap), r=[in_, ident], w=[out])

    def act(self, out, in_, func, bias=None, scale=None, accum=None):
        r = [in_]
        kw = {}
        if bias is not None:
            if isinstance(bias, V):
                r.append(bias)
                kw['bias'] = bias.ap
            else:
                kw['bias'] = float(bias)
        if scale is not None:
            if isinstance(scale, V):
                r.append(scale)
                kw['scale'] = scale.ap
            else:
                kw['scale'] = float(scale)
        w = [out]
        if accum is not None:
            kw['accum_out'] = accum.ap
            w.append(accum)
        return self.op('act', lambda e: e.activation(out.ap, in_.ap, func, **kw), r=r, w=w)

    def tt(self, eng, out, in0, in1, op):
        return self.op(eng, lambda e: e.tensor_tensor(out.ap, in0.ap, in1.ap, op), r=[in0, in1], w=[out])

    def ts(self, eng, out, in0, s1, s2=None, op0=ALU.mult, op1=None):
        r = [in0]
        a1 = s1.ap if isinstance(s1, V) else float(s1)
        if isinstance(s1, V):
            r.append(s1)
        a2 = None
        if s2 is not None:
            a2 = s2.ap if isinstance(s2, V) else float(s2)
            if isinstance(s2, V):
                r.append(s2)
        if op1 is None:
            return self.op(eng, lambda e: e.tensor_scalar(out.ap, in0.ap, a1, None, op0), r=r, w=[out])
        return self.op(eng, lambda e: e.tensor_scalar(out.ap, in0.ap, a1, a2, op0, op1), r=r, w=[out])

    def stt(self, out, in0, s, in1, op0, op1):
        r = [in0, in1]
        a = s.ap if isinstance(s, V) else float(s)
        if isinstance(s, V):
            r.append(s)
        return self.op('dve', lambda e: e.scalar_tensor_tensor(out.ap, in0.ap, a, in1.ap, op0, op1), r=r, w=[out])

    def cp(self, eng, out, in_):
        if eng == 'act':
            return self.op('act', lambda e: e.copy(out.ap, in_.ap), r=[in_], w=[out])
        return self.op(eng, lambda e: e.tensor_copy(out.ap, in_.ap), r=[in_], w=[out])

    def memset(self, eng, out, val):
        return self.op(eng, lambda e: e.memset(out.ap, val), r=[], w=[out])

    def scan(self, out, d0, d1, init, op0, op1):
        return self.op('dve', lambda e: e.tensor_tensor_scan(out.ap, d0.ap, d1.ap, float(init), op0, op1),
                       r=[d0, d1], w=[out])

    def recip(self, out, in_):
        return self.op('dve', lambda e: e.reciprocal(out.ap, in_.ap), r=[in_], w=[out])

    def reduce(self, out, in_, op=ALU.add):
        return self.op('dve', lambda e: e.tensor_reduce(out.ap, in_.ap, AX.X, op), r=[in_], w=[out])

    def emit(self):
        nc = self.nc
        sem_names = []
        for eng in ENGS:
            cnt = 0
            dcnt = 0
            for o in self.ops[eng]:
                if o.is_dma:
                    j = dcnt % NDSEM
                    name = f"d_{eng}_{j}"
                    o.sig = (name, 16 * (dcnt // NDSEM + 1), 16)
                    dcnt += 1
                elif o.has_dep:
                    ep = cnt // EPOCH
                    name = f"c_{eng}_{ep}"
                    o.sig = (name, cnt % EPOCH + 1, 1)
                    cnt += 1
                else:
                    continue
                if name not in sem_names:
                    sem_names.append(name)
        sems = {}
        for nm in sem_names:
            sems[nm] = self.es.enter_context(nc.semaphore(nm))
        self.nsem = len(sem_names)
        block = self.es.enter_context(nc.Block())
        prog = self

        def run(eng, e):
            known = {}
            for o in prog.ops[eng]:
                if o.is_dma:
                    nm, val, inc = o.sig
                    if val > 16 and known.get(nm, 0) < val - 16:
                        e.wait_ge(sems[nm], val - 16)
                        known[nm] = val - 16
                for d in o.deps:
                    nm, val, inc = d.sig
                    if known.get(nm, 0) < val:
                        e.wait_ge(sems[nm], val)
                        known[nm] = val
                inst = o.fn(e)
                if o.sig is not None:
                    nm, val, inc = o.sig
                    inst.then_inc(sems[nm], inc)
            if eng == 'sp':
                last = {}
                for en in ENGS:
                    for o in prog.ops[en]:
                        if o.is_dma:
                            nm, val, inc = o.sig
                            last[nm] = max(last.get(nm, 0), val)
                for nm, val in last.items():
                    if known.get(nm, 0) < val:
                        e.wait_ge(sems[nm], val)
                        known[nm] = val

        @block.tensor
        def _(e):
            run('pe', e)

        @block.scalar
        def _(e):
            run('act', e)

        @block.vector
        def _(e):
            run('dve', e)

        @block.gpsimd
        def _(e):
            run('pool', e)

        @block.sync
        def _(e):
            run('sp', e)


D = 1024
NKC = 8
NH = 8
NCH = 28
WCOLS = NCH * 128
C_DEC = float(np.exp(-0.5))
ATT_SCALE = float(96 ** -0.5)
NORM_EPS = 1e-6
GN_EPS = 64e-5
NV = 80
VG, VB, VMU, VMUV, VW0, VA0, VV0, VKK, VKA, VRK, VQG, VKVG = 0, 8, 32, 45, 46, 50, 54, 58, 62, 66, 70, 73
CI, CMT, CML, CTRI, CBO, CBC, CINV, CRST, NCC = 0, 128, 640, 768, 896, 1024, 1026, 1028, 1540
TWO_PI = float(2 * np.pi)
CW1 = 6.28125
CW2 = float(2 * np.pi - 6.28125)


def make_consts():
    c = np.zeros((128, NCC), np.float32)
    i = np.arange(128)
    c[:, CI:CI + 128] = np.eye(128)
    strict = (i[None, :] > i[:, None]).astype(np.float32)
    incl = (i[None, :] >= i[:, None]).astype(np.float32)
    c[:, CMT:CMT + 512] = np.concatenate([strict, incl, -strict, incl], axis=1)
    c[:, CML:CML + 128] = -(i[None, :] < i[:, None]).astype(np.float32)
    c[:, CTRI:CTRI + 128] = incl
    bo = np.zeros((128, 128), np.float32)
    bo[:64, :64] = 1
    bo[64:, 64:] = 1
    c[:, CBO:CBO + 128] = bo
    c[:64, CBC] = 1
    c[64:, CBC + 1] = 1
    inv = (10000.0 ** (-np.arange(0, 32, 2, dtype=np.float32) / 32)).astype(np.float32)
    c[:, CINV] = inv[i % 16]
    rs = np.ones(512, np.float32)
    rs[::128] = 0
    c[:, CRST:CRST + 512] = rs[None, :]
    return c


class _Stop(Exception):
    pass


def build(S, L, taps=(), stop=None):
    NB = S // 512
    NT = S // 128
    nc = bass.Bass("TRN2", target_bir_lowering=False)
    es = ExitStack()
    P = Prog(nc, es, arena_bytes=195 * 1024)
    tapset = set(taps)

    def din(name, shape, dt=F32):
        t = nc.dram_tensor(name, list(shape), dt, kind="ExternalInput")
        return V(t.ap(), Buf(name))

    def dscr(name, shape, dt):
        kind = "ExternalOutput" if name in tapset else "Internal"
        return P.dram(name, shape, dt, kind=kind)

    def finish():
        P.emit()
        es.close()
        return nc

    def chk(tag):
        if stop == tag:
            raise _Stop()

    x_d = din("x", [S, D])
    cT_d = din("cT", [128, 8])
    pos_d = din("pos", [1, S], I32)
    consts_d = din("consts", [128, NCC])
    vecs_d = din("vecs", [128, L, NV])
    rows_d = din("rows", [L, 1024])
    masku_d = din("masku", [128, 7 * 128], BF16)
    fing_d = din("final_g", [1, D])
    wada_d = din("w_ada", [L, D, 3 * D])
    win_d = din("w_in", [L, D, 3360])
    wvd_d = din("w_vd", [max(L - 1, 1), D, 32])
    wdec_d = din("w_dec", [L, 64, 512])
    wicl_d = din("w_icl", [L, 64, 512])
    wvup_d = din("w_vup", [max(L - 1, 1), 32, 512])
    wuq_d = din("w_uq", [L, 384, 768])
    wukv_d = din("w_ukv", [L, 256, 1024])
    wout_d = din("w_out", [L, D, D])
    out_d = P.dram("out", [S, D], F32, kind="ExternalOutput")

    xT_d = dscr("xT", [D, S], F32)
    cs_d = dscr("cs", [2, 128, S], F32)
    vfirst_d = dscr("vfirst", [512, S], F32)
    feat_d = dscr("feat", [NT, 64, NH, 4, 128], BF16)
    tokm_d = dscr("tokm", [S, 4, 512], BF16)
    gc_d = dscr("gc", [512, NT], F32)
    sbon_d = dscr("sbon", [S, NH], F32)
    gate_d = dscr("gate", [D, S], BF16)
    qT_d = dscr("qT", [NH, 96, S], BF16)
    kT_d = dscr("kT", [NH, 64, S], BF16)
    krT_d = dscr("krT", [32, S], BF16)
    v1_d = dscr("v1", [S, NH, 65], BF16)
    ytok_d = dscr("ytok", [S, D], BF16)
    xT_v = xT_d.re("(c p) t -> p c t", p=128)

    consts = P.sb("consts", [128, NCC])
    vecs = P.sb("vecs", [128, L, NV])
    mods = P.sb("mods", [128, L, 24])
    gs = P.sb("gs", [128, L, 8])
    omka = P.sb("omka", [128, L, 4])
    identb = P.sb("identb", [128, 128], BF16)
    onesb = P.sb("onesb", [128, 128], BF16)
    bonesb = P.sb("bonesb", [128, 128], BF16)
    bcolsb = P.sb("bcolsb", [128, 2], BF16)
    epsn = P.sb("epsn", [128, 1])
    epsg = P.sb("epsg", [128, 1])
    halfpi = P.sb("halfpi", [128, 1])
    eps24 = P.sb("eps24", [128, 1])
    masku = P.sb("masku", [128, 7, 128], BF16)
    sball = P.sb("sball", [128, NT, NH])
    gcall = P.sb("gcall", [128, 4, NT])
    ident = consts[:, CI:CI + 128]
    maskT = consts[:, CMT:CMT + 512]
    masklow = consts[:, CML:CML + 128]
    tri = consts[:, CTRI:CTRI + 128]
    invf = consts[:, CINV:CINV + 1]
    resetm = consts[:, CRST:CRST + 512]

    P.dma(consts, consts_d)
    P.dma(vecs, vecs_d)
    P.dma(masku, masku_d.re("p (j t) -> p j t", j=7))
    P.cp('dve', identb, ident)
    P.memset('pool', onesb, 1.0)
    P.memset('pool', epsn, NORM_EPS)
    P.memset('pool', epsg, GN_EPS)
    P.memset('pool', halfpi, float(np.pi / 2))
    P.memset('pool', eps24, 1e-24)
    P.cp('dve', bonesb, consts[:, CBO:CBO + 128])
    P.cp('dve', bcolsb, consts[:, CBC:CBC + 2])

    m0 = P.mark()
    cact = P.al("cact", [128, 8])
    P.dma(cact, cT_d)
    P.act(cact, cact, AF.Silu)
    wst_t = [P.al("wst0", [128, 8, 512]), P.al("wst1", [128, 8, 512])]
    mbank = P.bank()
    n = 0
    for l in range(L):
        for cg in range(6):
            wst = wst_t[n % 2]
            n += 1
            P.dma(wst, wada_d[l, :, cg * 512:(cg + 1) * 512].re("(kc p) n -> p kc n", p=128))
            for j in range(4):
                col = l * 24 + cg * 4 + j
                for kc in range(8):
                    P.mm(mbank[:, col:col + 1], wst[:, kc, j * 128:(j + 1) * 128], cact[:, kc:kc + 1],
                         start=(kc == 0), stop=(kc == 7))
    P.tt('dve', mods, mbank[:, 0:L * 24].re("p (l j) -> p l j", l=L), vecs[:, :, VB:VB + 24], ALU.add)
    P.stt(gs, mods[:, :, 8:16], 1.0, vecs[:, :, VG:VG + 8], ALU.add, ALU.mult)
    P.ts('dve', omka, vecs[:, :, VKA:VKA + 4], -1.0, 1.0, ALU.mult, ALU.add)

    xs = P.al("xs", [128, 4, D])
    xts = P.al("xts", [128, 8, 512])
    posi = P.al("posi", [128, 512], I32)
    ang = P.al("ang", [128, 512])
    rk_ = P.al("rk_", [128, 512])
    rki = P.al("rki", [128, 512], I32)
    rr = P.al("rr", [128, 512])
    mk = P.al("mk", [128, 512])
    for b in range(NB):
        t0 = b * 512
        P.dma(xs, x_d[t0:t0 + 512, :].re("(j p) d -> p j d", p=128))
        for c in range(8):
            bk = P.bank()
            for j in range(4):
                P.tr(bk[:, j * 128:(j + 1) * 128], xs[:, j, c * 128:(c + 1) * 128], ident)
            P.cp('act' if c % 2 else 'dve', xts[:, c, :], bk)
        P.dma(xT_v[:, :, t0:t0 + 512], xts)
        P.dma(posi, V(pos_d.ap[:, t0:t0 + 512].partition_broadcast(128), pos_d.buf))
        P.cp('pool', ang, posi)
        P.ts('dve', ang, ang, invf, None, ALU.mult)
        P.ts('dve', rk_, ang, 1.0 / TWO_PI, None, ALU.mult)
        P.cp('dve', rki, rk_)
        P.cp('dve', rk_, rki)
        P.stt(rr, rk_, -CW1, ang, ALU.mult, ALU.add)
        P.stt(rr, rk_, -CW2, rr, ALU.mult, ALU.add)
        P.ts('dve', mk, rr, float(np.pi), None, ALU.is_gt)
        P.stt(rr, mk, -TWO_PI, rr, ALU.mult, ALU.add)
        P.ts('dve', mk, rr, float(-np.pi), None, ALU.is_lt)
        P.stt(rr, mk, TWO_PI, rr, ALU.mult, ALU.add)
        P.ts('dve', rr, rr, float(np.pi), float(-np.pi), ALU.min, ALU.max)
        P.act(mk, rr, AF.Sin)
        P.dma(cs_d[1, :, t0:t0 + 512], mk)
        P.stt(rk_, rr, -1.0, rr, ALU.mult, ALU.max)
        P.act(ang, rk_, AF.Sin, bias=halfpi, scale=-1.0)
        P.dma(cs_d[0, :, t0:t0 + 512], ang)
    P.release(m0)
    if stop == 'p0':
        return finish()

    def layer(l):
        mL = P.mark()
        W = P.al("W", [128, 8, WCOLS], BF16)
        lora_up = P.al("lora_up", [128, 512], BF16)
        vmix_up = P.al("vmix_up", [128, 512], BF16)
        Wq = P.al("Wq", [128, 3, 1024], BF16)
        Wkv = P.al("Wkv", [128, 2, 1024], BF16)
        mW = P.mark()
        wstage = P.al("wstage", [128, WCOLS])
        P.memset('pool', wstage, 0.0)
        for kc in range(8):
            rows = slice(kc * 128, (kc + 1) * 128)
            P.dma(wstage[:, 0:2816], win_d[l, rows, 0:2816])
            P.dma(wstage[:, 2816:3328], win_d[l, rows, 2848:3360])
            P.dma(wstage[:, 3328:3360], win_d[l, rows, 2816:2848])
            if l > 0:
                P.dma(wstage[:, 3392:3424], wvd_d[l - 1, rows, :])
            P.dma(wstage[:, 3456:3472], win_d[l, rows, 2832:2848])
            P.dma(wstage[:, 3472:3488], win_d[l, rows, 2816:2832])
            P.cp('dve' if kc % 2 else 'pool', W[:, kc, :], wstage)
            P.ts('dve', W[:, kc, 3456:3472], W[:, kc, 3456:3472], -1.0, None, ALU.mult)
        lst = P.al("lst", [128, 512])
        P.dma(lst[0:64], wdec_d[l])
        P.dma(lst[64:128], wicl_d[l])
        P.cp('pool', lora_up, lst)
        if l > 0:
            vst = P.al("vst", [128, 512])
            P.dma(vst[64:96], wvup_d[l - 1])
            P.cp('pool', vmix_up[64:96], vst[64:96])
        qst = P.al("qst", [128, 1024])
        for kc in range(3):
            src = wuq_d[l, kc * 128:(kc + 1) * 128, :].re("p (h e) -> p h e", e=96)
            P.dma(qst[:, 0:512].re("p (h e) -> p h e", e=64), src[:, :, 0:64])
            P.dma(qst[:, 512:768].re("p (h e) -> p h e", e=32), src[:, :, 64:96])
            rot = qst[:, 768:1024].re("p (h e) -> p h e", e=32)
            P.dma(rot[:, :, 0:16], src[:, :, 80:96])
            P.dma(rot[:, :, 16:32], src[:, :, 64:80])
            P.ts('dve', Wq[:, kc, :], qst, vecs[:, l, VQG + kc:VQG + kc + 1], None, ALU.mult)
            wr = Wq[:, kc, 768:1024].re("p (h e) -> p h e", e=32)[:, :, 0:16]
            P.ts('dve', wr, wr, -1.0, None, ALU.mult)
        for kc in range(2):
            src = wukv_d[l, kc * 128:(kc + 1) * 128, :].re("p (h e) -> p h e", e=128)
            P.dma(qst[:, 0:512].re("p (h e) -> p h e", e=64), src[:, :, 0:64])
            P.dma(qst[:, 512:1024].re("p (h e) -> p h e", e=64), src[:, :, 64:128])
            P.ts('dve', Wkv[:, kc, :], qst, vecs[:, l, VKVG + kc:VKVG + kc + 1], None, ALU.mult)

        chk(f'w_{l}')
        P.release(mW)
        carry = {j: P.al(f"carry{j}", [128, 1]) for j in list(range(13)) + [26]}
        for j in carry:
            P.memset('pool', carry[j], 0.0)
        pe2 = [P.al("pe0", [128, 513]), P.al("pe1", [128, 513])]
        npe = [0]
        xt = P.al("xt", [128, 8, 512])
        sq = P.al("sq", [128, 8, 512], BF16)
        hT = P.al("hT", [128, 8, 512], BF16)
        rstd = P.al("rstd", [128, 512])
        dtmp = P.al("dtmp", [128, 512])
        lsh = P.al("lsh", [128, 512])
        u2 = [dtmp, lsh]
        lora_in = P.al("lora_in", [128, 512], BF16)
        vlo_in = P.al("vlo_in", [128, 512], BF16)
        csb = P.al("csb", [128, 2, 512])
        F_ = {nm: P.al(nm, [128, 512]) for nm in
              ("rs0", "ks0", "vs0", "rs1", "ks1", "vs1", "sg", "aa", "gv", "vf", "kk", "nrm", "bb", "Gs", "g1")}
        B_ = {nm: P.al(nm, [128, 512], BF16) for nm in
              ("kk2", "Rt", "KKt", "Kt", "Bt", "Kh", "Bh", "Vb", "rkb")}
        nb4 = P.al("nb4", [128, 4])
        gC4 = P.al("gC4", [128, 4])
        sb4 = P.al("sb4", [128, 4, 2])
        tmo = P.al("tmo", [128, 4, 4, 128], BF16)
        gst = [P.al("gst0", [128, 512], BF16), P.al("gst1", [128, 512], BF16)]
        cq = P.al("cq", [128, 3, 512], BF16)
        sqq = P.al("sqq", [128, 3, 512], BF16)
        rq, crs, srs, a1, a2, rkv = (P.al(nm, [128, 512]) for nm in ("rq", "crs", "srs", "a1", "a2", "rkv"))
        qn = [P.al("qn0", [128, 512], BF16), P.al("qn1", [128, 512], BF16)]
        qrp = [P.al("qrp0", [128, 512], BF16), P.al("qrp1", [128, 512], BF16)]
        ckv = P.al("ckv", [128, 2, 512], BF16)
        sqk = P.al("sqk", [128, 2, 512], BF16)
        rcol = P.al("rcol", [128, 1])
        v1s = P.al("v1s", [128, 4, NH, 65], BF16)
        krb = P.al("krb", [128, 512], BF16)
        P.memset('pool', v1s[:, :, :, 64:65], 1.0)

        for b in range(NB):
            t0 = b * 512
            tsl = slice(t0, t0 + 512)
            P.dma(xt, xT_v[:, :, tsl])
            P.dma(csb, cs_d[:, :, tsl].re("w p t -> p w t"))
            for c in range(8):
                if c % 2:
                    P.act(sq[:, c, :], xt[:, c, :], AF.Square)
                else:
                    P.tt('pool', sq[:, c, :], xt[:, c, :], xt[:, c, :], ALU.mult)
            bk = P.bank()
            for c in range(8):
                P.mm(bk, onesb, sq[:, c, :], start=(c == 0), stop=(c == 7))
            P.act(rstd, bk, AF.Ln, bias=epsn, scale=1.0 / D)
            P.act(rstd, rstd, AF.Exp, scale=-0.5)
            for c in range(8):
                u = u2[c % 2]
                P.tt('pool' if c % 2 else 'dve', u, xt[:, c, :], rstd, ALU.mult)
                P.act(hT[:, c, :], u, AF.Identity, bias=mods[:, l, c:c + 1], scale=gs[:, l, c:c + 1])

            def proj(j, M=128):
                bk = P.bank()
                for kc in range(8):
                    P.mm(bk[0:M, :], W[:, kc, j * 128:j * 128 + M], hT[:, kc, :], start=(kc == 0), stop=(kc == 7))
                return bk

            def shifted(j, out, mucol, rows=slice(0, 128)):
                bk = proj(j)
                pe = pe2[npe[0] % 2]
                npe[0] += 1
                P.cp('pool', pe[rows, 0:1], carry[j][rows, :])
                P.cp('act', pe[:, 1:513], bk)
                P.tt('dve', dtmp[rows], pe[rows, 0:512], pe[rows, 1:513], ALU.subtract)
                P.stt(out[rows], dtmp[rows], vecs[rows, l, mucol:mucol + 1], pe[rows, 1:513], ALU.mult, ALU.add)
                P.cp('pool', carry[j][rows, :], pe[rows, 512:513])
                return pe

            chk(f'h_{l}')
            shifted(12, lsh, VMU + 12)
            P.act(lora_in[0:64], lsh[0:64], AF.Tanh)
            P.cp('pool', lora_in[64:128], lsh[64:128])
            if l > 0:
                pe26 = shifted(26, lsh, VMUV, rows=slice(64, 96))
                P.cp('pool', vlo_in[64:96], lsh[64:96])
            else:
                bk26 = proj(26)
                pe26 = pe2[npe[0] % 2]
                npe[0] += 1
                P.cp('act', pe26[:, 1:513], bk26)
            bk27 = proj(27, M=32)
            P.tt('dve', a1[0:32], pe26[0:32, 1:513], csb[0:32, 0, :], ALU.mult)
            P.tt('dve', a2[0:32], bk27[0:32, :], csb[0:32, 1, :], ALU.mult)
            P.tt('pool', krb[0:32], a1[0:32], a2[0:32], ALU.add)
            P.dma(krT_d[:, tsl], krb[0:32])

            def proj_shift(hp_):
                shifted(hp_, F_["rs" + str(hp_ % 2)], VMU + hp_)
                yield
                shifted(4 + hp_, F_["ks" + str(hp_ % 2)], VMU + 4 + hp_)
                yield
                shifted(8 + hp_, F_["vs" + str(hp_ % 2)], VMU + 8 + hp_)
                yield

            def rw_gen():
                for hp in range(4):
                    rs, ks, vs = (F_[k + str(hp % 2)] for k in ("rs", "ks", "vs"))
                    sg, aa, gv, vf = (F_[k] for k in ("sg", "aa", "gv", "vf"))
                    kk, nrm, bb, Gs, g1 = (F_[k] for k in ("kk", "nrm", "bb", "Gs", "g1"))
                    kkn = kk
                    ff = nrm
                    kmod = nrm
                    gi = e1 = ge = ginv = gcr = g1
                    rk = rs
                    if hp == 0:
                        for _ in proj_shift(0):
                            yield
                    if hp + 1 < 4:
                        for _ in proj_shift(hp + 1):
                            yield
                    cols = slice(hp * 128, (hp + 1) * 128)
                    prow = slice(hp * 128, (hp + 1) * 128)
                    bkw = P.bank()
                    P.mm(bkw, lora_up[0:64, cols], lora_in[0:64, :])
                    P.act(sg, bkw, AF.Sigmoid, bias=vecs[:, l, VW0 + hp:VW0 + hp + 1])
                    bka = P.bank()
                    P.mm(bka, lora_up[64:128, cols], lora_in[64:128, :])
                    P.act(aa, bka, AF.Sigmoid, bias=vecs[:, l, VA0 + hp:VA0 + hp + 1])
                    yield
                    if l > 0:
                        bkv = P.bank()
                        P.mm(bkv, vmix_up[64:96, cols], vlo_in[64:96, :])
                        P.act(gv, bkv, AF.Sigmoid, bias=vecs[:, l, VV0 + hp:VV0 + hp + 1])
                        P.dma(vf, vfirst_d[prow, tsl])
                        P.tt('pool', vf, vf, vs, ALU.subtract)
                        P.tt('pool', vf, vf, gv, ALU.mult)
                        P.tt('pool', vs, vs, vf, ALU.add)
                    else:
                        P.dma(vfirst_d[prow, tsl], vs)
                    P.act(kk, ks, AF.Identity, scale=vecs[:, l, VKK + hp:VKK + hp + 1])
                    P.tt('dve', B_["kk2"], kk, kk, ALU.mult)
                    bks = P.bank()
                    P.mm(bks, bonesb, B_["kk2"])
                    P.act(nrm, bks, AF.Ln, bias=eps24)
                    P.act(nrm, nrm, AF.Exp, scale=-0.5)
                    yield
                    P.tt('dve', kkn, kk, nrm, ALU.mult)
                    P.ts('dve', ff, aa, vecs[:, l, VKA + hp:VKA + hp + 1], omka[:, l, hp:hp + 1], ALU.mult, ALU.add)
                    P.tt('dve', kmod, ff, ks, ALU.mult)
                    P.tt('dve', bb, kkn, aa, ALU.mult)
                    P.scan(Gs, resetm, sg, 0.0, ALU.mult, ALU.add)
                    yield
                    P.act(gi, Gs, AF.Exp, scale=-C_DEC)
                    P.tt('dve', B_["Rt"], rs, gi, ALU.mult)
                    P.tt('dve', e1, Gs, sg, ALU.subtract)
                    P.act(ge, e1, AF.Exp, scale=-C_DEC)
                    P.tt('dve', B_["KKt"], kkn, ge, ALU.mult)
                    yield
                    P.act(ginv, Gs, AF.Exp, scale=C_DEC)
                    P.tt('dve', B_["Kt"], kmod, ginv, ALU.mult)
                    P.tt('dve', B_["Bt"], bb, ginv, ALU.mult)
                    yield
                    GsC = Gs.re("p (q t) -> p q t", t=128)[:, :, 127]
                    P.ts('dve', nb4, GsC, -C_DEC, None, ALU.mult)
                    for q in range(4):
                        P.act(gcr[:, q * 128:(q + 1) * 128], Gs[:, q * 128:(q + 1) * 128], AF.Exp, scale=C_DEC,
                              bias=nb4[:, q:q + 1])
                    P.tt('dve', B_["Kh"], kmod, gcr, ALU.mult)
                    P.tt('dve', B_["Bh"], bb, gcr, ALU.mult)
                    yield
                    P.act(gcall[:, hp, b * 4:(b + 1) * 4], nb4, AF.Exp)
                    P.tt('dve', rk, rs, kmod, ALU.mult)
                    P.act(B_["rkb"], rk, AF.Identity, scale=vecs[:, l, VRK + hp:VRK + hp + 1])
                    bkb = P.bank()
                    for q in range(4):
                        P.mm(bkb[:, q * 2:(q + 1) * 2], B_["rkb"][:, q * 128:(q + 1) * 128], bcolsb)
                    P.cp('dve', sball[:, b * 4:(b + 1) * 4, 2 * hp:2 * hp + 2], bkb[:, 0:8].re("p (q h) -> p q h", h=2))
                    P.cp('pool', B_["Vb"], vs)
                    yield
                    for opi, nm in enumerate(("Kt", "Bt", "KKt", "Rt")):
                        for hh in range(2):
                            P.dma(feat_d[b * 4:(b + 1) * 4, :, 2 * hp + hh, opi, :].re("c k t -> k c t"),
                                  B_[nm][hh * 64:(hh + 1) * 64, :].re("k (c t) -> k c t", c=4))
                    for q in range(4):
                        bkt = P.bank().cast(BF16)
                        for opi, nm in enumerate(("KKt", "Vb", "Kh", "Bh")):
                            P.tr(bkt[:, opi * 128:(opi + 1) * 128], B_[nm][:, q * 128:(q + 1) * 128], identb)
                        P.cp('act' if q % 2 else 'dve', tmo[:, q, :, :], bkt[:, 0:512].re("p (o f) -> p o f", o=4))
                        yield
                    for q in range(4):
                        P.dma(tokm_d[t0 + q * 128:t0 + (q + 1) * 128, :, cols], tmo[:, q, :, :])

            def mla_gen():
                for gi_, j in enumerate([13, 14, 15, 16, 22, 23, 24, 25]):
                    bk = proj(j)
                    g = gst[gi_ % 2]
                    P.act(g, bk, AF.Silu)
                    P.dma(gate_d[gi_ * 128:(gi_ + 1) * 128, tsl], g)
                    yield
                for i in range(3):
                    bk = proj(17 + i)
                    P.cp('act', cq[:, i, :], bk)
                    P.act(sqq[:, i, :], bk, AF.Square)
                    yield
                bk = P.bank()
                for i in range(3):
                    P.mm(bk, onesb, sqq[:, i, :], start=(i == 0), stop=(i == 2))
                P.act(rq, bk, AF.Ln, bias=epsn, scale=1.0 / 384)
                P.act(rq, rq, AF.Exp, scale=-0.5)
                yield
                P.tt('pool', crs, csb[:, 0, :], rq, ALU.mult)
                P.tt('pool', srs, csb[:, 1, :], rq, ALU.mult)
                for i in range(4):
                    bk = P.bank()
                    for kc in range(3):
                        P.mm(bk, Wq[:, kc, i * 128:(i + 1) * 128], cq[:, kc, :], start=(kc == 0), stop=(kc == 2))
                    qq = qn[i % 2]
                    P.tt('dve', qq, bk, rq, ALU.mult)
                    for hh in range(2):
                        P.dma(qT_d[2 * i + hh, 0:64, tsl], qq[hh * 64:(hh + 1) * 64, :])
                    yield
                for i in range(2):
                    bk1 = P.bank()
                    for kc in range(3):
                        P.mm(bk1, Wq[:, kc, 512 + i * 128:512 + (i + 1) * 128], cq[:, kc, :], start=(kc == 0),
                             stop=(kc == 2))
                    bk2 = P.bank()
                    for kc in range(3):
                        P.mm(bk2, Wq[:, kc, 768 + i * 128:768 + (i + 1) * 128], cq[:, kc, :], start=(kc == 0),
                             stop=(kc == 2))
                    P.tt('dve', a1, bk1, crs, ALU.mult)
                    P.tt('dve', a2, bk2, srs, ALU.mult)
                    qq = qrp[i % 2]
                    P.tt('dve', qq, a1, a2, ALU.add)
                    for hh in range(4):
                        P.dma(qT_d[4 * i + hh, 64:96, tsl], qq[hh * 32:(hh + 1) * 32, :])
                    yield
                for i in range(2):
                    bk = proj(20 + i)
                    P.cp('act', ckv[:, i, :], bk)
                    P.act(sqk[:, i, :], bk, AF.Square)
                    yield
                bk = P.bank()
                for i in range(2):
                    P.mm(bk, onesb, sqk[:, i, :], start=(i == 0), stop=(i == 1))
                P.act(rkv, bk, AF.Ln, bias=epsn, scale=1.0 / 256)
                P.act(rkv, rkv, AF.Exp, scale=-0.5)
                for i in range(4):
                    bk = P.bank()
                    for kc in range(2):
                        P.mm(bk, Wkv[:, kc, i * 128:(i + 1) * 128], ckv[:, kc, :], start=(kc == 0), stop=(kc == 1))
                    qq = qn[i % 2]
                    P.tt('dve', qq, bk, rkv, ALU.mult)
                    for hh in range(2):
                        P.dma(kT_d[2 * i + hh, :, tsl], qq[hh * 64:(hh + 1) * 64, :])
                    yield
                for q in range(4):
                    qs = slice(q * 128, (q + 1) * 128)
                    bkc = P.bank()
                    for i in range(2):
                        P.mm(bkc[:, 0:1], sqk[:, i, qs], onesb[:, 0:1], start=(i == 0), stop=(i == 1))
                    P.act(rcol, bkc[:, 0:1], AF.Ln, bias=epsn, scale=1.0 / 256)
                    P.act(rcol, rcol, AF.Exp, scale=-0.5)
                    bkv = P.bank()
                    for i in range(2):
                        P.mm(bkv, ckv[:, i, qs], Wkv[:, i, 512:1024], start=(i == 0), stop=(i == 1))
                    P.ts('dve', v1s[:, q, :, 0:64], bkv.re("p (h e) -> p h e", e=64), rcol, None, ALU.mult)
                    yield
                for q in range(4):
                    P.dma(v1_d[t0 + q * 128:t0 + (q + 1) * 128], v1s[:, q, :, :])
            gens_ = [rw_gen(), mla_gen()]
            while gens_:
                nx_ = []
                for gg in gens_:
                    try:
                        next(gg)
                        nx_.append(gg)
                    except StopIteration:
                        pass
                gens_ = nx_
        for hp in range(4):
            P.dma(gc_d[hp * 128:(hp + 1) * 128, :], gcall[:, hp, :])
        if 'sbon' in tapset:
            P.dma(sbon_d.re("(c p) h -> p c h", p=128), sball)
        P.release(mL)
        if stop == f'p1_{l}':
            return True

        gcs = P.al("gcs", [64, NH, NT])
        P.dma(gcs, gc_d.re("(h k) c -> k h c", k=64))
        sbn = sball
        lnw = P.al("lnw", [128, 1024])
        P.dma(lnw, V(rows_d.ap[l:l + 1, :].partition_broadcast(128), rows_d.buf))
        lnw3 = lnw[:, 0:512].re("p (h e) -> p h e", e=64)
        lnb3 = lnw[:, 512:1024].re("p (h e) -> p h e", e=64)
        Sf = P.al("Sf", [64, NH, 64])
        Sb = P.al("Sb", [64, NH, 64], BF16)
        P.memset('pool', Sf, 0.0)
        P.memset('pool', Sb, 0.0)
        GL = 4

        def slot_tiles(i):
            d = {}
            d['F'] = P.al(f"F{i}", [64, NH, 4, 128], BF16)
            d['T'] = P.al(f"T{i}", [128, 4, 512], BF16)
            d['XK'] = P.al(f"XK{i}", [128, NH, 128], BF16)
            d['AT'] = P.al(f"AT{i}", [128, NH, 512], BF16)
            for nm in ('Nk', 'Dk', 'Wk', 'Y1', 'Dl'):
                d[nm] = [P.al(f"{nm}{i}_{g}", [128, 4, 128], BF16) for g in range(2)]
            d['MU'] = [P.al(f"MU{i}_{g}", [128, 4, 128], BF16) for g in range(2)]
            d['nPQ'] = P.al(f"nPQ{i}", [128, NH, 128], BF16)
            d['nZ'] = P.al(f"nZ{i}", [64, NH, 64], BF16)
            d['Psi'] = P.al(f"Psi{i}", [64, NH, 64])
            d['Ry'] = P.al(f"Ry{i}", [64, NH, 128], BF16)
            d['y'] = P.al(f"y{i}", [128, NH, 64])
            d['t1'] = d['y'][0:64]
            d['yc'] = P.al(f"yc{i}", [128, NH, 64])
            d['ysq'] = d['y']
            d['mean'] = P.al(f"mean{i}", [128, NH])
            d['var'] = P.al(f"var{i}", [128, NH])
            d['ybf'] = P.al(f"ybf{i}", [128, 512], BF16)
            return d
        slots = [slot_tiles(i) for i in range(GL)]

        def chunk_gen(c):
            d = slots[c % GL]
            F, T, XK, AT, Nk, Dk, Wk, Y1, Dl, MU = (d[k] for k in ('F', 'T', 'XK', 'AT', 'Nk', 'Dk', 'Wk', 'Y1', 'Dl', 'MU'))
            nPQ, nZ, Psi, Ry, t1, y, yc, ysq, mean, var, ybf = (d[k] for k in
                                                               ('nPQ', 'nZ', 'Psi', 'Ry', 't1', 'y', 'yc', 'ysq', 'mean', 'var', 'ybf'))
            crow = slice(c * 128, (c + 1) * 128)
            P.dma(F, feat_d[c])
            P.dma(T, tokm_d[crow])
            P.dma(XK[:, :, 0:64], tokm_d[crow, 0, :].re("p (h e) -> p h e", e=64))
            yield
            for h in range(NH):
                bk = P.bank()
                KR = F[:, h, 2:4, :].re("k o t -> k (o t)")
                P.mm(bk[:, 0:256], F[:, h, 0, :], KR)
                P.mm(bk[:, 256:512], F[:, h, 1, :], KR)
                P.tt('dve', AT[:, h, :], bk, maskT, ALU.mult)
                if h % 4 == 3:
                    yield
            for g in range(2):
                bkn = P.bank()
                for hh in range(4):
                    h = 4 * g + hh
                    P.mm(bkn[:, hh * 128:(hh + 1) * 128], F[:, h, 2, :], F[:, h, 1, :])
                P.tt('dve', Nk[g], bkn.re("p (h s) -> p h s", h=4), masklow.un(1).bc([128, 4, 128]), ALU.mult)
                m0b = masku[:, 0, :].un(1).bc([128, 4, 128])
                idb = identb.un(1).bc([128, 4, 128])
                P.tt('pool', Dk[g], Nk[g], m0b, ALU.mult)
                P.tt('dve', Dk[g], Dk[g], idb, ALU.add)
                P.tt('pool', Wk[g], AT[:, 4 * g:4 * g + 4, 256:384], m0b, ALU.mult)
                P.tt('dve', Wk[g], Wk[g], idb, ALU.add)
                yield
            for j in range(1, 7):
                bkYs = []
                for g in range(2):
                    P.tt('pool', MU[g], AT[:, 4 * g:4 * g + 4, 256:384], masku[:, j, :].un(1).bc([128, 4, 128]), ALU.mult)
                    bkY = P.bank()
                    for hh in range(4):
                        P.mm(bkY[:, hh * 128:(hh + 1) * 128], MU[g][:, hh, :], Dk[g][:, hh, :])
                    P.cp('act', Y1[g], bkY.re("p (h s) -> p h s", h=4))
                yield
                bkDs = []
                for g in range(2):
                    bkD = P.bank()
                    bkDs.append(bkD)
                    for hh in range(4):
                        P.mm(bkD[:, hh * 128:(hh + 1) * 128], Wk[g][:, hh, :], Y1[g][:, hh, :])
                    P.cp('act', Dl[g], bkD.re("p (h s) -> p h s", h=4))
                    if j < 6:
                        P.tt('dve', Dk[g], bkD.re("p (h s) -> p h s", h=4), Dk[g], ALU.add)
                yield
                for g in range(2):
                    bkT = P.bank().cast(BF16)
                    for hh in range(4):
                        P.tr(bkT[:, hh * 128:(hh + 1) * 128], Dl[g][:, hh, :], identb)
                    P.tt('dve', Wk[g], bkT[:, 0:512].re("p (h s) -> p h s", h=4), Wk[g], ALU.add)
                yield
            bkx = P.bank()
            for h in range(NH):
                hc = slice(h * 64, (h + 1) * 64)
                P.mm(bkx[:, hc], AT[:, h, 0:128], T[:, 1, hc])
            P.cp('act', XK[:, :, 64:128], bkx.re("p (h e) -> p h e", e=64))
            yield
            for g in range(2):
                bkp = P.bank()
                for hh in range(4):
                    h = 4 * g + hh
                    P.mm(bkp[:, hh * 128:(hh + 1) * 128], Wk[g][:, hh, :], XK[:, h, :])
                P.act(nPQ[:, 4 * g:4 * g + 4, :], bkp.re("p (h s) -> p h s", h=4), AF.Copy, scale=-1.0)
            yield
            bkz = P.bank()
            for h in range(NH):
                hc = slice(h * 64, (h + 1) * 64)
                P.mm(bkz[0:64, hc], nPQ[:, h, 0:64], T[:, 3, hc])
            P.cp('act', nZ, bkz[0:64, :].re("k (h e) -> k h e", e=64))
            bkpsi = P.bank()
            for h in range(NH):
                hc = slice(h * 64, (h + 1) * 64)
                P.mm(bkpsi[0:64, hc], T[:, 2, hc], T[:, 1, hc], start=True, stop=False)
                P.mm(bkpsi[0:64, hc], T[:, 3, hc], nPQ[:, h, 64:128], start=False, stop=True)
            P.cp('dve', Psi, bkpsi[0:64, :].re("k (h e) -> k h e", e=64))
            for g in range(2):
                bkr = P.bank()
                for hh in range(4):
                    h = 4 * g + hh
                    P.mm(bkr[0:64, hh * 128:(hh + 1) * 128], nPQ[:, h, 0:64], AT[:, h, 384:512])
                P.tt('dve', Ry[:, 4 * g:4 * g + 4, :], bkr[0:64, :].re("k (h t) -> k h t", h=4),
                     F[:, 4 * g:4 * g + 4, 3, :], ALU.add)
            yield
            bky = P.bank()
            for h in range(NH):
                hc = slice(h * 64, (h + 1) * 64)
                P.mm(bky[:, hc], AT[:, h, 128:256], T[:, 1, hc], start=True, stop=False)
                P.mm(bky[:, hc], AT[:, h, 384:512], nPQ[:, h, 64:128], start=False, stop=False)
                P.mm(bky[:, hc], Ry[:, h, :], Sb[:, h, :], start=False, stop=True)
            bku = P.bank()
            for h in range(NH):
                hc = slice(h * 64, (h + 1) * 64)
                P.mm(bku[0:64, hc], nZ[:, h, :], Sb[:, h, :])
            P.tt('pool', t1, Sf, gcs[:, :, c].un(2).bc([64, NH, 64]), ALU.mult)
            P.tt('pool', t1, t1, Psi, ALU.add)
            bku3 = bku[0:64, :].re("k (h e) -> k h e", e=64)
            P.tt('dve', Sb, bku3, t1, ALU.add)
            P.tt('dve', Sf, bku3, t1, ALU.add)
            yield
            P.cp('act', y, bky.re("p (h e) -> p h e", e=64))
            P.reduce(mean, y)
            P.ts('dve', mean, mean, 1.0 / 64, None, ALU.mult)
            P.tt('dve', yc, y, mean.un(2).bc([128, NH, 64]), ALU.subtract)
            P.tt('dve', ysq, yc, yc, ALU.mult)
            P.reduce(var, ysq)
            P.act(var, var, AF.Ln, bias=epsg, scale=1.0 / 64)
            P.act(var, var, AF.Exp, scale=-0.5)
            yield
            P.tt('pool', yc, yc, var.un(2).bc([128, NH, 64]), ALU.mult)
            P.tt('pool', yc, yc, lnw3, ALU.mult)
            P.tt('pool', yc, yc, lnb3, ALU.add)
            P.tt('pool', ysq, T[:, 1, :].re("p (h e) -> p h e", e=64), sbn[:, c, :].un(2).bc([128, NH, 64]), ALU.mult)
            P.tt('dve', ybf.re("p (h e) -> p h e", e=64), yc, ysq, ALU.add)
            P.dma(ytok_d[crow, 0:512], ybf)

        def run_lockstep(gens):
            live = list(gens)
            while live:
                nxt = []
                for gg in live:
                    try:
                        next(gg)
                        nxt.append(gg)
                    except StopIteration:
                        pass
                live = nxt
        for c0 in range(0, NT, GL):
            run_lockstep([chunk_gen(c) for c in range(c0, min(NT, c0 + GL))])
        P.release(mL)
        if stop == f'p2_{l}':
            return True

        v1a = P.al("v1a", [128, NT, NH * 65], BF16)
        P.dma(v1a, v1_d.re("(j p) h e -> p j (h e)", p=128))
        kTq = [P.al(f"kTh{i}", [96, S], BF16) for i in range(2)]
        qTq = [P.al(f"qTh{i}", [96, S], BF16) for i in range(2)]
        Et = [P.al(f"E{i}", [128, 512], BF16) for i in range(4)]
        trib = P.al("trib", [128, 128], BF16)
        P.cp('pool', trib, tri)
        rden = P.al("rden", [128, 1])
        yh = [P.al(f"yh{i}", [128, 64], BF16) for i in range(2)]
        Ob = P.banks[0:4]
        Sbk = P.banks[4:8]
        nsb = 0
        for h in range(NH):
            kTh = kTq[h % 2]
            qTh = qTq[h % 2]
            P.dma(kTh[0:64], kT_d[h])
            P.dma(kTh[64:96], krT_d)
            P.dma(qTh, qT_d[h])
            for Q in range(NB):
                nkb = 4 * Q + 4
                qend = (Q + 1) * 512

                def qk(j):
                    nonlocal nsb
                    qlo = max(Q * 512, j * 128)
                    N = qend - qlo
                    bs = Sbk[nsb % 4]
                    E = Et[nsb % 4]
                    nsb += 1
                    P.mm(bs[:, 0:N], kTh[:, j * 128:(j + 1) * 128], qTh[:, qlo:qend])
                    P.act(E[:, 0:N], bs[:, 0:N], AF.Exp, scale=ATT_SCALE)
                    if j >= 4 * Q:
                        P.tt('pool', E[:, 0:128], E[:, 0:128], trib, ALU.mult)
                    return E, qlo

                def pv(j, E, qlo):
                    for t in range(max(4 * Q, j), 4 * Q + 4):
                        off = t * 128 - qlo
                        P.mm(Ob[t - 4 * Q][:, 0:65], E[:, off:off + 128], v1a[:, j, h * 65:(h + 1) * 65],
                             start=(j == 0), stop=(j == t))
                pq_ = [qk(0)]
                if nkb > 1:
                    pq_.append(qk(1))
                for j in range(nkb):
                    if j + 2 < nkb:
                        pq_.append(qk(j + 2))
                    pv(j, *pq_.pop(0))
                for tq in range(4):
                    t = 4 * Q + tq
                    yy = yh[tq % 2]
                    P.recip(rden, Ob[tq][:, 64:65])
                    P.ts('dve', yy, Ob[tq][:, 0:64], rden, None, ALU.mult)
                    P.dma(ytok_d[t * 128:(t + 1) * 128, 512 + h * 64:512 + (h + 1) * 64], yy)
        P.release(mL)
        if stop == f'p3_{l}':
            return True

        Wo = P.al("Wo", [128, 8, D], BF16)
        ost = P.al("ost", [128, D])
        for kc in range(8):
            P.dma(ost, wout_d[l, kc * 128:(kc + 1) * 128, :])
            P.cp('pool' if kc % 2 else 'dve', Wo[:, kc, :], ost)
        ytq = [P.al(f"yt{i}", [128, 4, D], BF16) for i in range(2)]
        gtq = [P.al(f"gt{i}", [128, 8, 512], BF16) for i in range(2)]
        xtq = [P.al(f"xt4{i}", [128, 8, 512]) for i in range(2)]
        yT = P.al("yT", [128, 8, 512], BF16)

        def p4_load(b):
            tsl = slice(b * 512, (b + 1) * 512)
            P.dma(ytq[b % 2], ytok_d[tsl].re("(q p) f -> p q f", p=128))
            P.dma(gtq[b % 2], gate_d.re("(c p) t -> p c t", p=128)[:, :, tsl])
            P.dma(xtq[b % 2], xT_v[:, :, tsl])
        p4_load(0)
        for b in range(NB):
            tsl = slice(b * 512, (b + 1) * 512)
            yt, gt, xt4 = ytq[b % 2], gtq[b % 2], xtq[b % 2]
            if b + 1 < NB:
                p4_load(b + 1)
            for f in range(8):
                bkt = P.bank().cast(BF16)
                for q in range(4):
                    P.tr(bkt[:, q * 128:(q + 1) * 128], yt[:, q, f * 128:(f + 1) * 128], identb)
                P.tt('dve', yT[:, f, :], bkt[:, 0:512], gt[:, f, :], ALU.mult)
            for dch in range(8):
                bk = P.bank()
                for f in range(8):
                    P.mm(bk, Wo[:, f, dch * 128:(dch + 1) * 128], yT[:, f, :], start=(f == 0), stop=(f == 7))
                P.stt(xt4[:, dch, :], bk, mods[:, l, 16 + dch:17 + dch], xt4[:, dch, :], ALU.mult, ALU.add)
            P.dma(xT_v[:, :, tsl], xt4)
        P.release(mL)
        return False

    try:
        for l in range(L):
            if layer(l):
                return finish()
    except _Stop:
        return finish()

    m0 = P.mark()
    fg = P.al("fg", [128, D])
    P.dma(fg, V(fing_d.ap.partition_broadcast(128), fing_d.buf))
    xtf = P.al("xtf", [128, 8, 512])
    xo = [P.al("xo0", [128, D]), P.al("xo1", [128, D])]
    junk = P.al("junkf", [128, D])
    ssq = [P.al("ssq0", [128, 1]), P.al("ssq1", [128, 1])]
    for b in range(NB):
        t0 = b * 512
        P.dma(xtf, xT_v[:, :, t0:t0 + 512])
        for j in range(4):
            bk0 = P.bank()
            bk1 = P.bank()
            for c in range(8):
                bk = bk0 if c < 4 else bk1
                P.tr(bk[:, (c % 4) * 128:(c % 4 + 1) * 128], xtf[:, c, j * 128:(j + 1) * 128], ident)
            o = xo[j % 2]
            s = ssq[j % 2]
            P.cp('dve', o[:, 0:512], bk0)
            P.cp('act', o[:, 512:1024], bk1)
            P.act(junk, o, AF.Square, accum=s)
            P.act(s, s, AF.Ln, bias=epsn, scale=1.0 / D)
            P.act(s, s, AF.Exp, scale=-0.5)
            P.stt(o, o, s, fg, ALU.mult, ALU.mult)
            P.dma(out_d[t0 + j * 128:t0 + (j + 1) * 128, :], o, final=True)
    P.release(m0)
    return finish()


def make_masku():
    i = np.arange(128)
    m = np.zeros((128, 7, 128), np.float32)
    for j in range(7):
        bs = 1 << j
        m[:, j, :] = ((i[:, None] // (2 * bs)) == (i[None, :] // (2 * bs))) & ((i[:, None] // bs) != (i[None, :] // bs))
    return m.reshape(128, 7 * 128).astype(ml_dtypes.bfloat16)


def host_layout(inputs, S, L):
    f = lambda a: np.ascontiguousarray(np.asarray(a, dtype=np.float32))

    def cols(v):
        v = np.asarray(v, np.float32)
        return v.reshape(-1, 128).T

    vecs = np.zeros((128, L, NV), np.float32)
    rows = np.zeros((L, 1024), np.float32)
    for l in range(L):
        vecs[:, l, VG:VG + 8] = cols(inputs['norm_g'][l])
        vecs[:, l, VB:VB + 24] = cols(inputs['b_ada'][l])
        vecs[:, l, VMU:VMU + 13] = cols(inputs['mu_shift'][l])
        if l > 0:
            vecs[64:96, l, VMUV] = np.asarray(inputs['mu_vmix'][l - 1], np.float32)
            vecs[:, l, VV0:VV0 + 4] = cols(inputs['v0'][l - 1])
        vecs[:, l, VW0:VW0 + 4] = cols(inputs['w0'][l])
        vecs[:, l, VA0:VA0 + 4] = cols(inputs['a0'][l])
        vecs[:, l, VKK:VKK + 4] = cols(inputs['k_k'][l])
        vecs[:, l, VKA:VKA + 4] = cols(inputs['k_a'][l])
        vecs[:, l, VRK:VRK + 4] = cols(np.asarray(inputs['r_k'][l]).reshape(-1))
        vecs[:, l, VQG:VQG + 3] = cols(inputs['q_norm_g'][l])
        vecs[:, l, VKVG:VKVG + 2] = cols(inputs['kv_norm_g'][l])
        rows[l, 0:512] = np.asarray(inputs['lnx_w'][l], np.float32)
        rows[l, 512:1024] = np.asarray(inputs['lnx_b'][l], np.float32)
    shared = {
        "consts": make_consts(), "vecs": vecs, "rows": rows, "masku": make_masku(),
        "final_g": f(inputs['final_g']).reshape(1, D),
        "w_ada": f(inputs['w_ada'])[:L], "w_in": f(inputs['w_in'])[:L],
        "w_vd": f(inputs['w_vmix_down'])[:max(L - 1, 1)],
        "w_dec": f(inputs['w_decay_up'])[:L], "w_icl": f(inputs['w_iclr_up'])[:L],
        "w_vup": f(inputs['w_vmix_up'])[:max(L - 1, 1)],
        "w_uq": f(inputs['w_uq'])[:L], "w_ukv": f(inputs['w_ukv'])[:L], "w_out": f(inputs['w_out'])[:L],
    }
    x = np.asarray(inputs['x'], np.float32)
    c = np.asarray(inputs['c'], np.float32)
    pos = np.asarray(inputs['positions']).astype(np.int32)
    B = x.shape[0]
    per = []
    for b in range(B):
        m = dict(shared)
        m["x"] = np.ascontiguousarray(x[b, :S])
        m["cT"] = np.ascontiguousarray(c[b].reshape(8, 128).T)
        m["pos"] = np.ascontiguousarray(pos[b, :S].reshape(1, S))
        per.append(m)
    return per


_NC_CACHE = {}


def kernel(**inputs):
    S, L = 4096, 4
    B = np.asarray(inputs['x']).shape[0]
    per = host_layout(inputs, S, L)
    if (S, L) not in _NC_CACHE:
        _NC_CACHE[(S, L)] = build(S, L)
    nc = _NC_CACHE[(S, L)]
    res = run_bass_kernel_spmd(nc, per, core_ids=list(range(B)))
    return np.stack([np.asarray(r["out"], np.float32) for r in res.results], axis=0)
```

```python
import numpy as np
import ml_dtypes
import concourse.bass as bass
import concourse.mybir as mybir
from concourse.bass_utils import run_bass_kernel_spmd
from contextlib import ExitStack

F32 = mybir.dt.float32
BF16 = mybir.dt.bfloat16
I32 = mybir.dt.int32
ALU = mybir.AluOpType
AF = mybir.ActivationFunctionType
AX = mybir.AxisListType

ENGS = ['pe', 'act', 'dve', 'pool', 'sp']
EPOCH = 20000
NDSEM = 24
DT_BYTES = {F32: 4, BF16: 2, I32: 4}


class Buf:
    __slots__ = ('name', 'last_w', 'readers', 'dma_ws', 'excl')

    def __init__(self, name, excl=False):
        self.name = name
        self.excl = excl
        self.last_w = None
        self.dma_ws = []
        self.readers = {}


class V:
    __slots__ = ('ap', 'buf')

    def __init__(self, ap, buf):
        self.ap = ap
        self.buf = buf

    def __getitem__(self, k):
        return V(self.ap[k], self.buf)

    def re(self, s, **kw):
        return V(self.ap.rearrange(s, **kw), self.buf)

    def bc(self, shape):
        return V(self.ap.to_broadcast(list(shape)), self.buf)

    def un(self, ax):
        return V(self.ap.unsqueeze(ax), self.buf)

    def cast(self, dt):
        return V(self.ap.bitcast(dt), self.buf)


class Op:
    __slots__ = ('eng', 'fn', 'deps', 'sig', 'is_dma', 'has_dep', 'gidx')

    def __init__(self, eng, fn, is_dma):
        self.eng = eng
        self.fn = fn
        self.is_dma = is_dma
        self.deps = []
        self.sig = None
        self.has_dep = False


class Prog:
    def __init__(self, nc, es, arena_bytes):
        self.nc = nc
        self.es = es
        self.ops = {e: [] for e in ENGS}
        self.n = 0
        self.final_dmas = []
        h = es.enter_context(nc.sbuf_tensor("arena", [128, arena_bytes // 2], BF16))
        self.arena = h[:]
        self.arena_bytes = arena_bytes
        self.live = []
        self.sp_ = 0
        self.banks = []
        for i in range(8):
            hb = es.enter_context(nc.psum_tensor(f"bank{i}", [128, 512], F32))
            self.banks.append(V(hb[:], Buf(f"bank{i}", excl=True)))
        self.bi = 0

    def sb(self, name, shape, dt=F32):
        h = self.es.enter_context(self.nc.sbuf_tensor("s_" + name, list(shape), dt))
        return V(h[:], Buf(name))

    def mark(self):
        return self.sp_

    def release(self, m):
        self.sp_ = m

    def al(self, name, shape, dt=F32):
        per = int(np.prod(shape[1:])) * DT_BYTES[dt]
        start = (self.sp_ + 63) // 64 * 64
        end = start + per
        assert end <= self.arena_bytes, (name, end, self.arena_bytes)
        self.sp_ = end
        b = Buf(name)
        keep = []
        for (s0, e0, ob) in self.live:
            if s0 < end and start < e0:
                cands = list(ob.readers.values()) + list(ob.dma_ws)
                if ob.last_w is not None:
                    cands.append(ob.last_w)
                for d in cands:
                    k = ('dma', id(d)) if d.is_dma else d.eng
                    if k not in b.readers or (not d.is_dma and b.readers[k].gidx < d.gidx):
                        b.readers[k] = d
                if not (start <= s0 and e0 <= end):
                    keep.append((s0, e0, ob))
            else:
                keep.append((s0, e0, ob))
        keep.append((start, end, b))
        self.live = keep
        ap = self.arena[0:shape[0], start // 2: end // 2]
        if dt != BF16:
            ap = ap.bitcast(dt)
        if len(shape) == 3:
            ap = ap.rearrange("p (a b) -> p a b", a=shape[1])
        elif len(shape) == 4:
            ap = ap.rearrange("p (a b c) -> p a b c", a=shape[1], b=shape[2])
        return V(ap, b)

    def bank(self):
        b = self.banks[self.bi % 8]
        self.bi += 1
        return b

    def dram(self, name, shape, dt, kind="Internal"):
        t = self.nc.dram_tensor(name, list(shape), dt, kind=kind)
        return V(t.ap(), Buf(name))

    def op(self, eng, fn, r=(), w=(), is_dma=False):
        o = Op(eng, fn, is_dma)
        o.gidx = self.n
        self.n += 1
        deps = {}

        def add(d):
            if d is None or d is o:
                return
            if d.is_dma:
                deps[('dma', id(d))] = d
            else:
                if d.eng == eng and eng == 'pe' and not is_dma:
                    return
                k = d.eng
                if k not in deps or deps[k].gidx < d.gidx:
                    deps[k] = d
        rb = [x.buf for x in r]
        wb = [x.buf for x in w]
        for b in rb:
            add(b.last_w)
            for d in b.dma_ws:
                add(d)
            if b.excl:
                for d in b.readers.values():
                    if d.eng != eng:
                        add(d)
        for b in wb:
            add(b.last_w)
            for d in b.readers.values():
                add(d)
            if not is_dma:
                for d in b.dma_ws:
                    add(d)
        for b in wb:
            if is_dma:
                if b.readers:
                    b.dma_ws = []
                    b.readers = {}
                b.dma_ws.append(o)
            else:
                b.last_w = o
                b.dma_ws = []
                b.readers = {}
        for b in rb:
            if b in wb:
                continue
            if is_dma:
                b.readers[('dma', id(o))] = o
            else:
                b.readers[eng] = o
        o.deps = list(deps.values())
        for d in o.deps:
            d.has_dep = True
        self.ops[eng].append(o)
        return o

    def dma(self, out, in_, eng='sp', final=False, **kw):
        o = self.op(eng, lambda e: e.dma_start(out=out.ap, in_=in_.ap, **kw), r=[in_], w=[out], is_dma=True)
        o.has_dep = True
        if final:
            self.final_dmas.append(o)
        return o

    def mm(self, out, lhsT, rhs, start=True, stop=True, **kw):
        return self.op('pe', lambda e: e.matmul(out.ap, lhsT.ap, rhs.ap, start=start, stop=stop, **kw),
                       r=[lhsT, rhs] + ([] if start else [out]), w=[out])

    def tr(self, out, in_, ident):
        return self.op('pe', lambda e: e.transpose(out.ap, in_.ap, ident.ap), r=[in_, ident], w=[out])

    def act(self, out, in_, func, bias=None, scale=None, accum=None):
        r = [in_]
        kw = {}
        if bias is not None:
            if isinstance(bias, V):
                r.append(bias)
                kw['bias'] = bias.ap
            else:
                kw['bias'] = float(bias)
        if scale is not None:
            if isinstance(scale, V):
                r.append(scale)
                kw['scale'] = scale.ap
            else:
                kw['scale'] = float(scale)
        w = [out]
        if accum is not None:
            kw['accum_out'] = accum.ap
            w.append(accum)
        return self.op('act', lambda e: e.activation(out.ap, in_.ap, func, **kw), r=r, w=w)

    def tt(self, eng, out, in0, in1, op):
        return self.op(eng, lambda e: e.tensor_tensor(out.ap, in0.ap, in1.ap, op), r=[in0, in1], w=[out])

    def ts(self, eng, out, in0, s1, s2=None, op0=ALU.mult, op1=None):
        r = [in0]
        a1 = s1.ap if isinstance(s1, V) else float(s1)
        if isinstance(s1, V):
            r.append(s1)
        a2 = None
        if s2 is not None:
            a2 = s2.ap if isinstance(s2, V) else float(s2)
            if isinstance(s2, V):
                r.append(s2)
        if op1 is None:
            return self.op(eng, lambda e: e.tensor_scalar(out.ap, in0.ap, a1, None, op0), r=r, w=[out])
        return self.op(eng, lambda e: e.tensor_scalar(out.ap, in0.ap, a1, a2, op0, op1), r=r, w=[out])

    def stt(self, out, in0, s, in1, op0, op1):
        r = [in0, in1]
        a = s.ap if isinstance(s, V) else float(s)
        if isinstance(s, V):
            r.append(s)
        return self.op('dve', lambda e: e.scalar_tensor_tensor(out.ap, in0.ap, a, in1.ap, op0, op1), r=r, w=[out])

    def cp(self, eng, out, in_):
        if eng == 'act':
            return self.op('act', lambda e: e.copy(out.ap, in_.ap), r=[in_], w=[out])
        return self.op(eng, lambda e: e.tensor_copy(out.ap, in_.ap), r=[in_], w=[out])

    def memset(self, eng, out, val):
        return self.op(eng, lambda e: e.memset(out.ap, val), r=[], w=[out])

    def scan(self, out, d0, d1, init, op0, op1):
        return self.op('dve', lambda e: e.tensor_tensor_scan(out.ap, d0.ap, d1.ap, float(init), op0, op1),
                       r=[d0, d1], w=[out])

    def recip(self, out, in_):
        return self.op('dve', lambda e: e.reciprocal(out.ap, in_.ap), r=[in_], w=[out])

    def reduce(self, out, in_, op=ALU.add):
        return self.op('dve', lambda e: e.tensor_reduce(out.ap, in_.ap, AX.X, op), r=[in_], w=[out])

    def emit(self):
        nc = self.nc
        sem_names = []
        for eng in ENGS:
            cnt = 0
            dcnt = 0
            for o in self.ops[eng]:
                if o.is_dma:
                    j = dcnt % NDSEM
                    name = f"d_{eng}_{j}"
                    o.sig = (name, 16 * (dcnt // NDSEM + 1), 16)
                    dcnt += 1
                elif o.has_dep:
                    ep = cnt // EPOCH
                    name = f"c_{eng}_{ep}"
                    o.sig = (name, cnt % EPOCH + 1, 1)
                    cnt += 1
                else:
                    continue
                if name not in sem_names:
                    sem_names.append(name)
        sems = {}
        for nm in sem_names:
            sems[nm] = self.es.enter_context(nc.semaphore(nm))
        self.nsem = len(sem_names)
        block = self.es.enter_context(nc.Block())
        prog = self

        def run(eng, e):
            known = {}
            for o in prog.ops[eng]:
                if o.is_dma:
                    nm, val, inc = o.sig
                    if val > 16 and known.get(nm, 0) < val - 16:
                        e.wait_ge(sems[nm], val - 16)
                        known[nm] = val - 16
                for d in o.deps:
                    nm, val, inc = d.sig
                    if known.get(nm, 0) < val:
                        e.wait_ge(sems[nm], val)
                        known[nm] = val
                inst = o.fn(e)
                if o.sig is not None:
                    nm, val, inc = o.sig
                    inst.then_inc(sems[nm], inc)
            if eng == 'sp':
                last = {}
                for en in ENGS:
                    for o in prog.ops[en]:
                        if o.is_dma:
                            nm, val, inc = o.sig
                            last[nm] = max(last.get(nm, 0), val)
                for nm, val in last.items():
                    if known.get(nm, 0) < val:
                        e.wait_ge(sems[nm], val)
                        known[nm] = val

        @block.tensor
        def _(e):
            run('pe', e)

        @block.scalar
        def _(e):
            run('act', e)

        @block.vector
        def _(e):
            run('dve', e)

        @block.gpsimd
        def _(e):
            run('pool', e)

        @block.sync
        def _(e):
            run('sp', e)


D = 1024
NKC = 8
NH = 8
NCH = 28
WCOLS = NCH * 128
C_DEC = float(np.exp(-0.5))
ATT_SCALE = float(96 ** -0.5)
NORM_EPS = 1e-6
GN_EPS = 64e-5
NV = 80
VG, VB, VMU, VMUV, VW0, VA0, VV0, VKK, VKA, VRK, VQG, VKVG = 0, 8, 32, 45, 46, 50, 54, 58, 62, 66, 70, 73
CI, CMT, CML, CTRI, CBO, CBC, CINV, CRST, NCC = 0, 128, 640, 768, 896, 1024, 1026, 1028, 1540
TWO_PI = float(2 * np.pi)
CW1 = 6.28125
CW2 = float(2 * np.pi - 6.28125)


def make_consts():
    c = np.zeros((128, NCC), np.float32)
    i = np.arange(128)
    c[:, CI:CI + 128] = np.eye(128)
    strict = (i[None, :] > i[:, None]).astype(np.float32)
    incl = (i[None, :] >= i[:, None]).astype(np.float32)
    c[:, CMT:CMT + 512] = np.concatenate([strict, incl, -strict, incl], axis=1)
    c[:, CML:CML + 128] = -(i[None, :] < i[:, None]).astype(np.float32)
    c[:, CTRI:CTRI + 128] = incl
    bo = np.zeros((128, 128), np.float32)
    bo[:64, :64] = 1
    bo[64:, 64:] = 1
    c[:, CBO:CBO + 128] = bo
    c[:64, CBC] = 1
    c[64:, CBC + 1] = 1
    inv = (10000.0 ** (-np.arange(0, 32, 2, dtype=np.float32) / 32)).astype(np.float32)
    c[:, CINV] = inv[i % 16]
    rs = np.ones(512, np.float32)
    rs[::128] = 0
    c[:, CRST:CRST + 512] = rs[None, :]
    return c


class _Stop(Exception):
    pass


def build(S, L, taps=(), stop=None):
    NB = S // 512
    NT = S // 128
    nc = bass.Bass("TRN2", target_bir_lowering=False)
    es = ExitStack()
    P = Prog(nc, es, arena_bytes=195 * 1024)
    tapset = set(taps)

    def din(name, shape, dt=F32):
        t = nc.dram_tensor(name, list(shape), dt, kind="ExternalInput")
        return V(t.ap(), Buf(name))

    def dscr(name, shape, dt):
        kind = "ExternalOutput" if name in tapset else "Internal"
        return P.dram(name, shape, dt, kind=kind)

    def finish():
        P.emit()
        es.close()
        return nc

    def chk(tag):
        if stop == tag:
            raise _Stop()

    x_d = din("x", [S, D])
    cT_d = din("cT", [128, 8])
    pos_d = din("pos", [1, S], I32)
    consts_d = din("consts", [128, NCC])
    vecs_d = din("vecs", [128, L, NV])
    rows_d = din("rows", [L, 1024])
    masku_d = din("masku", [128, 7 * 128], BF16)
    fing_d = din("final_g", [1, D])
    wada_d = din("w_ada", [L, D, 3 * D])
    win_d = din("w_in", [L, D, 3360])
    wvd_d = din("w_vd", [max(L - 1, 1), D, 32])
    wdec_d = din("w_dec", [L, 64, 512])
    wicl_d = din("w_icl", [L, 64, 512])
    wvup_d = din("w_vup", [max(L - 1, 1), 32, 512])
    wuq_d = din("w_uq", [L, 384, 768])
    wukv_d = din("w_ukv", [L, 256, 1024])
    wout_d = din("w_out", [L, D, D])
    out_d = P.dram("out", [S, D], F32, kind="ExternalOutput")

    xT_d = dscr("xT", [D, S], F32)
    cs_d = dscr("cs", [2, 128, S], F32)
    vfirst_d = dscr("vfirst", [512, S], F32)
    feat_d = dscr("feat", [NT, 64, NH, 4, 128], BF16)
    tokm_d = dscr("tokm", [S, 4, 512], BF16)
    gc_d = dscr("gc", [512, NT], F32)
    sbon_d = dscr("sbon", [S, NH], F32)
    gate_d = dscr("gate", [D, S], BF16)
    qT_d = dscr("qT", [NH, 96, S], BF16)
    kT_d = dscr("kT", [NH, 64, S], BF16)
    krT_d = dscr("krT", [32, S], BF16)
    v1_d = dscr("v1", [S, NH, 65], BF16)
    ytok_d = dscr("ytok", [S, D], BF16)
    xT_v = xT_d.re("(c p) t -> p c t", p=128)

    consts = P.sb("consts", [128, NCC])
    vecs = P.sb("vecs", [128, L, NV])
    mods = P.sb("mods", [128, L, 24])
    gs = P.sb("gs", [128, L, 8])
    omka = P.sb("omka", [128, L, 4])
    identb = P.sb("identb", [128, 128], BF16)
    onesb = P.sb("onesb", [128, 128], BF16)
    bonesb = P.sb("bonesb", [128, 128], BF16)
    bcolsb = P.sb("bcolsb", [128, 2], BF16)
    epsn = P.sb("epsn", [128, 1])
    epsg = P.sb("epsg", [128, 1])
    halfpi = P.sb("halfpi", [128, 1])
    eps24 = P.sb("eps24", [128, 1])
    masku = P.sb("masku", [128, 7, 128], BF16)
    sball = P.sb("sball", [128, NT, NH])
    gcall = P.sb("gcall", [128, 4, NT])
    ident = consts[:, CI:CI + 128]
    maskT = consts[:, CMT:CMT + 512]
    masklow = consts[:, CML:CML + 128]
    tri = consts[:, CTRI:CTRI + 128]
    invf = consts[:, CINV:CINV + 1]
    resetm = consts[:, CRST:CRST + 512]

    P.dma(consts, consts_d)
    P.dma(vecs, vecs_d)
    P.dma(masku, masku_d.re("p (j t) -> p j t", j=7))
    P.cp('dve', identb, ident)
    P.memset('pool', onesb, 1.0)
    P.memset('pool', epsn, NORM_EPS)
    P.memset('pool', epsg, GN_EPS)
    P.memset('pool', halfpi, float(np.pi / 2))
    P.memset('pool', eps24, 1e-24)
    P.cp('dve', bonesb, consts[:, CBO:CBO + 128])
    P.cp('dve', bcolsb, consts[:, CBC:CBC + 2])

    m0 = P.mark()
    cact = P.al("cact", [128, 8])
    P.dma(cact, cT_d)
    P.act(cact, cact, AF.Silu)
    wst_t = [P.al("wst0", [128, 8, 512]), P.al("wst1", [128, 8, 512])]
    mbank = P.bank()
    n = 0
    for l in range(L):
        for cg in range(6):
            wst = wst_t[n % 2]
            n += 1
            P.dma(wst, wada_d[l, :, cg * 512:(cg + 1) * 512].re("(kc p) n -> p kc n", p=128))
            for j in range(4):
                col = l * 24 + cg * 4 + j
                for kc in range(8):
                    P.mm(mbank[:, col:col + 1], wst[:, kc, j * 128:(j + 1) * 128], cact[:, kc:kc + 1],
                         start=(kc == 0), stop=(kc == 7))
    P.tt('dve', mods, mbank[:, 0:L * 24].re("p (l j) -> p l j", l=L), vecs[:, :, VB:VB + 24], ALU.add)
    P.stt(gs, mods[:, :, 8:16], 1.0, vecs[:, :, VG:VG + 8], ALU.add, ALU.mult)
    P.ts('dve', omka, vecs[:, :, VKA:VKA + 4], -1.0, 1.0, ALU.mult, ALU.add)

    xs = P.al("xs", [128, 4, D])
    xts = P.al("xts", [128, 8, 512])
    posi = P.al("posi", [128, 512], I32)
    ang = P.al("ang", [128, 512])
    rk_ = P.al("rk_", [128, 512])
    rki = P.al("rki", [128, 512], I32)
    rr = P.al("rr", [128, 512])
    mk = P.al("mk", [128, 512])
    for b in range(NB):
        t0 = b * 512
        P.dma(xs, x_d[t0:t0 + 512, :].re("(j p) d -> p j d", p=128))
        for c in range(8):
            bk = P.bank()
            for j in range(4):
                P.tr(bk[:, j * 128:(j + 1) * 128], xs[:, j, c * 128:(c + 1) * 128], ident)
            P.cp('act' if c % 2 else 'dve', xts[:, c, :], bk)
        P.dma(xT_v[:, :, t0:t0 + 512], xts)
        P.dma(posi, V(pos_d.ap[:, t0:t0 + 512].partition_broadcast(128), pos_d.buf))
        P.cp('pool', ang, posi)
        P.ts('dve', ang, ang, invf, None, ALU.mult)
        P.ts('dve', rk_, ang, 1.0 / TWO_PI, None, ALU.mult)
        P.cp('dve', rki, rk_)
        P.cp('dve', rk_, rki)
        P.stt(rr, rk_, -CW1, ang, ALU.mult, ALU.add)
        P.stt(rr, rk_, -CW2, rr, ALU.mult, ALU.add)
        P.ts('dve', mk, rr, float(np.pi), None, ALU.is_gt)
        P.stt(rr, mk, -TWO_PI, rr, ALU.mult, ALU.add)
        P.ts('dve', mk, rr, float(-np.pi), None, ALU.is_lt)
        P.stt(rr, mk, TWO_PI, rr, ALU.mult, ALU.add)
        P.ts('dve', rr, rr, float(np.pi), float(-np.pi), ALU.min, ALU.max)
        P.act(mk, rr, AF.Sin)
        P.dma(cs_d[1, :, t0:t0 + 512], mk)
        P.stt(rk_, rr, -1.0, rr, ALU.mult, ALU.max)
        P.act(ang, rk_, AF.Sin, bias=halfpi, scale=-1.0)
        P.dma(cs_d[0, :, t0:t0 + 512], ang)
    P.release(m0)
    if stop == 'p0':
        return finish()

    def layer(l):
        mL = P.mark()
        W = P.al("W", [128, 8, WCOLS], BF16)
        lora_up = P.al("lora_up", [128, 512], BF16)
        vmix_up = P.al("vmix_up", [128, 512], BF16)
        Wq = P.al("Wq", [128, 3, 1024], BF16)
        Wkv = P.al("Wkv", [128, 2, 1024], BF16)
        mW = P.mark()
        wstage = P.al("wstage", [128, WCOLS])
        P.memset('pool', wstage, 0.0)
        for kc in range(8):
            rows = slice(kc * 128, (kc + 1) * 128)
            P.dma(wstage[:, 0:2816], win_d[l, rows, 0:2816])
            P.dma(wstage[:, 2816:3328], win_d[l, rows, 2848:3360])
            P.dma(wstage[:, 3328:3360], win_d[l, rows, 2816:2848])
            if l > 0:
                P.dma(wstage[:, 3392:3424], wvd_d[l - 1, rows, :])
            P.dma(wstage[:, 3456:3472], win_d[l, rows, 2832:2848])
            P.dma(wstage[:, 3472:3488], win_d[l, rows, 2816:2832])
            P.cp('dve' if kc % 2 else 'act', W[:, kc, :], wstage)
            P.ts('dve', W[:, kc, 3456:3472], W[:, kc, 3456:3472], -1.0, None, ALU.mult)
        lst = P.al("lst", [128, 512])
        P.dma(lst[0:64], wdec_d[l])
        P.dma(lst[64:128], wicl_d[l])
        P.cp('pool', lora_up, lst)
        if l > 0:
            vst = P.al("vst", [128, 512])
            P.dma(vst[64:96], wvup_d[l - 1])
            P.cp('pool', vmix_up[64:96], vst[64:96])
        qst = P.al("qst", [128, 1024])
        for kc in range(3):
            src = wuq_d[l, kc * 128:(kc + 1) * 128, :].re("p (h e) -> p h e", e=96)
            P.dma(qst[:, 0:512].re("p (h e) -> p h e", e=64), src[:, :, 0:64])
            P.dma(qst[:, 512:768].re("p (h e) -> p h e", e=32), src[:, :, 64:96])
            rot = qst[:, 768:1024].re("p (h e) -> p h e", e=32)
            P.dma(rot[:, :, 0:16], src[:, :, 80:96])
            P.dma(rot[:, :, 16:32], src[:, :, 64:80])
            P.ts('dve', Wq[:, kc, :], qst, vecs[:, l, VQG + kc:VQG + kc + 1], None, ALU.mult)
            wr = Wq[:, kc, 768:1024].re("p (h e) -> p h e", e=32)[:, :, 0:16]
            P.ts('dve', wr, wr, -1.0, None, ALU.mult)
        for kc in range(2):
            src = wukv_d[l, kc * 128:(kc + 1) * 128, :].re("p (h e) -> p h e", e=128)
            P.dma(qst[:, 0:512].re("p (h e) -> p h e", e=64), src[:, :, 0:64])
            P.dma(qst[:, 512:1024].re("p (h e) -> p h e", e=64), src[:, :, 64:128])
            P.ts('dve', Wkv[:, kc, :], qst, vecs[:, l, VKVG + kc:VKVG + kc + 1], None, ALU.mult)

        chk(f'w_{l}')
        P.release(mW)
        carry = {j: P.al(f"carry{j}", [128, 1]) for j in list(range(13)) + [26]}
        for j in carry:
            P.memset('pool', carry[j], 0.0)
        pe2 = [P.al("pe0", [128, 513]), P.al("pe1", [128, 513])]
        npe = [0]
        xt = P.al("xt", [128, 8, 512])
        sq = P.al("sq", [128, 8, 512], BF16)
        hT = P.al("hT", [128, 8, 512], BF16)
        rstd = P.al("rstd", [128, 512])
        dtmp = P.al("dtmp", [128, 512])
        lsh = P.al("lsh", [128, 512])
        u2 = [dtmp, lsh]
        lora_in = P.al("lora_in", [128, 512], BF16)
        vlo_in = P.al("vlo_in", [128, 512], BF16)
        csb = P.al("csb", [128, 2, 512])
        F_ = {nm: P.al(nm, [128, 512]) for nm in
              ("rs0", "ks0", "vs0", "rs1", "ks1", "vs1", "sg", "aa", "gv", "vf", "kk", "nrm", "bb", "Gs", "g1")}
        B_ = {nm: P.al(nm, [128, 512], BF16) for nm in
              ("kk2", "Rt", "KKt", "Kt", "Bt", "Kh", "Bh", "Vb", "rkb")}
        nb4 = P.al("nb4", [128, 4])
        gC4 = P.al("gC4", [128, 4])
        sb4 = P.al("sb4", [128, 4, 2])
        tmo = P.al("tmo", [128, 4, 4, 128], BF16)
        gst = [P.al("gst0", [128, 512], BF16), P.al("gst1", [128, 512], BF16)]
        cq = P.al("cq", [128, 3, 512], BF16)
        sqq = P.al("sqq", [128, 3, 512], BF16)
        rq, crs, srs, a1, a2, rkv = (P.al(nm, [128, 512]) for nm in ("rq", "crs", "srs", "a1", "a2", "rkv"))
        qn = [P.al("qn0", [128, 512], BF16), P.al("qn1", [128, 512], BF16)]
        qrp = [P.al("qrp0", [128, 512], BF16), P.al("qrp1", [128, 512], BF16)]
        ckv = P.al("ckv", [128, 2, 512], BF16)
        sqk = P.al("sqk", [128, 2, 512], BF16)
        rcol = P.al("rcol", [128, 1])
        v1s = P.al("v1s", [128, 4, NH, 65], BF16)
        krb = P.al("krb", [128, 512], BF16)
        P.memset('pool', v1s[:, :, :, 64:65], 1.0)

        for b in range(NB):
            t0 = b * 512
            tsl = slice(t0, t0 + 512)
            P.dma(xt, xT_v[:, :, tsl])
            P.dma(csb, cs_d[:, :, tsl].re("w p t -> p w t"))
            for c in range(8):
                if c % 2:
                    P.act(sq[:, c, :], xt[:, c, :], AF.Square)
                else:
                    P.tt('pool', sq[:, c, :], xt[:, c, :], xt[:, c, :], ALU.mult)
            bk = P.bank()
            for c in range(8):
                P.mm(bk, onesb, sq[:, c, :], start=(c == 0), stop=(c == 7))
            P.act(rstd, bk, AF.Ln, bias=epsn, scale=1.0 / D)
            P.act(rstd, rstd, AF.Exp, scale=-0.5)
            for c in range(8):
                u = u2[c % 2]
                P.tt('pool' if c % 2 else 'dve', u, xt[:, c, :], rstd, ALU.mult)
                P.act(hT[:, c, :], u, AF.Identity, bias=mods[:, l, c:c + 1], scale=gs[:, l, c:c + 1])

            def proj(j, M=128):
                bk = P.bank()
                for kc in range(8):
                    P.mm(bk[0:M, :], W[:, kc, j * 128:j * 128 + M], hT[:, kc, :], start=(kc == 0), stop=(kc == 7))
                return bk

            def shifted(j, out, mucol, rows=slice(0, 128)):
                bk = proj(j)
                pe = pe2[npe[0] % 2]
                npe[0] += 1
                P.cp('pool', pe[rows, 0:1], carry[j][rows, :])
                P.cp('act', pe[:, 1:513], bk)
                P.tt('dve', dtmp[rows], pe[rows, 0:512], pe[rows, 1:513], ALU.subtract)
                P.stt(out[rows], dtmp[rows], vecs[rows, l, mucol:mucol + 1], pe[rows, 1:513], ALU.mult, ALU.add)
                P.cp('pool', carry[j][rows, :], pe[rows, 512:513])
                return pe

            chk(f'h_{l}')
            shifted(12, lsh, VMU + 12)
            P.act(lora_in[0:64], lsh[0:64], AF.Tanh)
            P.cp('pool', lora_in[64:128], lsh[64:128])
            if l > 0:
                pe26 = shifted(26, lsh, VMUV, rows=slice(64, 96))
                P.cp('pool', vlo_in[64:96], lsh[64:96])
            else:
                bk26 = proj(26)
                pe26 = pe2[npe[0] % 2]
                npe[0] += 1
                P.cp('act', pe26[:, 1:513], bk26)
            bk27 = proj(27, M=32)
            P.tt('dve', a1[0:32], pe26[0:32, 1:513], csb[0:32, 0, :], ALU.mult)
            P.tt('dve', a2[0:32], bk27[0:32, :], csb[0:32, 1, :], ALU.mult)
            P.tt('pool', krb[0:32], a1[0:32], a2[0:32], ALU.add)
            P.dma(krT_d[:, tsl], krb[0:32])

            def proj_shift(hp_):
                shifted(hp_, F_["rs" + str(hp_ % 2)], VMU + hp_)
                yield
                shifted(4 + hp_, F_["ks" + str(hp_ % 2)], VMU + 4 + hp_)
                yield
                shifted(8 + hp_, F_["vs" + str(hp_ % 2)], VMU + 8 + hp_)
                yield

            def rw_gen():
                for hp in range(4):
                    rs, ks, vs = (F_[k + str(hp % 2)] for k in ("rs", "ks", "vs"))
                    sg, aa, gv, vf = (F_[k] for k in ("sg", "aa", "gv", "vf"))
                    kk, nrm, bb, Gs, g1 = (F_[k] for k in ("kk", "nrm", "bb", "Gs", "g1"))
                    kkn = kk
                    ff = nrm
                    kmod = nrm
                    gi = e1 = ge = ginv = gcr = g1
                    rk = rs
                    if hp == 0:
                        for _ in proj_shift(0):
                            yield
                    if hp + 1 < 4:
                        for _ in proj_shift(hp + 1):
                            yield
                    cols = slice(hp * 128, (hp + 1) * 128)
                    prow = slice(hp * 128, (hp + 1) * 128)
                    bkw = P.bank()
                    P.mm(bkw, lora_up[0:64, cols], lora_in[0:64, :])
                    P.act(sg, bkw, AF.Sigmoid, bias=vecs[:, l, VW0 + hp:VW0 + hp + 1])
                    bka = P.bank()
                    P.mm(bka, lora_up[64:128, cols], lora_in[64:128, :])
                    P.act(aa, bka, AF.Sigmoid, bias=vecs[:, l, VA0 + hp:VA0 + hp + 1])
                    yield
                    if l > 0:
                        bkv = P.bank()
                        P.mm(bkv, vmix_up[64:96, cols], vlo_in[64:96, :])
                        P.act(gv, bkv, AF.Sigmoid, bias=vecs[:, l, VV0 + hp:VV0 + hp + 1])
                        P.dma(vf, vfirst_d[prow, tsl])
                        P.tt('pool', vf, vf, vs, ALU.subtract)
                        P.tt('pool', vf, vf, gv, ALU.mult)
                        P.tt('pool', vs, vs, vf, ALU.add)
                    else:
                        P.dma(vfirst_d[prow, tsl], vs)
                    P.act(kk, ks, AF.Identity, scale=vecs[:, l, VKK + hp:VKK + hp + 1])
                    P.tt('dve', B_["kk2"], kk, kk, ALU.mult)
                    bks = P.bank()
                    P.mm(bks, bonesb, B_["kk2"])
                    P.act(nrm, bks, AF.Ln, bias=eps24)
                    P.act(nrm, nrm, AF.Exp, scale=-0.5)
                    yield
                    P.tt('dve', kkn, kk, nrm, ALU.mult)
                    P.ts('dve', ff, aa, vecs[:, l, VKA + hp:VKA + hp + 1], omka[:, l, hp:hp + 1], ALU.mult, ALU.add)
                    P.tt('dve', kmod, ff, ks, ALU.mult)
                    P.tt('dve', bb, kkn, aa, ALU.mult)
                    P.scan(Gs, resetm, sg, 0.0, ALU.mult, ALU.add)
                    yield
                    P.act(gi, Gs, AF.Exp, scale=-C_DEC)
                    P.tt('dve', B_["Rt"], rs, gi, ALU.mult)
                    P.tt('dve', e1, Gs, sg, ALU.subtract)
                    P.act(ge, e1, AF.Exp, scale=-C_DEC)
                    P.tt('dve', B_["KKt"], kkn, ge, ALU.mult)
                    yield
                    P.act(ginv, Gs, AF.Exp, scale=C_DEC)
                    P.tt('dve', B_["Kt"], kmod, ginv, ALU.mult)
                    P.tt('dve', B_["Bt"], bb, ginv, ALU.mult)
                    yield
                    GsC = Gs.re("p (q t) -> p q t", t=128)[:, :, 127]
                    P.ts('dve', nb4, GsC, -C_DEC, None, ALU.mult)
                    for q in range(4):
                        P.act(gcr[:, q * 128:(q + 1) * 128], Gs[:, q * 128:(q + 1) * 128], AF.Exp, scale=C_DEC,
                              bias=nb4[:, q:q + 1])
                    P.tt('dve', B_["Kh"], kmod, gcr, ALU.mult)
                    P.tt('dve', B_["Bh"], bb, gcr, ALU.mult)
                    yield
                    P.act(gcall[:, hp, b * 4:(b + 1) * 4], nb4, AF.Exp)
                    P.tt('dve', rk, rs, kmod, ALU.mult)
                    P.act(B_["rkb"], rk, AF.Identity, scale=vecs[:, l, VRK + hp:VRK + hp + 1])
                    bkb = P.bank()
                    for q in range(4):
                        P.mm(bkb[:, q * 2:(q + 1) * 2], B_["rkb"][:, q * 128:(q + 1) * 128], bcolsb)
                    P.cp('dve', sball[:, b * 4:(b + 1) * 4, 2 * hp:2 * hp + 2], bkb[:, 0:8].re("p (q h) -> p q h", h=2))
                    P.cp('pool', B_["Vb"], vs)
                    yield
                    for opi, nm in enumerate(("Kt", "Bt", "KKt", "Rt")):
                        for hh in range(2):
                            P.dma(feat_d[b * 4:(b + 1) * 4, :, 2 * hp + hh, opi, :].re("c k t -> k c t"),
                                  B_[nm][hh * 64:(hh + 1) * 64, :].re("k (c t) -> k c t", c=4))
                    for q in range(4):
                        bkt = P.bank().cast(BF16)
                        for opi, nm in enumerate(("KKt", "Vb", "Kh", "Bh")):
                            P.tr(bkt[:, opi * 128:(opi + 1) * 128], B_[nm][:, q * 128:(q + 1) * 128], identb)
                        P.cp('act' if q % 2 else 'dve', tmo[:, q, :, :], bkt[:, 0:512].re("p (o f) -> p o f", o=4))
                        yield
                    for q in range(4):
                        P.dma(tokm_d[t0 + q * 128:t0 + (q + 1) * 128, :, cols], tmo[:, q, :, :])

            def mla_gen():
                for gi_, j in enumerate([13, 14, 15, 16, 22, 23, 24, 25]):
                    bk = proj(j)
                    g = gst[gi_ % 2]
                    P.act(g, bk, AF.Silu)
                    P.dma(gate_d[gi_ * 128:(gi_ + 1) * 128, tsl], g)
                    yield
                for i in range(3):
                    bk = proj(17 + i)
                    P.cp('act', cq[:, i, :], bk)
                    P.act(sqq[:, i, :], bk, AF.Square)
                    yield
                bk = P.bank()
                for i in range(3):
                    P.mm(bk, onesb, sqq[:, i, :], start=(i == 0), stop=(i == 2))
                P.act(rq, bk, AF.Ln, bias=epsn, scale=1.0 / 384)
                P.act(rq, rq, AF.Exp, scale=-0.5)
                yield
                P.tt('pool', crs, csb[:, 0, :], rq, ALU.mult)
                P.tt('pool', srs, csb[:, 1, :], rq, ALU.mult)
                for i in range(4):
                    bk = P.bank()
                    for kc in range(3):
                        P.mm(bk, Wq[:, kc, i * 128:(i + 1) * 128], cq[:, kc, :], start=(kc == 0), stop=(kc == 2))
                    qq = qn[i % 2]
                    P.tt('dve', qq, bk, rq, ALU.mult)
                    for hh in range(2):
                        P.dma(qT_d[2 * i + hh, 0:64, tsl], qq[hh * 64:(hh + 1) * 64, :])
                    yield
                for i in range(2):
                    bk1 = P.bank()
                    for kc in range(3):
                        P.mm(bk1, Wq[:, kc, 512 + i * 128:512 + (i + 1) * 128], cq[:, kc, :], start=(kc == 0),
                             stop=(kc == 2))
                    bk2 = P.bank()
                    for kc in range(3):
                        P.mm(bk2, Wq[:, kc, 768 + i * 128:768 + (i + 1) * 128], cq[:, kc, :], start=(kc == 0),
                             stop=(kc == 2))
                    P.tt('dve', a1, bk1, crs, ALU.mult)
                    P.tt('dve', a2, bk2, srs, ALU.mult)
                    qq = qrp[i % 2]
                    P.tt('dve', qq, a1, a2, ALU.add)
                    for hh in range(4):
                        P.dma(qT_d[4 * i + hh, 64:96, tsl], qq[hh * 32:(hh + 1) * 32, :])
                    yield
                for i in range(2):
                    bk = proj(20 + i)
                    P.cp('act', ckv[:, i, :], bk)
                    P.act(sqk[:, i, :], bk, AF.Square)
                    yield
                bk = P.bank()
                for i in range(2):
                    P.mm(bk, onesb, sqk[:, i, :], start=(i == 0), stop=(i == 1))
                P.act(rkv, bk, AF.Ln, bias=epsn, scale=1.0 / 256)
                P.act(rkv, rkv, AF.Exp, scale=-0.5)
                for i in range(4):
                    bk = P.bank()
                    for kc in range(2):
                        P.mm(bk, Wkv[:, kc, i * 128:(i + 1) * 128], ckv[:, kc, :], start=(kc == 0), stop=(kc == 1))
                    qq = qn[i % 2]
                    P.tt('dve', qq, bk, rkv, ALU.mult)
                    for hh in range(2):
                        P.dma(kT_d[2 * i + hh, :, tsl], qq[hh * 64:(hh + 1) * 64, :])
                    yield
                for q in range(4):
                    qs = slice(q * 128, (q + 1) * 128)
                    bkc = P.bank()
                    for i in range(2):
                        P.mm(bkc[:, 0:1], sqk[:, i, qs], onesb[:, 0:1], start=(i == 0), stop=(i == 1))
                    P.act(rcol, bkc[:, 0:1], AF.Ln, bias=epsn, scale=1.0 / 256)
                    P.act(rcol, rcol, AF.Exp, scale=-0.5)
                    bkv = P.bank()
                    for i in range(2):
                        P.mm(bkv, ckv[:, i, qs], Wkv[:, i, 512:1024], start=(i == 0), stop=(i == 1))
                    P.ts('dve', v1s[:, q, :, 0:64], bkv.re("p (h e) -> p h e", e=64), rcol, None, ALU.mult)
                    yield
                for q in range(4):
                    P.dma(v1_d[t0 + q * 128:t0 + (q + 1) * 128], v1s[:, q, :, :])
            gens_ = [rw_gen(), mla_gen()]
            while gens_:
                nx_ = []
                for gg in gens_:
                    try:
                        next(gg)
                        nx_.append(gg)
                    except StopIteration:
                        pass
                gens_ = nx_
        for hp in range(4):
            P.dma(gc_d[hp * 128:(hp + 1) * 128, :], gcall[:, hp, :])
        if 'sbon' in tapset:
            P.dma(sbon_d.re("(c p) h -> p c h", p=128), sball)
        P.release(mL)
        if stop == f'p1_{l}':
            return True

        gcs = P.al("gcs", [64, NH, NT])
        P.dma(gcs, gc_d.re("(h k) c -> k h c", k=64))
        sbn = sball
        lnw = P.al("lnw", [128, 1024])
        P.dma(lnw, V(rows_d.ap[l:l + 1, :].partition_broadcast(128), rows_d.buf))
        lnw3 = lnw[:, 0:512].re("p (h e) -> p h e", e=64)
        lnb3 = lnw[:, 512:1024].re("p (h e) -> p h e", e=64)
        Sf = P.al("Sf", [64, NH, 64])
        Sb = P.al("Sb", [64, NH, 64], BF16)
        P.memset('pool', Sf, 0.0)
        P.memset('pool', Sb, 0.0)
        GL = 4

        def slot_tiles(i):
            d = {}
            d['F'] = P.al(f"F{i}", [64, NH, 4, 128], BF16)
            d['T'] = P.al(f"T{i}", [128, 4, 512], BF16)
            d['XK'] = P.al(f"XK{i}", [128, NH, 128], BF16)
            d['AT'] = P.al(f"AT{i}", [128, NH, 512], BF16)
            for nm in ('Nk', 'Dk', 'Wk', 'Y1', 'Dl'):
                d[nm] = [P.al(f"{nm}{i}_{g}", [128, 4, 128], BF16) for g in range(2)]
            d['MU'] = [P.al(f"MU{i}_{g}", [128, 4, 128], BF16) for g in range(2)]
            d['nPQ'] = P.al(f"nPQ{i}", [128, NH, 128], BF16)
            d['nZ'] = P.al(f"nZ{i}", [64, NH, 64], BF16)
            d['Psi'] = P.al(f"Psi{i}", [64, NH, 64])
            d['Ry'] = P.al(f"Ry{i}", [64, NH, 128], BF16)
            d['y'] = P.al(f"y{i}", [128, NH, 64])
            d['t1'] = d['y'][0:64]
            d['yc'] = P.al(f"yc{i}", [128, NH, 64])
            d['ysq'] = d['y']
            d['mean'] = P.al(f"mean{i}", [128, NH])
            d['var'] = P.al(f"var{i}", [128, NH])
            d['ybf'] = P.al(f"ybf{i}", [128, 512], BF16)
            return d
        slots = [slot_tiles(i) for i in range(GL)]

        def chunk_gen(c):
            d = slots[c % GL]
            F, T, XK, AT, Nk, Dk, Wk, Y1, Dl, MU = (d[k] for k in ('F', 'T', 'XK', 'AT', 'Nk', 'Dk', 'Wk', 'Y1', 'Dl', 'MU'))
            nPQ, nZ, Psi, Ry, t1, y, yc, ysq, mean, var, ybf = (d[k] for k in
                                                               ('nPQ', 'nZ', 'Psi', 'Ry', 't1', 'y', 'yc', 'ysq', 'mean', 'var', 'ybf'))
            crow = slice(c * 128, (c + 1) * 128)
            P.dma(F, feat_d[c])
            P.dma(T, tokm_d[crow])
            P.dma(XK[:, :, 0:64], tokm_d[crow, 0, :].re("p (h e) -> p h e", e=64))
            yield
            for h in range(NH):
                bk = P.bank()
                KR = F[:, h, 2:4, :].re("k o t -> k (o t)")
                P.mm(bk[:, 0:256], F[:, h, 0, :], KR)
                P.mm(bk[:, 256:512], F[:, h, 1, :], KR)
                P.tt('dve', AT[:, h, :], bk, maskT, ALU.mult)
                if h % 4 == 3:
                    yield
            for g in range(2):
                bkn = P.bank()
                for hh in range(4):
                    h = 4 * g + hh
                    P.mm(bkn[:, hh * 128:(hh + 1) * 128], F[:, h, 2, :], F[:, h, 1, :])
                P.tt('dve', Nk[g], bkn.re("p (h s) -> p h s", h=4), masklow.un(1).bc([128, 4, 128]), ALU.mult)
                m0b = masku[:, 0, :].un(1).bc([128, 4, 128])
                idb = identb.un(1).bc([128, 4, 128])
                P.tt('pool', Dk[g], Nk[g], m0b, ALU.mult)
                P.tt('dve', Dk[g], Dk[g], idb, ALU.add)
                P.tt('pool', Wk[g], AT[:, 4 * g:4 * g + 4, 256:384], m0b, ALU.mult)
                P.tt('dve', Wk[g], Wk[g], idb, ALU.add)
                yield
            for j in range(1, 7):
                bkYs = []
                for g in range(2):
                    P.tt('pool', MU[g], AT[:, 4 * g:4 * g + 4, 256:384], masku[:, j, :].un(1).bc([128, 4, 128]), ALU.mult)
                    bkY = P.bank()
                    for hh in range(4):
                        P.mm(bkY[:, hh * 128:(hh + 1) * 128], MU[g][:, hh, :], Dk[g][:, hh, :])
                    P.cp('act', Y1[g], bkY.re("p (h s) -> p h s", h=4))
                yield
                bkDs = []
                for g in range(2):
                    bkD = P.bank()
                    bkDs.append(bkD)
                    for hh in range(4):
                        P.mm(bkD[:, hh * 128:(hh + 1) * 128], Wk[g][:, hh, :], Y1[g][:, hh, :])
                    P.cp('act', Dl[g], bkD.re("p (h s) -> p h s", h=4))
                    if j < 6:
                        P.tt('dve', Dk[g], bkD.re("p (h s) -> p h s", h=4), Dk[g], ALU.add)
                yield
                for g in range(2):
                    bkT = P.bank().cast(BF16)
                    for hh in range(4):
                        P.tr(bkT[:, hh * 128:(hh + 1) * 128], Dl[g][:, hh, :], identb)
                    P.tt('dve', Wk[g], bkT[:, 0:512].re("p (h s) -> p h s", h=4), Wk[g], ALU.add)
                yield
            bkx = P.bank()
            for h in range(NH):
                hc = slice(h * 64, (h + 1) * 64)
                P.mm(bkx[:, hc], AT[:, h, 0:128], T[:, 1, hc])
            P.cp('act', XK[:, :, 64:128], bkx.re("p (h e) -> p h e", e=64))
            yield
            for g in range(2):
                bkp = P.bank()
                for hh in range(4):
                    h = 4 * g + hh
                    P.mm(bkp[:, hh * 128:(hh + 1) * 128], Wk[g][:, hh, :], XK[:, h, :])
                P.act(nPQ[:, 4 * g:4 * g + 4, :], bkp.re("p (h s) -> p h s", h=4), AF.Copy, scale=-1.0)
            yield
            bkz = P.bank()
            for h in range(NH):
                hc = slice(h * 64, (h + 1) * 64)
                P.mm(bkz[0:64, hc], nPQ[:, h, 0:64], T[:, 3, hc])
            P.cp('act', nZ, bkz[0:64, :].re("k (h e) -> k h e", e=64))
            bkpsi = P.bank()
            for h in range(NH):
                hc = slice(h * 64, (h + 1) * 64)
                P.mm(bkpsi[0:64, hc], T[:, 2, hc], T[:, 1, hc], start=True, stop=False)
                P.mm(bkpsi[0:64, hc], T[:, 3, hc], nPQ[:, h, 64:128], start=False, stop=True)
            P.cp('dve', Psi, bkpsi[0:64, :].re("k (h e) -> k h e", e=64))
            for g in range(2):
                bkr = P.bank()
                for hh in range(4):
                    h = 4 * g + hh
                    P.mm(bkr[0:64, hh * 128:(hh + 1) * 128], nPQ[:, h, 0:64], AT[:, h, 384:512])
                P.tt('dve', Ry[:, 4 * g:4 * g + 4, :], bkr[0:64, :].re("k (h t) -> k h t", h=4),
                     F[:, 4 * g:4 * g + 4, 3, :], ALU.add)
            yield
            bky = P.bank()
            for h in range(NH):
                hc = slice(h * 64, (h + 1) * 64)
                P.mm(bky[:, hc], AT[:, h, 128:256], T[:, 1, hc], start=True, stop=False)
                P.mm(bky[:, hc], AT[:, h, 384:512], nPQ[:, h, 64:128], start=False, stop=False)
                P.mm(bky[:, hc], Ry[:, h, :], Sb[:, h, :], start=False, stop=True)
            bku = P.bank()
            for h in range(NH):
                hc = slice(h * 64, (h + 1) * 64)
                P.mm(bku[0:64, hc], nZ[:, h, :], Sb[:, h, :])
            P.tt('pool', t1, Sf, gcs[:, :, c].un(2).bc([64, NH, 64]), ALU.mult)
            P.tt('pool', t1, t1, Psi, ALU.add)
            bku3 = bku[0:64, :].re("k (h e) -> k h e", e=64)
            P.tt('dve', Sb, bku3, t1, ALU.add)
            P.tt('dve', Sf, bku3, t1, ALU.add)
            yield
            P.cp('act', y, bky.re("p (h e) -> p h e", e=64))
            P.reduce(mean, y)
            P.ts('dve', mean, mean, 1.0 / 64, None, ALU.mult)
            P.tt('dve', yc, y, mean.un(2).bc([128, NH, 64]), ALU.subtract)
            P.tt('dve', ysq, yc, yc, ALU.mult)
            P.reduce(var, ysq)
            P.act(var, var, AF.Ln, bias=epsg, scale=1.0 / 64)
            P.act(var, var, AF.Exp, scale=-0.5)
            yield
            P.tt('pool', yc, yc, var.un(2).bc([128, NH, 64]), ALU.mult)
            P.tt('pool', yc, yc, lnw3, ALU.mult)
            P.tt('pool', yc, yc, lnb3, ALU.add)
            P.tt('pool', ysq, T[:, 1, :].re("p (h e) -> p h e", e=64), sbn[:, c, :].un(2).bc([128, NH, 64]), ALU.mult)
            P.tt('dve', ybf.re("p (h e) -> p h e", e=64), yc, ysq, ALU.add)
            P.dma(ytok_d[crow, 0:512], ybf)

        def run_lockstep(gens):
            live = list(gens)
            while live:
                nxt = []
                for gg in live:
                    try:
                        next(gg)
                        nxt.append(gg)
                    except StopIteration:
                        pass
                live = nxt
        for c0 in range(0, NT, GL):
            run_lockstep([chunk_gen(c) for c in range(c0, min(NT, c0 + GL))])
        P.release(mL)
        if stop == f'p2_{l}':
            return True

        v1a = P.al("v1a", [128, NT, NH * 65], BF16)
        P.dma(v1a, v1_d.re("(j p) h e -> p j (h e)", p=128))
        kTq = [P.al(f"kTh{i}", [96, S], BF16) for i in range(2)]
        qTq = [P.al(f"qTh{i}", [96, S], BF16) for i in range(2)]
        Et = [P.al(f"E{i}", [128, 512], BF16) for i in range(4)]
        trib = P.al("trib", [128, 128], BF16)
        P.cp('pool', trib, tri)
        rden = P.al("rden", [128, 1])
        yh = [P.al(f"yh{i}", [128, 64], BF16) for i in range(2)]
        Ob = P.banks[0:4]
        Sbk = P.banks[4:8]
        nsb = 0
        for h in range(NH):
            kTh = kTq[h % 2]
            qTh = qTq[h % 2]
            P.dma(kTh[0:64], kT_d[h])
            P.dma(kTh[64:96], krT_d)
            P.dma(qTh, qT_d[h])
            for Q in range(NB):
                nkb = 4 * Q + 4
                qend = (Q + 1) * 512

                def qk(j):
                    nonlocal nsb
                    qlo = max(Q * 512, j * 128)
                    N = qend - qlo
                    bs = Sbk[nsb % 4]
                    E = Et[nsb % 4]
                    nsb += 1
                    P.mm(bs[:, 0:N], kTh[:, j * 128:(j + 1) * 128], qTh[:, qlo:qend])
                    P.act(E[:, 0:N], bs[:, 0:N], AF.Exp, scale=ATT_SCALE)
                    if j >= 4 * Q:
                        P.tt('pool', E[:, 0:128], E[:, 0:128], trib, ALU.mult)
                    return E, qlo

                def pv(j, E, qlo):
                    for t in range(max(4 * Q, j), 4 * Q + 4):
                        off = t * 128 - qlo
                        P.mm(Ob[t - 4 * Q][:, 0:65], E[:, off:off + 128], v1a[:, j, h * 65:(h + 1) * 65],
                             start=(j == 0), stop=(j == t))
                pq_ = [qk(0)]
                if nkb > 1:
                    pq_.append(qk(1))
                for j in range(nkb):
                    if j + 2 < nkb:
                        pq_.append(qk(j + 2))
                    pv(j, *pq_.pop(0))
                for tq in range(4):
                    t = 4 * Q + tq
                    yy = yh[tq % 2]
                    P.recip(rden, Ob[tq][:, 64:65])
                    P.ts('dve', yy, Ob[tq][:, 0:64], rden, None, ALU.mult)
                    P.dma(ytok_d[t * 128:(t + 1) * 128, 512 + h * 64:512 + (h + 1) * 64], yy)
        P.release(mL)
        if stop == f'p3_{l}':
            return True

        Wo = P.al("Wo", [128, 8, D], BF16)
        ostq = [P.al("ost0", [128, D]), P.al("ost1", [128, D])]
        for kc in range(8):
            P.dma(ostq[kc % 2], wout_d[l, kc * 128:(kc + 1) * 128, :])
            P.cp('act' if kc % 2 else 'dve', Wo[:, kc, :], ostq[kc % 2])
        ytq = [P.al(f"yt{i}", [128, 4, D], BF16) for i in range(2)]
        gtq = [P.al(f"gt{i}", [128, 8, 512], BF16) for i in range(2)]
        xtq = [P.al(f"xt4{i}", [128, 8, 512]) for i in range(2)]
        yT = P.al("yT", [128, 8, 512], BF16)

        def p4_load(b):
            tsl = slice(b * 512, (b + 1) * 512)
            P.dma(ytq[b % 2], ytok_d[tsl].re("(q p) f -> p q f", p=128))
            P.dma(gtq[b % 2], gate_d.re("(c p) t -> p c t", p=128)[:, :, tsl])
            P.dma(xtq[b % 2], xT_v[:, :, tsl])
        p4_load(0)
        for b in range(NB):
            tsl = slice(b * 512, (b + 1) * 512)
            yt, gt, xt4 = ytq[b % 2], gtq[b % 2], xtq[b % 2]
            if b + 1 < NB:
                p4_load(b + 1)
            for f in range(8):
                bkt = P.bank().cast(BF16)
                for q in range(4):
                    P.tr(bkt[:, q * 128:(q + 1) * 128], yt[:, q, f * 128:(f + 1) * 128], identb)
                P.tt('dve', yT[:, f, :], bkt[:, 0:512], gt[:, f, :], ALU.mult)
            for dch in range(8):
                bk = P.bank()
                for f in range(8):
                    P.mm(bk, Wo[:, f, dch * 128:(dch + 1) * 128], yT[:, f, :], start=(f == 0), stop=(f == 7))
                P.stt(xt4[:, dch, :], bk, mods[:, l, 16 + dch:17 + dch], xt4[:, dch, :], ALU.mult, ALU.add)
            P.dma(xT_v[:, :, tsl], xt4)
        P.release(mL)
        return False

    try:
        for l in range(L):
            if layer(l):
                return finish()
    except _Stop:
        return finish()

    m0 = P.mark()
    fg = P.al("fg", [128, D])
    P.dma(fg, V(fing_d.ap.partition_broadcast(128), fing_d.buf))
    xtf = P.al("xtf", [128, 8, 512])
    xo = [P.al("xo0", [128, D]), P.al("xo1", [128, D])]
    junk = P.al("junkf", [128, D])
    ssq = [P.al("ssq0", [128, 1]), P.al("ssq1", [128, 1])]
    for b in range(NB):
        t0 = b * 512
        P.dma(xtf, xT_v[:, :, t0:t0 + 512])
        for j in range(4):
            bk0 = P.bank()
            bk1 = P.bank()
            for c in range(8):
                bk = bk0 if c < 4 else bk1
                P.tr(bk[:, (c % 4) * 128:(c % 4 + 1) * 128], xtf[:, c, j * 128:(j + 1) * 128], ident)
            o = xo[j % 2]
            s = ssq[j % 2]
            P.cp('dve', o[:, 0:512], bk0)
            P.cp('act', o[:, 512:1024], bk1)
            P.act(junk, o, AF.Square, accum=s)
            P.act(s, s, AF.Ln, bias=epsn, scale=1.0 / D)
            P.act(s, s, AF.Exp, scale=-0.5)
            P.stt(o, o, s, fg, ALU.mult, ALU.mult)
            P.dma(out_d[t0 + j * 128:t0 + (j + 1) * 128, :], o, final=True)
    P.release(m0)
    return finish()


def make_masku():
    i = np.arange(128)
    m = np.zeros((128, 7, 128), np.float32)
    for j in range(7):
        bs = 1 << j
        m[:, j, :] = ((i[:, None] // (2 * bs)) == (i[None, :] // (2 * bs))) & ((i[:, None] // bs) != (i[None, :] // bs))
    return m.reshape(128, 7 * 128).astype(ml_dtypes.bfloat16)


def host_layout(inputs, S, L):
    f = lambda a: np.ascontiguousarray(np.asarray(a, dtype=np.float32))

    def cols(v):
        v = np.asarray(v, np.float32)
        return v.reshape(-1, 128).T

    vecs = np.zeros((128, L, NV), np.float32)
    rows = np.zeros((L, 1024), np.float32)
    for l in range(L):
        vecs[:, l, VG:VG + 8] = cols(inputs['norm_g'][l])
        vecs[:, l, VB:VB + 24] = cols(inputs['b_ada'][l])
        vecs[:, l, VMU:VMU + 13] = cols(inputs['mu_shift'][l])
        if l > 0:
            vecs[64:96, l, VMUV] = np.asarray(inputs['mu_vmix'][l - 1], np.float32)
            vecs[:, l, VV0:VV0 + 4] = cols(inputs['v0'][l - 1])
        vecs[:, l, VW0:VW0 + 4] = cols(inputs['w0'][l])
        vecs[:, l, VA0:VA0 + 4] = cols(inputs['a0'][l])
        vecs[:, l, VKK:VKK + 4] = cols(inputs['k_k'][l])
        vecs[:, l, VKA:VKA + 4] = cols(inputs['k_a'][l])
        vecs[:, l, VRK:VRK + 4] = cols(np.asarray(inputs['r_k'][l]).reshape(-1))
        vecs[:, l, VQG:VQG + 3] = cols(inputs['q_norm_g'][l])
        vecs[:, l, VKVG:VKVG + 2] = cols(inputs['kv_norm_g'][l])
        rows[l, 0:512] = np.asarray(inputs['lnx_w'][l], np.float32)
        rows[l, 512:1024] = np.asarray(inputs['lnx_b'][l], np.float32)
    shared = {
        "consts": make_consts(), "vecs": vecs, "rows": rows, "masku": make_masku(),
        "final_g": f(inputs['final_g']).reshape(1, D),
        "w_ada": f(inputs['w_ada'])[:L], "w_in": f(inputs['w_in'])[:L],
        "w_vd": f(inputs['w_vmix_down'])[:max(L - 1, 1)],
        "w_dec": f(inputs['w_decay_up'])[:L], "w_icl": f(inputs['w_iclr_up'])[:L],
        "w_vup": f(inputs['w_vmix_up'])[:max(L - 1, 1)],
        "w_uq": f(inputs['w_uq'])[:L], "w_ukv": f(inputs['w_ukv'])[:L], "w_out": f(inputs['w_out'])[:L],
    }
    x = np.asarray(inputs['x'], np.float32)
    c = np.asarray(inputs['c'], np.float32)
    pos = np.asarray(inputs['positions']).astype(np.int32)
    B = x.shape[0]
    per = []
    for b in range(B):
        m = dict(shared)
        m["x"] = np.ascontiguousarray(x[b, :S])
        m["cT"] = np.ascontiguousarray(c[b].reshape(8, 128).T)
        m["pos"] = np.ascontiguousarray(pos[b, :S].reshape(1, S))
        per.append(m)
    return per


_NC_CACHE = {}


def kernel(**inputs):
    S, L = 4096, 4
    B = np.asarray(inputs['x']).shape[0]
    per = host_layout(inputs, S, L)
    if (S, L) not in _NC_CACHE:
        _NC_CACHE[(S, L)] = build(S, L)
    nc = _NC_CACHE[(S, L)]
    res = run_bass_kernel_spmd(nc, per, core_ids=list(range(B)))
    return np.stack([np.asarray(r["out"], np.float32) for r in res.results], axis=0)
```
